# Optimizing a Trainium2 kernel written in Bass

```python
import math
import jax, jax.numpy as jnp
from jax import lax
import numpy as np

D_MODEL = 1024
BATCH = 8
SEQ = 4096
DEPTH = 1
DEC_BATCH = 4
DEC_SEQ = 8192
PAST_LEN = 128

D_HYENA = D_MODEL // 2
D_HGRN = D_MODEL // 2
HGRN_DK = 128
HGRN_HEADS = D_HGRN // HGRN_DK
HGRN_DV = D_HGRN // HGRN_HEADS
CHUNK = 64
D_FF = 4 * D_MODEL
SHORT_CONV = 3
FILTER_EMB = 33
FILTER_BANDS = (FILTER_EMB - 1) // 2
FILTER_HIDDEN = 64
DECAY_TARGET = 1e-2
FAST_DECAY_PCT = 0.3
SLOW_DECAY_PCT = 1.5
EPS = 1e-6
D_IN = 3 * D_HYENA + 5 * D_HGRN + 2 * D_MODEL

kernel_name = 'hyena_hgrn2_gated_merge_encoder'


def rmsnorm(x, g):
    xf = x.astype(jnp.float32)
    y = xf * lax.rsqrt(jnp.mean(xf * xf, axis=-1, keepdims=True) + EPS)
    return (y * g.astype(jnp.float32)).astype(x.dtype)


def adaln(h, shift, scale):
    return h * (1 + scale[:, None, :]) + shift[:, None, :]


def short_conv3(u, w, b):
    up = jnp.pad(u, ((0, 0), (1, 1), (0, 0)))
    return up[:, :-2] * w[0] + up[:, 1:-1] * w[1] + up[:, 2:] * w[2] + b


def hyena_filters(L, w1, b1, w2, b2, w3, b3, wo, freq):
    f32 = jnp.float32
    t = jnp.linspace(0.0, 1.0, L, dtype=f32)[:, None]
    w = 2 * math.pi * jnp.arange(L, dtype=f32)[:, None] / L
    fb = jnp.linspace(1e-4, FILTER_BANDS - 1, FILTER_BANDS, dtype=f32)[None, :]
    z = jnp.concatenate([t, jnp.cos(fb * w), -jnp.sin(fb * w)], axis=-1)
    freq = freq.astype(f32)
    h = jnp.sin(freq[0] * (z @ w1.astype(f32) + b1.astype(f32)))
    h = jnp.sin(freq[1] * (h @ w2.astype(f32) + b2.astype(f32)))
    h = jnp.sin(freq[2] * (h @ w3.astype(f32) + b3.astype(f32)))
    h = h @ wo.astype(f32)
    max_decay = math.log(DECAY_TARGET) / FAST_DECAY_PCT
    min_decay = math.log(DECAY_TARGET) / SLOW_DECAY_PCT
    deltas = jnp.abs(jnp.linspace(min_decay, max_decay, D_HYENA, dtype=f32))[None, :]
    window = jnp.exp(-t * deltas)
    h_f = h[:, :D_HYENA] * window
    h_b = h[:, D_HYENA:] * window
    norm = jnp.sum(jnp.abs(h_f), axis=0) + jnp.sum(jnp.abs(h_b[1:]), axis=0) + EPS
    return h_f / norm, h_b / norm


def bidir_long_conv(v, h_f, h_b):
    L = v.shape[1]
    k = jnp.concatenate([h_f, jnp.zeros((1, h_f.shape[1]), h_f.dtype), h_b[1:][::-1]], axis=0)
    v_f = jnp.fft.rfft(v, n=2 * L, axis=1)
    k_f = jnp.fft.rfft(k, n=2 * L, axis=0)
    return jnp.fft.irfft(v_f * k_f[None], n=2 * L, axis=1)[:, :L]


def hyena_branch(u, conv_w, conv_b, w1, b1, w2, b2, w3, b3, wo, freq, fbias):
    f32 = jnp.float32
    u = short_conv3(u, conv_w, conv_b)
    x0, x1, v = jnp.split(u, 3, axis=-1)
    h_f, h_b = hyena_filters(u.shape[1], w1, b1, w2, b2, w3, b3, wo, freq)
    v = (v * x1).astype(f32)
    y = bidir_long_conv(v, h_f, h_b) + v * fbias.astype(f32)
    return (y * x0.astype(f32)).astype(u.dtype)


def chunk_scan(q, k, g, v):
    B, L, H, dk = q.shape
    dv = v.shape[-1]
    N = L // CHUNK

    def to_chunks(a):
        return a.reshape(B, N, CHUNK, H, a.shape[-1]).transpose(1, 0, 3, 2, 4)

    mask = jnp.tril(jnp.ones((CHUNK, CHUNK), dtype=bool))[None, None, :, :, None]

    def step(S, inp):
        q_, k_, g_, v_ = inp
        G = jnp.cumsum(g_, axis=2)
        diff = G[:, :, :, None, :] - G[:, :, None, :, :]
        decay = jnp.exp(jnp.where(mask, diff, -jnp.inf))
        A = jnp.einsum('bhtsk,bhsk->bhts', decay * q_[:, :, :, None, :], k_)
        o = jnp.einsum('bhts,bhsv->bhtv', A, v_) + jnp.einsum('bhtk,bhkv->bhtv', q_ * jnp.exp(G), S)
        G_last = G[:, :, -1:, :]
        S_new = jnp.exp(G_last[:, :, 0, :])[..., None] * S + jnp.einsum(
            'bhsk,bhsv->bhkv', k_ * jnp.exp(G_last - G), v_)
        return S_new, o

    S0 = jnp.zeros((B, H, dk, dv), jnp.float32)
    _, o = lax.scan(step, S0, (to_chunks(q), to_chunks(k), to_chunks(g), to_chunks(v)))
    return o.transpose(1, 0, 3, 2, 4).reshape(B, L, H, dv)


def hgrn2_branch(p, lb_f, lb_b, gnorm_g):
    f32 = jnp.float32
    B, L, _ = p.shape
    q, i, ff, fb, og = jnp.split(p, 5, axis=-1)
    heads = lambda a: a.reshape(B, L, HGRN_HEADS, -1)
    q = heads(jax.nn.silu(q.astype(f32)))
    i = heads(i.astype(f32))

    def gate(fr, lb):
        f = lb + (1 - lb) * jax.nn.sigmoid(fr.astype(f32))
        return heads(1 - f), heads(jnp.log(f))

    k_f, g_f = gate(ff, lb_f)
    k_b, g_b = gate(fb, lb_b)
    flip = lambda a: a[:, ::-1]
    o = chunk_scan(jnp.concatenate([q, flip(q)], axis=2),
                   jnp.concatenate([k_f, flip(k_b)], axis=2),
                   jnp.concatenate([g_f, flip(g_b)], axis=2),
                   jnp.concatenate([i, flip(i)], axis=2))
    o = o[:, :, :HGRN_HEADS] + flip(o[:, :, HGRN_HEADS:])
    o = o * lax.rsqrt(jnp.mean(o * o, axis=-1, keepdims=True) + EPS)
    o = o.reshape(B, L, D_HGRN) * gnorm_g.astype(f32) * jax.nn.silu(og.astype(f32))
    return o.astype(p.dtype)


def encoder(x, c, params):
    (ada_w, ada_b, norm1_g, w_in, conv_w, conv_b, filt_w1, filt_b1, filt_w2, filt_b2,
     filt_w3, filt_b3, filt_wo, filt_freq, filt_bias, hgrn_lb, gnorm_g, w_branch, w_out,
     norm2_g, w_ff1, w_ff2, final_g) = params
    lb_all = jnp.cumsum(jax.nn.softmax(hgrn_lb.astype(jnp.float32), axis=0), axis=0)
    sc = jax.nn.silu(c)
    for l in range(DEPTH):
        mod = sc @ ada_w[l] + ada_b[l]
        sh1, sc1, gt1, sh2, sc2, gt2 = jnp.split(mod, 6, axis=-1)
        h = adaln(rmsnorm(x, norm1_g[l]), sh1, sc1)
        proj = h @ w_in[l]
        p_hy = proj[..., :3 * D_HYENA]
        p_hg = proj[..., 3 * D_HYENA:3 * D_HYENA + 5 * D_HGRN]
        p_gt = proj[..., 3 * D_HYENA + 5 * D_HGRN:]
        y_a = hyena_branch(p_hy, conv_w[l], conv_b[l], filt_w1[l], filt_b1[l], filt_w2[l],
                           filt_b2[l], filt_w3[l], filt_b3[l], filt_wo[l], filt_freq[l], filt_bias[l])
        y_b = hgrn2_branch(p_hg, lb_all[l, 0], lb_all[l, 1], gnorm_g[l])
        g_a, g_b = jnp.split(jax.nn.sigmoid(p_gt), 2, axis=-1)
        m = g_a * (y_a @ w_branch[l][:D_HYENA]) + g_b * (y_b @ w_branch[l][D_HYENA:])
        x = x + gt1[:, None, :] * (m @ w_out[l])
        h2 = adaln(rmsnorm(x, norm2_g[l]), sh2, sc2)
        x = x + gt2[:, None, :] * (jnp.square(jax.nn.relu(h2 @ w_ff1[l])) @ w_ff2[l])
    return rmsnorm(x, final_g)


def setup_inputs(seed: int = 0) -> dict:
    key = jax.random.key(seed)
    ks = list(jax.random.split(key, 32))

    def nrm(shape, scale):
        return jax.random.normal(ks.pop(), shape, jnp.float32) * scale

    D = D_MODEL
    return {
        'x_prompt': nrm((BATCH, SEQ, D), 1.0),
        'x_sample': nrm((DEC_BATCH, DEC_SEQ, D), 1.0),
        'c_prompt': nrm((BATCH, D), 1.0),
        'c_sample': nrm((DEC_BATCH, D), 1.0),
        'ada_w': nrm((DEPTH, D, 6 * D), D ** -0.5),
        'ada_b': nrm((DEPTH, 6 * D), 0.02),
        'norm1_g': 1.0 + nrm((DEPTH, D), 0.02),
        'w_in': nrm((DEPTH, D, D_IN), D ** -0.5),
        'conv_w': nrm((DEPTH, SHORT_CONV, 3 * D_HYENA), SHORT_CONV ** -0.5),
        'conv_b': nrm((DEPTH, 3 * D_HYENA), 0.02),
        'filt_w1': nrm((DEPTH, FILTER_EMB, FILTER_HIDDEN), FILTER_EMB ** -0.5),
        'filt_b1': nrm((DEPTH, FILTER_HIDDEN), 0.02),
        'filt_w2': nrm((DEPTH, FILTER_HIDDEN, FILTER_HIDDEN), FILTER_HIDDEN ** -0.5),
        'filt_b2': nrm((DEPTH, FILTER_HIDDEN), 0.02),
        'filt_w3': nrm((DEPTH, FILTER_HIDDEN, FILTER_HIDDEN), FILTER_HIDDEN ** -0.5),
        'filt_b3': nrm((DEPTH, FILTER_HIDDEN), 0.02),
        'filt_wo': nrm((DEPTH, FILTER_HIDDEN, 2 * D_HYENA), FILTER_HIDDEN ** -0.5),
        'filt_freq': 1.0 + nrm((DEPTH, 3, FILTER_HIDDEN), 0.05),
        'filt_bias': nrm((DEPTH, D_HYENA), 1.0),
        'hgrn_lb': 1.0 + nrm((DEPTH + 1, 2, D_HGRN), 0.1),
        'gnorm_g': 1.0 + nrm((DEPTH, D_HGRN), 0.02),
        'w_branch': nrm((DEPTH, D_HYENA + D_HGRN, D), D_HYENA ** -0.5),
        'w_out': nrm((DEPTH, D, D), D ** -0.5),
        'norm2_g': 1.0 + nrm((DEPTH, D), 0.02),
        'w_ff1': nrm((DEPTH, D, D_FF), D ** -0.5),
        'w_ff2': nrm((DEPTH, D_FF, D), D_FF ** -0.5),
        'final_g': 1.0 + nrm((D,), 0.02),
    }


def reference(x_prompt, x_sample, c_prompt, c_sample, ada_w, ada_b, norm1_g, w_in, conv_w, conv_b,
              filt_w1, filt_b1, filt_w2, filt_b2, filt_w3, filt_b3, filt_wo, filt_freq, filt_bias,
              hgrn_lb, gnorm_g, w_branch, w_out, norm2_g, w_ff1, w_ff2, final_g):
    params = (ada_w, ada_b, norm1_g, w_in, conv_w, conv_b, filt_w1, filt_b1, filt_w2, filt_b2,
              filt_w3, filt_b3, filt_wo, filt_freq, filt_bias, hgrn_lb, gnorm_g, w_branch, w_out,
              norm2_g, w_ff1, w_ff2, final_g)
    y_prompt = encoder(x_prompt, c_prompt, params)
    y_sample = encoder(x_sample, c_sample, params)
    return (y_prompt, y_sample)
```

```python
import math
import numpy as np
from contextlib import ExitStack
import concourse.bass as bass
import concourse.mybir as mybir
from concourse.bass_utils import run_bass_kernel_spmd

F32 = mybir.dt.float32
BF16 = mybir.dt.bfloat16
I32 = mybir.dt.int32
ALU = mybir.AluOpType
AF = mybir.ActivationFunctionType
AX = mybir.AxisListType

NT = 64
SEG = 32
EPS = 1e-6
TWO_PI = 2.0 * math.pi


class Buf:
    def __init__(self, name):
        self.name = name
        self.w = {}
        self.r = {}
        self.dsem = None
        self.dcnt = 0


class Sched:
    def __init__(self, nc, es):
        self.nc = nc
        self.es = es
        self.eng = {}
        self.sem = {}
        self.cnt = {}
        self.seen = {}
        self.dbufs = []
        self.nins = 0

    def add_engine(self, name, eng):
        self.eng[name] = eng
        self.sem[name] = self.es.enter_context(self.nc.semaphore("s_" + name))
        self.cnt[name] = 0
        self.seen[name] = {}

    def _wait(self, e, toks):
        eng = self.eng[e]
        seen = self.seen[e]
        for sid, (sem, val) in toks.items():
            if seen.get(sid, 0) >= val:
                continue
            eng.wait_ge(sem, val)
            self.nins += 1
            seen[sid] = val

    @staticmethod
    def _merge(dst, sid, sem, val):
        if sid not in dst or dst[sid][1] < val:
            dst[sid] = (sem, val)

    def _deps(self, reads, writes):
        toks = {}
        for b in reads:
            for sid, (sem, val) in b.w.items():
                self._merge(toks, sid, sem, val)
        for b in writes:
            for sid, (sem, val) in b.w.items():
                self._merge(toks, sid, sem, val)
            for sid, (sem, val) in b.r.items():
                self._merge(toks, sid, sem, val)
        return toks

    def op(self, e, fn, reads=(), writes=(), acc=False):
        toks = self._deps(reads, writes)
        if e == "pe":
            toks.pop(id(self.sem[e]), None)
        self._wait(e, toks)
        ins = fn(self.eng[e])
        self.cnt[e] += 1
        self.nins += 1
        sem = self.sem[e]
        ins.then_inc(sem, 1)
        sid, val = id(sem), self.cnt[e]
        for b in reads:
            self._merge(b.r, sid, sem, val)
        for b in writes:
            if acc:
                self._merge(b.w, sid, sem, val)
            else:
                b.w = {sid: (sem, val)}
                b.r = {}
        return ins

    def dma(self, q, out, in_, key, reads=(), writes=(), acc=True, **kw):
        toks = self._deps(reads, writes)
        self._wait(q, toks)
        if key.dsem is None:
            key.dsem = self.es.enter_context(self.nc.semaphore("d_" + key.name))
            self.dbufs.append(key)
        ins = self.eng[q].dma_start(out=out, in_=in_, **kw)
        key.dcnt += 1
        self.nins += 1
        ins.then_inc(key.dsem, 16)
        sid, sem, val = id(key.dsem), key.dsem, 16 * key.dcnt
        for b in reads:
            self._merge(b.r, sid, sem, val)
        for b in writes:
            if acc:
                self._merge(b.w, sid, sem, val)
            else:
                b.w = {sid: (sem, val)}
                b.r = {}
        return ins

    def wait_bufs(self, e, bufs):
        toks = {}
        for b in bufs:
            for sid, (sem, val) in b.w.items():
                self._merge(toks, sid, sem, val)
            for sid, (sem, val) in b.r.items():
                self._merge(toks, sid, sem, val)
        self._wait(e, toks)

    def barrier(self):
        toks = {}
        for e, sem in self.sem.items():
            if self.cnt[e] > 0:
                toks[id(sem)] = (sem, self.cnt[e])
        for b in self.dbufs:
            toks[id(b.dsem)] = (b.dsem, 16 * b.dcnt)
        for e in self.eng:
            t = dict(toks)
            t.pop(id(self.sem[e]), None)
            self._wait(e, t)


class Rot:
    def __init__(self, name, aps):
        self.aps = aps
        self.bufs = [Buf(f"{name}{i}") for i in range(len(aps))]
        self.i = -1

    def next(self):
        self.i = (self.i + 1) % len(self.aps)
        return self.aps[self.i], self.bufs[self.i]


def build(dbg=False):
    nc = bass.Bass("TRN2", target_bir_lowering=False)

    def din(name, shape, dt=F32):
        return nc.dram_tensor(name, list(shape), dt, kind="ExternalInput").ap()

    def dscr(name, shape, dt):
        kind = "ExternalOutput" if dbg else "Internal"
        return nc.dram_tensor(name, list(shape), dt, kind=kind).ap()

    x = din("x", [NT * 128, 1024])
    cT = din("cT", [128, 16])
    meta = din("meta", [128, 8])
    ada_w = din("ada_w", [1024, 6144])
    ada_bT = din("ada_bT", [128, 48])
    g1T = din("g1T", [128, 8])
    g2T = din("g2T", [128, 8])
    w_in = din("w_in", [1024, 6144])
    convw_b = din("convw_b", [128, 3 * 1536])
    convb_b = din("convb_b", [128, 1536])
    fw1 = din("fw1", [33, 64])
    fw2 = din("fw2", [64, 64])
    fw3 = din("fw3", [64, 64])
    fwo = din("fwo", [64, 1024])
    fbT = din("fbT", [64, 3])
    ffreqT = din("ffreqT", [64, 3])
    fbiasT = din("fbiasT", [128, 4])
    fconst = din("fconst", [128, 8])
    lbT = din("lbT", [128, 16])
    gnorm_b = din("gnorm_b", [128, 512])
    finalg_b = din("finalg_b", [128, 1024])
    w_branch = din("w_branch", [1024, 1024])
    w_out = din("w_out", [1024, 1024])
    w_ff1 = din("w_ff1", [1024, 4096])
    w_ff2 = din("w_ff2", [4096, 1024])
    cmat = din("cmat", [128, 4 * 128])
    y = nc.dram_tensor("y", [NT * 128, 1024], F32, kind="ExternalOutput").ap()

    VT = dscr("VT", [NT, 128, 512], BF16)
    X0T = dscr("X0T", [NT, 128, 512], BF16)
    IT = dscr("IT", [NT, 128, 512], BF16)
    OGT = dscr("OGT", [NT, 128, 512], BF16)
    QT = dscr("QT", [NT, 128, 512], BF16)
    KFT = dscr("KFT", [NT, 128, 2, 512], BF16)
    GFT = dscr("GFT", [NT, 128, 2, 512], F32)
    GTT = dscr("GTT", [NT, 128, 2048], BF16)
    HTD = dscr("HTD", [NT // 4, 128, 8, 512], BF16)

    with ExitStack() as es:
        S = Sched(nc, es)
        S.add_engine("sp", nc.sync)
        S.add_engine("act", nc.scalar)
        S.add_engine("dve", nc.vector)
        S.add_engine("pool", nc.gpsimd)
        S.add_engine("pe", nc.tensor)

        def sb(st, name, shape, dt):
            return st.enter_context(nc.sbuf_tensor(name, list(shape), dt))

        psb = [es.enter_context(nc.psum_tensor(f"ps{i}", [128, 512], F32)) for i in range(8)]
        PS = Rot("ps", psb)

        cm_f = sb(es, "cm_f", [128, 512], F32)
        cm_b = sb(es, "cm_b", [128, 512], BF16)
        ones_f = sb(es, "ones_f", [128, 128], F32)
        meta_t = sb(es, "meta_t", [128, 8], F32)
        modT = sb(es, "modT", [128, 48, 2], F32)
        a1T = sb(es, "a1T", [128, 8, 2], F32)
        a2T = sb(es, "a2T", [128, 8, 2], F32)
        lb_t = sb(es, "lb_t", [128, 16], F32)
        lbv = sb(es, "lbv", [128, 8], F32)
        omlv = sb(es, "omlv", [128, 8], F32)
        Bconst = Buf("const")
        Bmod = Buf("mod")
        ident_f = cm_f[:, 0:128]
        ident_b = cm_b[:, 0:128]
        anti_b = cm_b[:, 128:256]
        FLAG = meta_t[:, 0:1]
        NFLAG = meta_t[:, 1:2]

        S.dma("sp", cm_f[:], cmat[:, :], key=Bconst, writes=[Bconst])
        S.dma("sp", meta_t[:], meta[:, :], key=Bconst, writes=[Bconst])
        S.dma("sp", lb_t[:], lbT[:, :], key=Bconst, writes=[Bconst])
        S.op("dve", lambda e: e.tensor_copy(out=cm_b[:], in_=cm_f[:]), reads=[Bconst], writes=[Bconst], acc=True)
        S.op("pool", lambda e: e.memset(ones_f[:], 1.0), writes=[Bconst], acc=True)
        S.op("dve", lambda e: e.tensor_tensor(out=lbv[:], in0=lb_t[:, 0:8], in1=lb_t[:, 8:16], op=ALU.subtract),
             reads=[Bconst], writes=[Bconst], acc=True)
        S.op("act", lambda e: e.activation(out=lbv[:], in_=lbv[:], func=AF.Sigmoid), reads=[Bconst], writes=[Bconst], acc=True)
        S.op("dve", lambda e: e.tensor_scalar(out=omlv[:], in0=lbv[:], scalar1=-1.0, scalar2=1.0, op0=ALU.mult, op1=ALU.add),
             reads=[Bconst], writes=[Bconst], acc=True)

        with ExitStack() as st:
            scT = sb(st, "scT", [128, 16], F32)
            adab = sb(st, "adab", [128, 48], F32)
            g12 = sb(st, "g12", [128, 16], F32)
            acc = sb(st, "acc", [128, 96], F32)
            stg = [sb(st, f"adastg{i}", [128, 3072], F32) for i in range(2)]
            STG = Rot("adastg", stg)
            Bsc, Bacc = Buf("sc"), Buf("acc")
            S.dma("sp", scT[:], cT[:, :], key=Bsc, writes=[Bsc])
            S.dma("sp", adab[:], ada_bT[:, :], key=Bsc, writes=[Bsc])
            S.dma("sp", g12[:, 0:8], g1T[:, :], key=Bsc, writes=[Bsc])
            S.dma("sp", g12[:, 8:16], g2T[:, :], key=Bsc, writes=[Bsc])
            S.op("act", lambda e: e.activation(out=scT[:], in_=scT[:], func=AF.Silu), reads=[Bsc], writes=[Bsc], acc=True)
            S.op("pool", lambda e: e.memset(acc[:], 0.0), writes=[Bacc])
            for k in range(8):
                for hf in range(2):
                    stt, Bst = STG.next()
                    S.dma("sp", stt[:], ada_w[128 * k:128 * (k + 1), 3072 * hf:3072 * (hf + 1)], key=Bst, writes=[Bst], acc=False)
                    pt, Bp = PS.next()
                    for jj in range(24):
                        S.op("pe", lambda e, jj=jj: e.matmul(pt[:, 2 * jj:2 * jj + 2], lhsT=stt[:, 128 * jj:128 * (jj + 1)],
                                                             rhs=scT[:, 2 * k:2 * k + 2], start=True, stop=True),
                             reads=[Bst, Bsc], writes=[Bp], acc=(jj > 0))
                    S.op("dve", lambda e: e.tensor_tensor(out=acc[:, 48 * hf:48 * (hf + 1)], in0=acc[:, 48 * hf:48 * (hf + 1)],
                                                          in1=pt[:, 0:48], op=ALU.add), reads=[Bp, Bacc], writes=[Bacc], acc=True)
            S.op("dve", lambda e: e.tensor_tensor(out=modT[:], in0=acc[:].rearrange("p (j s) -> p j s", s=2),
                                                  in1=adab[:].unsqueeze(2).to_broadcast([128, 48, 2]), op=ALU.add),
                 reads=[Bacc, Bsc], writes=[Bmod])
            for (aT, goff, joff) in ((a1T, 0, 8), (a2T, 8, 32)):
                S.op("dve", lambda e, aT=aT, joff=joff: e.tensor_scalar(out=aT[:], in0=modT[:, joff:joff + 8, :], scalar1=1.0,
                                                                         scalar2=None, op0=ALU.add), reads=[Bmod], writes=[Bmod], acc=True)
                S.op("dve", lambda e, aT=aT, goff=goff: e.tensor_tensor(out=aT[:], in0=aT[:],
                                                                         in1=g12[:, goff:goff + 8].unsqueeze(2).to_broadcast([128, 8, 2]),
                                                                         op=ALU.mult), reads=[Bmod, Bsc], writes=[Bmod], acc=True)
            S.barrier()

        with ExitStack() as st:
            Wc = sb(st, "Wc", [128, 3, 8, 1536], BF16)
            Wr = sb(st, "Wr", [128, 8, 4608], BF16)
            Bw = Buf("w")
            with ExitStack() as st2:
                cwb = sb(st2, "cwb", [128, 4608], F32)
                stg = [sb(st2, f"wstg{i}", [128, 3072], F32) for i in range(2)]
                STG = Rot("wstg", stg)
                Bcw = Buf("cw")
                S.dma("sp", cwb[:], convw_b[:, :], key=Bcw, writes=[Bcw])
                ei = 0
                for k in range(8):
                    for hf in range(2):
                        stt, Bst = STG.next()
                        S.dma("sp", stt[:], w_in[128 * k:128 * (k + 1), 3072 * hf:3072 * (hf + 1)], key=Bst, writes=[Bst], acc=False)
                        if hf == 0:
                            for j in range(3):
                                en = ("dve", "pool")[ei % 2]
                                ei += 1
                                S.op(en, lambda e, j=j: e.tensor_tensor(out=Wc[:, j, k, :], in0=stt[:, 0:1536],
                                                                        in1=cwb[:, 1536 * j:1536 * (j + 1)], op=ALU.mult),
                                     reads=[Bst, Bcw], writes=[Bw], acc=True)
                            S.op("act", lambda e: e.activation(out=Wr[:, k, 0:1536], in_=stt[:, 1536:3072], func=AF.Copy),
                                 reads=[Bst], writes=[Bw], acc=True)
                        else:
                            S.op("act", lambda e: e.activation(out=Wr[:, k, 1536:3072], in_=stt[:, 0:1536], func=AF.Copy),
                                 reads=[Bst], writes=[Bw], acc=True)
                            en = ("dve", "pool")[ei % 2]
                            ei += 1
                            S.op(en, lambda e: e.tensor_copy(out=Wr[:, k, 3072:4608], in_=stt[:, 1536:3072]),
                                 reads=[Bst], writes=[Bw], acc=True)
                S.barrier()

            cvb = sb(st, "cvb", [128, 1536], F32)
            Bcv = Buf("cvb")
            S.dma("sp", cvb[:], convb_b[:, :], key=Bcv, writes=[Bcv])
            XT = Rot("xt", [sb(st, f"xt{i}", [128, 1024], F32) for i in range(2)])
            SS = Rot("ss", [sb(st, f"ss{i}", [128, 2], F32) for i in range(2)])
            XN = Rot("xn", [sb(st, f"xn{i}", [128, 1024], BF16) for i in range(3)])
            HT = Rot("hT", [sb(st, f"hT{i}", [128, 8, 130], BF16) for i in range(4)])
            HY = Rot("hy", [sb(st, f"hy{i}", [128, 1536], F32) for i in range(1)])
            VP = Rot("vp", [sb(st, f"vp{i}", [128, 512], BF16) for i in range(2)])
            VR = Rot("vr", [sb(st, f"vr{i}", [128, 512], BF16) for i in range(2)])
            X0 = Rot("x0", [sb(st, f"x0{i}", [128, 512], BF16) for i in range(2)])
            IO = Rot("io", [sb(st, f"io{i}", [128, 2, 512], BF16) for i in range(2)])

            hslots = {}
            BHTD = Buf("HTD")

            xns = {}

            def prepA(n):
                xt, Bx = XT.next()
                S.dma("sp", xt[:], x[128 * n:128 * (n + 1), :], key=Bx, writes=[Bx], acc=False)
                ss, Bss = SS.next()
                sq, Bsq = HY.aps[0], HY.bufs[0]
                S.op("dve", lambda e: e.tensor_tensor(out=sq[:, 0:1024], in0=xt[:], in1=xt[:], op=ALU.mult), reads=[Bx], writes=[Bsq])
                S.op("dve", lambda e: e.tensor_reduce(out=ss[:, 0:1], in_=sq[:, 0:1024], axis=AX.X, op=ALU.add), reads=[Bsq], writes=[Bss])
                S.op("dve", lambda e: e.tensor_scalar(out=ss[:, 1:2], in0=ss[:, 0:1], scalar1=1.0 / 1024, scalar2=EPS,
                                                      op0=ALU.mult, op1=ALU.add), reads=[Bss], writes=[Bss], acc=True)
                S.op("act", lambda e: e.activation(out=ss[:, 1:2], in_=ss[:, 1:2], func=AF.Ln), reads=[Bss], writes=[Bss], acc=True)
                S.op("act", lambda e: e.activation(out=ss[:, 1:2], in_=ss[:, 1:2], func=AF.Exp, scale=-0.5), reads=[Bss], writes=[Bss], acc=True)
                xn, Bxn = XN.next()
                S.op("act", lambda e: e.activation(out=xn[:], in_=xt[:], func=AF.Copy, scale=ss[:, 1:2]), reads=[Bx, Bss], writes=[Bxn])
                xns[n] = (xn, Bxn)

            def prepB(n):
                seg = n // SEG
                xn, Bxn = xns.pop(n)
                pt, Bp = PS.next()
                ptb = pt[:].bitcast(BF16)
                for k in range(8):
                    S.op("pe", lambda e, k=k: e.transpose(out=ptb[:, 128 * k:128 * (k + 1)], in_=xn[:, 128 * k:128 * (k + 1)], identity=ident_b),
                         reads=[Bxn, Bconst], writes=[Bp], acc=(k > 0))
                hT, Bh = HT.next()
                for k in range(8):
                    en = ("dve", "act")[k % 2]
                    if en == "dve":
                        S.op("dve", lambda e, k=k: e.tensor_scalar(out=hT[:, k, 1:129], in0=ptb[:, 128 * k:128 * (k + 1)],
                                                                   scalar1=a1T[:, k, seg:seg + 1], scalar2=modT[:, k, seg:seg + 1],
                                                                   op0=ALU.mult, op1=ALU.add), reads=[Bp, Bmod], writes=[Bh], acc=(k > 0))
                    else:
                        S.op("act", lambda e, k=k: e.activation(out=hT[:, k, 1:129], in_=ptb[:, 128 * k:128 * (k + 1)], func=AF.Identity,
                                                                scale=a1T[:, k, seg:seg + 1], bias=modT[:, k, seg:seg + 1]),
                             reads=[Bp, Bmod], writes=[Bh], acc=(k > 0))
                hslots[n] = (hT, Bh)
                S.dma("sp", HTD[n // 4, :, :, 128 * (n % 4):128 * (n % 4 + 1)], hT[:, :, 1:129], key=Bh, reads=[Bh], writes=[BHTD])
                if n == 0:
                    S.op("pool", lambda e: e.memset(hT[:, :, 0:1], 0.0), writes=[Bh], acc=True)
                else:
                    pT, Bph = hslots[n - 1]
                    if n == SEG:
                        S.op("dve", lambda e: e.tensor_scalar(out=hT[:, :, 0:1], in0=pT[:, :, 128:129], scalar1=FLAG, scalar2=None, op0=ALU.mult),
                             reads=[Bph, Bconst], writes=[Bh], acc=True)
                        S.op("dve", lambda e: e.tensor_scalar(out=pT[:, :, 129:130], in0=hT[:, :, 1:2], scalar1=FLAG, scalar2=None, op0=ALU.mult),
                             reads=[Bh, Bconst], writes=[Bph], acc=True)
                    else:
                        S.op("dve", lambda e: e.tensor_copy(out=hT[:, :, 0:1], in_=pT[:, :, 128:129]), reads=[Bph], writes=[Bh], acc=True)
                        S.op("dve", lambda e: e.tensor_copy(out=pT[:, :, 129:130], in_=hT[:, :, 1:2]), reads=[Bh], writes=[Bph], acc=True)
                if n == NT - 1:
                    S.op("pool", lambda e: e.memset(hT[:, :, 129:130], 0.0), writes=[Bh], acc=True)

            def mm_tile(n):
                hT, Bh = hslots[n]
                hy, Bhy = HY.next()
                for cb in range(3):
                    pt, Bp = PS.next()
                    i = 0
                    for j in range(3):
                        for k in range(8):
                            S.op("pe", lambda e, j=j, k=k, i=i: e.matmul(pt[:, :], lhsT=hT[:, k, j:j + 128],
                                                                         rhs=Wc[:, j, k, 512 * cb:512 * (cb + 1)],
                                                                         start=(i == 0), stop=(i == 23)),
                                 reads=[Bh, Bw], writes=[Bp], acc=(i > 0))
                            i += 1
                    S.op("dve", lambda e: e.tensor_tensor(out=hy[:, 512 * cb:512 * (cb + 1)], in0=pt[:, :], in1=cvb[:, 512 * cb:512 * (cb + 1)],
                                                          op=ALU.add), reads=[Bp, Bcv], writes=[Bhy], acc=(cb > 0))
                vp, Bvp = VP.next()
                S.op("dve", lambda e: e.tensor_tensor(out=vp[:], in0=hy[:, 1024:1536], in1=hy[:, 512:1024], op=ALU.mult),
                     reads=[Bhy], writes=[Bvp])
                x0, Bx0 = X0.next()
                S.op("act", lambda e: e.activation(out=x0[:], in_=hy[:, 0:512], func=AF.Copy), reads=[Bhy], writes=[Bx0])
                S.dma("sp", X0T[n], x0[:], key=Bx0, reads=[Bx0])
                io, Bio = IO.next()
                for ii, (c0, fn) in enumerate(((512, AF.Copy), (2048, AF.Silu))):
                    pt, Bp = PS.next()
                    for k in range(8):
                        S.op("pe", lambda e, k=k: e.matmul(pt[:, :], lhsT=hT[:, k, 1:129], rhs=Wr[:, k, c0:c0 + 512],
                                                           start=(k == 0), stop=(k == 7)), reads=[Bh, Bw], writes=[Bp], acc=(k > 0))
                    S.op("act", lambda e: e.activation(out=io[:, ii, :], in_=pt[:, :], func=fn), reads=[Bp], writes=[Bio], acc=(ii > 0))
                pt, Bp = PS.next()
                S.op("pe", lambda e: e.matmul(pt[:, :], lhsT=anti_b, rhs=vp[:], start=True, stop=True), reads=[Bvp, Bconst], writes=[Bp])
                vr, Bvr = VR.next()
                S.op("act", lambda e: e.activation(out=vr[:], in_=pt[:, :], func=AF.Copy), reads=[Bp], writes=[Bvr])
                S.dma("sp", VT[n], vr[:], key=Bvr, reads=[Bvr])
                S.dma("sp", IT[n], io[:, 0, :], key=Bio, reads=[Bio])
                S.dma("sp", OGT[n], io[:, 1, :], key=Bio, reads=[Bio])


            prepA(0)
            prepA(1)
            prepA(2)
            prepB(0)
            prepB(1)
            for n in range(NT):
                if n + 3 < NT:
                    prepA(n + 3)
                if n + 2 < NT:
                    prepB(n + 2)
                mm_tile(n)
            S.barrier()

        with ExitStack() as st:
            WrB = sb(st, "WrB", [128, 8, 3584], BF16)
            BwB = Buf("wB")
            with ExitStack() as st2:
                stg = [sb(st2, f"wstgB{i}", [128, 2048], F32) for i in range(2)]
                STG = Rot("wstgB", stg)
                ei = 0
                for k in range(8):
                    for (src0, wd, dst0) in ((1536, 512, 0), (2560, 1024, 512), (4096, 2048, 1536)):
                        stt, Bst = STG.next()
                        S.dma("sp", stt[:, 0:wd], w_in[128 * k:128 * (k + 1), src0:src0 + wd], key=Bst, writes=[Bst], acc=False)
                        en = ("dve", "pool", "act")[ei % 3]
                        ei += 1
                        if en == "act":
                            S.op("act", lambda e: e.activation(out=WrB[:, k, dst0:dst0 + wd], in_=stt[:, 0:wd], func=AF.Copy),
                                 reads=[Bst], writes=[BwB], acc=True)
                        else:
                            S.op(en, lambda e: e.tensor_copy(out=WrB[:, k, dst0:dst0 + wd], in_=stt[:, 0:wd]), reads=[Bst], writes=[BwB], acc=True)
                S.barrier()
            HS = Rot("hs", [sb(st, f"hs{i}", [128, 8, 512], BF16) for i in range(2)])
            QS = Rot("qs", [sb(st, f"qs{i}", [128, 4, 4, 128], BF16) for i in range(2)])
            SG = Rot("sg", [sb(st, f"sg{i}", [128, 4, 8, 128], F32) for i in range(1)])
            KS = Rot("ks", [sb(st, f"ks{i}", [128, 4, 8, 128], BF16) for i in range(2)])
            GS = Rot("gs", [sb(st, f"gs{i}", [128, 4, 8, 128], F32) for i in range(1)])
            GA = Rot("ga", [sb(st, f"ga{i}", [128, 4, 16, 128], BF16) for i in range(2)])
            def hs_load(su):
                hs, Bhs = HS.next()
                S.dma("sp", hs[:], HTD[su], key=Bhs, reads=[BHTD], writes=[Bhs], acc=False)
                return hs, Bhs

            hs_next = hs_load(0)
            for su in range(NT // 4):
                hs, Bhs = hs_next
                if su + 1 < NT // 4:
                    hs_next = hs_load(su + 1)

                def fm(c0):
                    pt, Bp = PS.next()
                    for k in range(8):
                        S.op("pe", lambda e, k=k: e.matmul(pt[:, :], lhsT=WrB[:, k, c0:c0 + 128], rhs=hs[:, k, :], start=(k == 0), stop=(k == 7)),
                             reads=[BwB, Bhs], writes=[Bp], acc=(k > 0))
                    return pt[:, :].rearrange("p (j t) -> p j t", t=128), Bp

                qs, Bqs = QS.next()
                for h in range(4):
                    pv, Bp = fm(128 * h)
                    S.op("act", lambda e, h=h: e.activation(out=qs[:, :, h, :], in_=pv, func=AF.Silu), reads=[Bp], writes=[Bqs], acc=(h > 0))
                sg, Bsg = SG.next()
                for dh in range(8):
                    pv, Bp = fm(512 + 128 * dh)
                    S.op("act", lambda e, dh=dh: e.activation(out=sg[:, :, dh, :], in_=pv, func=AF.Sigmoid), reads=[Bp], writes=[Bsg], acc=(dh > 0))
                    S.op("dve", lambda e, dh=dh: e.tensor_scalar(out=sg[:, :, dh, :], in0=sg[:, :, dh, :], scalar1=omlv[:, dh:dh + 1],
                                                                 scalar2=lbv[:, dh:dh + 1], op0=ALU.mult, op1=ALU.add),
                         reads=[Bsg, Bconst], writes=[Bsg], acc=True)
                ks, Bks = KS.next()
                gs, Bgs = GS.next()
                S.op("dve", lambda e: e.tensor_scalar(out=ks[:], in0=sg[:], scalar1=-1.0, scalar2=1.0, op0=ALU.mult, op1=ALU.add),
                     reads=[Bsg], writes=[Bks])
                S.op("act", lambda e: e.activation(out=gs[:], in_=sg[:], func=AF.Ln), reads=[Bsg], writes=[Bgs])
                ga, Bga = GA.next()
                for gq in range(16):
                    pv, Bp = fm(1536 + 128 * gq)
                    S.op("act", lambda e, gq=gq: e.activation(out=ga[:, :, gq, :], in_=pv, func=AF.Sigmoid), reads=[Bp], writes=[Bga], acc=(gq > 0))
                for j in range(4):
                    n = 4 * su + j
                    S.dma("pool", QT[n], qs[:, j, :, :].rearrange("p a b -> p (a b)"), key=Bqs, reads=[Bqs])
                    S.dma("pool", KFT[n], ks[:, j, :, :].rearrange("p (d h) t -> p d (h t)", d=2), key=Bks, reads=[Bks])
                    S.dma("pool", GFT[n], gs[:, j, :, :].rearrange("p (d h) t -> p d (h t)", d=2), key=Bgs, reads=[Bgs])
                    S.dma("pool", GTT[n], ga[:, j, :, :].rearrange("p a b -> p (a b)"), key=Bga, reads=[Bga])
            S.barrier()

        G = dscr("G", [512, 16384], BF16)
        YAT = dscr("YAT", [NT, 128, 4, 128], BF16)
        BG = Buf("G")
        RNb = sb(es, "RNb", [128, 3, 512], F32)
        Brn = Buf("rn")
        with ExitStack() as st:
            w1t = sb(st, "w1t", [33, 64], F32)
            w2t = sb(st, "w2t", [64, 64], F32)
            w3t = sb(st, "w3t", [64, 64], F32)
            wot = sb(st, "wot", [64, 1024], F32)
            fbt = sb(st, "fbt", [64, 3], F32)
            frq = sb(st, "frq", [64, 3], F32)
            bfq = sb(st, "bfq", [64, 3], F32)
            fcs = sb(st, "fcs", [128, 8], F32)
            fbi = sb(st, "fbi", [128, 4], F32)
            acol = sb(st, "acol", [128, 1], F32)
            wsc = sb(st, "wsc", [128, 4], F32)
            nsum = sb(st, "nsum", [128, 2, 4, 16], F32)
            lag0 = sb(st, "lag0", [128, 4], F32)
            nrm = sb(st, "nrm", [128, 4], F32)
            rn = sb(st, "rn", [128, 4], F32)
            pch = sb(st, "pch", [128, 4], BF16)
            dg = sb(st, "dgf", [128, 128], F32)
            Bfp, Bns, Bdg = Buf("fp"), Buf("ns"), Buf("dgf")
            for (t_, d_) in ((w1t, fw1), (w2t, fw2), (w3t, fw3), (wot, fwo), (fbt, fbT), (frq, ffreqT), (fcs, fconst), (fbi, fbiasT)):
                S.dma("sp", t_[:], d_[:, :], key=Bfp, writes=[Bfp])
            S.op("dve", lambda e: e.tensor_tensor(out=bfq[:], in0=fbt[:], in1=frq[:], op=ALU.mult), reads=[Bfp], writes=[Bfp], acc=True)
            S.op("dve", lambda e: e.tensor_tensor(out=acol[:], in0=fcs[:, 0:1], in1=meta_t[:, 3:4], op=ALU.mult),
                 reads=[Bfp, Bconst], writes=[Bfp], acc=True)
            S.op("dve", lambda e: e.tensor_scalar(out=wsc[:], in0=fcs[:, 2:6], scalar1=meta_t[:, 2:3], scalar2=-1.0, op0=ALU.mult, op1=ALU.mult),
                 reads=[Bfp, Bconst], writes=[Bfp], acc=True)
            S.op("pool", lambda e: e.memset(nsum[:], 0.0), writes=[Bns])
            NI = Rot("ni", [sb(st, f"ni{i}", [128, 512], I32) for i in range(2)])
            NF = Rot("nf", [sb(st, f"nf{i}", [128, 512], F32) for i in range(4)])
            MK = Rot("mk", [sb(st, f"mk{i}", [128, 512], F32) for i in range(2)])
            AR = Rot("ar", [sb(st, f"ar{i}", [64, 512], F32) for i in range(4)])
            KI = Rot("ki", [sb(st, f"ki{i}", [64, 512], I32) for i in range(4)])
            KF = Rot("kf", [sb(st, f"kf{i}", [64, 512], F32) for i in range(4)])
            HH = Rot("hh", [sb(st, f"hh{i}", [64, 512], F32) for i in range(6)])
            WN = Rot("wn", [sb(st, f"wn{i}", [128, 512], F32) for i in range(4)])
            HW = Rot("hw", [sb(st, f"hw{i}", [128, 512], F32) for i in range(4)])
            AB = Rot("ab", [sb(st, f"ab{i}", [128, 512], F32) for i in range(4)])
            HB = Rot("hb", [sb(st, f"hb{i}", [128, 512], BF16) for i in range(4)])

            def sin_reduce(arg, Barg, rows, out, Bout):
                ki, Bki = KI.next()
                kf, Bkf = KF.next()
                S.op("dve", lambda e: e.tensor_scalar(out=ki[0:rows, :], in0=arg[0:rows, :], scalar1=1.0 / TWO_PI, scalar2=None, op0=ALU.mult),
                     reads=[Barg], writes=[Bki])
                S.op("dve", lambda e: e.tensor_copy(out=kf[0:rows, :], in_=ki[0:rows, :]), reads=[Bki], writes=[Bkf])
                S.op("dve", lambda e: e.scalar_tensor_tensor(out=arg[0:rows, :], in0=kf[0:rows, :], scalar=-TWO_PI, in1=arg[0:rows, :],
                                                             op0=ALU.mult, op1=ALU.add), reads=[Bkf, Barg], writes=[Barg])
                S.op("dve", lambda e: e.tensor_scalar(out=arg[0:rows, :], in0=arg[0:rows, :], scalar1=-3.141592, scalar2=3.141592,
                                                      op0=ALU.max, op1=ALU.min), reads=[Barg], writes=[Barg])
                S.op("act", lambda e: e.activation(out=out[0:rows, :], in_=arg[0:rows, :], func=AF.Sin), reads=[Barg], writes=[Bout])

            def f_gen(tiles):
                for ptile in tiles:
                    m0 = 512 * ptile
                    ni, Bni = NI.next()
                    nf, Bnf = NF.next()
                    S.op("pool", lambda e: e.iota(ni[:], pattern=[[1, 512]], base=m0 - 8192, channel_multiplier=0), writes=[Bni])
                    S.op("dve", lambda e: e.tensor_copy(out=nf[:], in_=ni[:]), reads=[Bni], writes=[Bnf])
                    S.op("dve", lambda e: e.scalar_tensor_tensor(out=nf[:], in0=nf[:], scalar=-1.0, in1=nf[:], op0=ALU.mult, op1=ALU.max),
                         reads=[Bnf], writes=[Bnf])
                    mk, Bmk = MK.next()
                    S.op("dve", lambda e: e.tensor_scalar(out=mk[:], in0=nf[:], scalar1=meta_t[:, 4:5], scalar2=None, op0=ALU.is_ge),
                         reads=[Bnf, Bconst], writes=[Bmk])
                    S.op("dve", lambda e: e.scalar_tensor_tensor(out=mk[:], in0=mk[:], scalar=1.0e6, in1=nf[:], op0=ALU.mult, op1=ALU.add),
                         reads=[Bmk, Bnf], writes=[Bmk])
                    ar, Bar = AR.next()
                    S.op("dve", lambda e: e.tensor_scalar(out=ar[0:33, :], in0=nf[0:33, :], scalar1=acol[0:33, 0:1], scalar2=fcs[0:33, 1:2],
                                                          op0=ALU.mult, op1=ALU.add), reads=[Bnf, Bfp], writes=[Bar])
                    hcur, Bhc = HH.next()
                    sin_reduce(ar, Bar, 33, hcur, Bhc)
                    S.op("dve", lambda e: e.tensor_scalar(out=hcur[0:1, :], in0=nf[0:1, :], scalar1=meta_t[0:1, 2:3], scalar2=None, op0=ALU.mult),
                         reads=[Bnf, Bconst], writes=[Bhc], acc=True)
                    yield
                    K_in = 33
                    for li, wt in enumerate((w1t, w2t, w3t)):
                        pt, Bp = PS.next()
                        S.op("pe", lambda e: e.matmul(pt[0:64, :], lhsT=wt[0:K_in, :], rhs=hcur[0:K_in, :], start=True, stop=True),
                             reads=[Bfp, Bhc], writes=[Bp])
                        ar, Bar = AR.next()
                        S.op("dve", lambda e: e.tensor_scalar(out=ar[:, :], in0=pt[0:64, :], scalar1=frq[:, li:li + 1], scalar2=bfq[:, li:li + 1],
                                                              op0=ALU.mult, op1=ALU.add), reads=[Bp, Bfp], writes=[Bar])
                        hcur, Bhc = HH.next()
                        sin_reduce(ar, Bar, 64, hcur, Bhc)
                        K_in = 64
                        yield
                    bwd = ptile < 16
                    for j in range(4):
                        c0 = (512 if bwd else 0) + 128 * j
                        pt, Bp = PS.next()
                        S.op("pe", lambda e: e.matmul(pt[:, :], lhsT=wot[:, c0:c0 + 128], rhs=hcur[0:64, :], start=True, stop=True),
                             reads=[Bfp, Bhc], writes=[Bp])
                        wn, Bwn = WN.next()
                        S.op("act", lambda e: e.activation(out=wn[:], in_=mk[:], func=AF.Exp, scale=wsc[:, j:j + 1]), reads=[Bmk, Bfp], writes=[Bwn])
                        hw, Bhw = HW.next()
                        S.op("dve", lambda e: e.tensor_tensor(out=hw[:], in0=pt[:, :], in1=wn[:], op=ALU.mult), reads=[Bp, Bwn], writes=[Bhw])
                        ab, Bab = AB.next()
                        S.op("act", lambda e: e.activation(out=ab[:], in_=hw[:], func=AF.Abs), reads=[Bhw], writes=[Bab])
                        S.op("dve", lambda e: e.tensor_reduce(out=nsum[:, (0 if bwd else 1), j, (ptile % 16):(ptile % 16) + 1], in_=ab[:],
                                                              axis=AX.X, op=ALU.add), reads=[Bab], writes=[Bns], acc=True)
                        if ptile == 16:
                            S.op("dve", lambda e: e.tensor_copy(out=lag0[:, j:j + 1], in_=hw[:, 0:1]), reads=[Bhw], writes=[Bns], acc=True)
                        hb, Bhb = HB.next()
                        S.op("act", lambda e: e.activation(out=hb[:], in_=hw[:], func=AF.Copy), reads=[Bhw], writes=[Bhb])
                        S.dma("sp", G[128 * j:128 * (j + 1), m0:m0 + 512], hb[:], key=Bhb, reads=[Bhb], writes=[BG])
                        yield

            gens = [f_gen(list(range(0, 16))), f_gen(list(range(16, 32)))]
            alive = [True, True]
            while any(alive):
                for gi in range(2):
                    if alive[gi]:
                        try:
                            next(gens[gi])
                        except StopIteration:
                            alive[gi] = False
            S.op("dve", lambda e: e.tensor_reduce(out=nrm[:], in_=nsum[:].rearrange("p d j t -> p j d t"), axis=AX.XY, op=ALU.add),
                 reads=[Bns], writes=[Bns], acc=True)
            S.op("dve", lambda e: e.tensor_scalar(out=nrm[:], in0=nrm[:], scalar1=EPS, scalar2=None, op0=ALU.add), reads=[Bns], writes=[Bns], acc=True)
            S.op("dve", lambda e: e.reciprocal(out=rn[:], in_=nrm[:]), reads=[Bns], writes=[Bns], acc=True)
            S.op("dve", lambda e: e.tensor_tensor(out=nrm[:], in0=nrm[:], in1=fbi[:], op=ALU.mult), reads=[Bns, Bfp], writes=[Bns], acc=True)
            S.op("dve", lambda e: e.tensor_tensor(out=pch[:], in0=nrm[:], in1=lag0[:], op=ALU.add), reads=[Bns], writes=[Bns], acc=True)
            S.barrier()
            for j in range(4):
                S.dma("sp", G[128 * j:128 * (j + 1), 8192:8193], pch[:, j:j + 1], key=Bns, reads=[Bns], writes=[BG], allow_slow_non_contiguous=True)
                S.op("dve", lambda e: e.tensor_scalar(out=dg[:], in0=ident_f, scalar1=rn[:, j:j + 1], scalar2=None, op0=ALU.mult),
                     reads=[Bns, Bconst], writes=[Bdg])
                pt, Bp = PS.next()
                S.op("pe", lambda e: e.matmul(pt[:, 0:128], lhsT=ones_f[:], rhs=dg[:], start=True, stop=True), reads=[Bdg, Bconst], writes=[Bp])
                S.op("act", lambda e: e.activation(out=RNb[:, 0, 128 * j:128 * (j + 1)], in_=pt[:, 0:128], func=AF.Copy),
                     reads=[Bp], writes=[Brn], acc=True)
            S.op("dve", lambda e: e.tensor_scalar(out=RNb[:, 1, :], in0=RNb[:, 0, :], scalar1=FLAG, scalar2=None, op0=ALU.mult),
                 reads=[Brn, Bconst], writes=[Brn], acc=True)
            S.op("dve", lambda e: e.tensor_scalar(out=RNb[:, 2, :], in0=RNb[:, 0, :], scalar1=NFLAG, scalar2=None, op0=ALU.mult),
                 reads=[Brn, Bconst], writes=[Brn], acc=True)
            S.barrier()

        OF = dscr("OF", [NT, 128, 512], F32)
        YBT = dscr("YBT", [NT, 128, 4, 128], BF16)
        BOF = Buf("OF")
        with ExitStack() as st:
            V2 = sb(st, "V2", [128, 128, 96], BF16)
            X0g = sb(st, "X0g", [128, 64, 128], BF16)
            Yb = sb(st, "Yb", [128, 64, 128], BF16)
            HK = Rot("hk", [sb(st, f"hk{i}", [128, 16264], BF16) for i in range(2)])
            YS = Rot("ys", [sb(st, f"ys{i}", [128, 8, 128], BF16) for i in range(2)])
            TH = Rot("th", [sb(st, f"th{i}", [128, 2, 32], F32) for i in range(2)])
            TH0 = Rot("th0", [sb(st, f"th0{i}", [128, 32], F32) for i in range(2)])
            Bve, Bxg, Byb = Buf("vext"), Buf("x0g"), Buf("yb")
            VTv = VT.rearrange("n p c -> p n c")
            X0v = X0T.rearrange("n p c -> p n c")
            dlist = [0] + [dd for a in range(1, 64) for dd in (a, -a)]
            HBK = Rot("hbk", [psb[6], psb[7]])
            HBK.bufs = [PS.bufs[6], PS.bufs[7]]

            def h_gen():
                for cg in range(4):
                    S.dma("sp", Yb[:], VTv[:, :, 128 * cg:128 * (cg + 1)], key=Byb, writes=[Byb], acc=False)
                    S.dma("sp", X0g[:], X0v[:, :, 128 * cg:128 * (cg + 1)], key=Bxg, writes=[Bxg], acc=False)
                    S.op("dve", lambda e: e.tensor_copy(out=V2[:, :, 0:32].rearrange("p c s -> p s c"), in_=Yb[:, 0:32, :]),
                         reads=[Byb], writes=[Bve])
                    S.op("dve", lambda e: e.tensor_scalar(out=V2[:, :, 32:64].rearrange("p c s -> p s c"), in0=Yb[:, 32:64, :], scalar1=FLAG,
                                                          scalar2=None, op0=ALU.mult), reads=[Byb, Bconst], writes=[Bve], acc=True)
                    S.op("act", lambda e: e.activation(out=V2[:, :, 64:96].rearrange("p c s -> p s c"), in_=Yb[:, 32:64, :], func=AF.Copy,
                                                       scale=NFLAG), reads=[Byb, Bconst], writes=[Bve], acc=True)
                    for c in range(128):
                        cc = 128 * cg + c
                        hk, Bhk = HK.next()
                        src = bass.AP(G.tensor, G.offset + cc * 16384, [[1, 128], [1, 16257]])
                        S.dma("sp", hk[:, 0:16257], src, key=Bhk, reads=[BG], writes=[Bhk], acc=False)
                        pt, Bp = HBK.next()
                        for di, d in enumerate(dlist):
                            yd = 8065 + 128 * d
                            if d >= 0:
                                o0, o1, i0 = d, 96, 0
                            else:
                                o0, o1, i0 = 0, 96 + d, -d
                            nn = o1 - o0
                            S.op("pe", lambda e, yd=yd, o0=o0, o1=o1, i0=i0, nn=nn, di=di: e.matmul(
                                pt[:, o0:o1], lhsT=hk[:, yd:yd + 128], rhs=V2[:, c, i0:i0 + nn],
                                start=(di == 0), stop=(di == len(dlist) - 1), skip_group_check=True),
                                reads=[Bhk, Bve], writes=[Bp], acc=(di > 0))
                        th, Bth = TH.next()
                        th0, Bth0 = TH0.next()
                        S.op("act", lambda e: e.activation(out=th0[:], in_=pt[:, 0:32], func=AF.Copy, scale=RNb[:, 0, cc:cc + 1]),
                             reads=[Bp, Brn], writes=[Bth0])
                        S.op("pool", lambda e: e.tensor_tensor(out=Yb[:, 0:32, c], in0=th0[:], in1=X0g[:, 0:32, c], op=ALU.mult),
                             reads=[Bth0, Bxg], writes=[Byb], acc=True)
                        S.op("act", lambda e: e.activation(out=th[:, 0, :], in_=pt[:, 32:64], func=AF.Copy, scale=RNb[:, 1, cc:cc + 1]),
                             reads=[Bp, Brn], writes=[Bth])
                        S.op("dve", lambda e: e.scalar_tensor_tensor(out=th[:, 1, :], in0=pt[:, 64:96], scalar=RNb[:, 2, cc:cc + 1],
                                                                     in1=th[:, 0, :], op0=ALU.mult, op1=ALU.add),
                             reads=[Bp, Brn, Bth], writes=[Bth], acc=True)
                        S.op("dve", lambda e: e.tensor_tensor(out=Yb[:, 32:64, c], in0=th[:, 1, :], in1=X0g[:, 32:64, c], op=ALU.mult),
                             reads=[Bth, Bxg], writes=[Byb], acc=True)
                        yield
                    for n0 in range(0, NT, 8):
                        pt, Bp = HBK.next()
                        ptb = pt[:].bitcast(BF16)
                        for q in range(8):
                            S.op("pe", lambda e, q=q: e.transpose(out=ptb[:, 128 * q:128 * (q + 1)], in_=Yb[:, n0 + q, :], identity=ident_b),
                                 reads=[Byb, Bconst], writes=[Bp], acc=(q > 0))
                        ys, Bys = YS.next()
                        S.op("act", lambda e: e.activation(out=ys[:].rearrange("p a b -> p (a b)"), in_=ptb[:, :], func=AF.Copy), reads=[Bp], writes=[Bys])
                        S.dma("pool", YAT[n0:n0 + 8, :, cg, :].rearrange("n p t -> p n t"), ys[:], key=Bys, reads=[Bys])
                    yield

            ones512 = sb(st, "ones512", [128, 512], F32)
            gnb = sb(st, "gnb", [64, 512], F32)
            Sf = sb(st, "Sf", [128, 4, 128], F32)
            Sbf = sb(st, "Sbf", [128, 4, 128], BF16)
            BSf = [Buf(f"Sf{h}") for h in range(4)]
            BSb = [Buf(f"Sb{h}") for h in range(4)]
            Bg1 = Buf("g1c")
            S.op("pool", lambda e: e.memset(ones512[:], 1.0), writes=[Bg1])
            S.dma("sp", gnb[:], gnorm_b[0:64, :], key=Bg1, writes=[Bg1])
            QL = Rot("ql", [sb(st, f"ql{i}", [128, 512], BF16) for i in range(2)])
            KL = Rot("kl", [sb(st, f"kl{i}", [128, 512], BF16) for i in range(2)])
            GL = Rot("gl", [sb(st, f"gl{i}", [128, 512], F32) for i in range(2)])
            VL = Rot("vl", [sb(st, f"vl{i}", [64, 2, 512], BF16) for i in range(2)])
            PG = Rot("pg", [sb(st, f"pg{i}", [128, 520], F32) for i in range(2)])
            AA = Rot("aa", [sb(st, f"aa{i}", [128, 512], F32) for i in range(2)])
            X1 = Rot("x1", [sb(st, f"x1{i}", [128, 512], F32) for i in range(2)])
            X4 = Rot("x4", [sb(st, f"x4{i}", [128, 512], F32) for i in range(1)])
            EE = Rot("ee", [sb(st, f"ee{i}", [128, 4, 512], F32) for i in range(1)])
            DC = Rot("dc", [sb(st, f"dc{i}", [128, 8], F32) for i in range(2)])
            QK = Rot("qk", [sb(st, f"qk{i}", [128, 4, 512], BF16) for i in range(1)])
            AM = Rot("am", [sb(st, f"am{i}", [64, 64], BF16) for i in range(8)])
            KT = Rot("kt", [sb(st, f"kt{i}", [64, 128], BF16) for i in range(8)])
            OO = Rot("oo", [sb(st, f"oo{i}", [64, 2, 512], F32) for i in range(1)])
            OFL = Rot("ofl", [sb(st, f"ofl{i}", [64, 2, 512], F32) for i in range(1)])
            OGL = Rot("ogl", [sb(st, f"ogl{i}", [64, 2, 512], BF16) for i in range(1)])
            SQ = Rot("sq", [sb(st, f"sq{i}", [64, 2, 512], F32) for i in range(1)])
            MS = Rot("ms", [sb(st, f"ms{i}", [64, 8], F32) for i in range(2)])
            YB = Rot("yb", [sb(st, f"yb{i}", [64, 2, 512], BF16) for i in range(1)])
            YT = Rot("yt", [sb(st, f"yt{i}", [128, 4, 128], BF16) for i in range(2)])

            def g3(ap):
                return ap.rearrange("p (g t) -> p g t", t=64)

            def g_loads(d, n):
                ql, Bql = QL.next()
                kl, Bkl = KL.next()
                gl, Bgl = GL.next()
                vl, Bvl = VL.next()
                S.dma("sp", ql[:], QT[n], key=Bql, writes=[Bql], acc=False)
                S.dma("sp", kl[:], KFT[n, :, d, :], key=Bkl, writes=[Bkl], acc=False)
                S.dma("sp", gl[:], GFT[n, :, d, :], key=Bgl, writes=[Bgl], acc=False)
                S.dma("sp", vl[:], IT[n].rearrange("(c t) f -> t c f", c=2), key=Bvl, writes=[Bvl], acc=False)
                return (ql, Bql, kl, Bkl, gl, Bgl, vl, Bvl)

            def g_gen():
                steps = [(0, n) for n in range(NT)] + [(1, n) for n in range(NT - 1, -1, -1)]
                pend = g_loads(*steps[0]) if steps else None
                for si, (d, n) in enumerate(steps):
                    cur = pend
                    pend = g_loads(*steps[si + 1]) if si + 1 < len(steps) else None
                    if n == (0 if d == 0 else NT - 1):
                        for h in range(4):
                            S.op("pool", lambda e, h=h: e.memset(Sf[:, h, :], 0.0), writes=[BSf[h]])
                            S.op("pool", lambda e, h=h: e.memset(Sbf[:, h, :], 0.0), writes=[BSb[h]])
                    mask = cm_b[0:64, 256:320] if d == 0 else cm_b[0:64, 320:384]
                    if (d == 0 and n == SEG) or (d == 1 and n == SEG - 1):
                        for h in range(4):
                            S.op("dve", lambda e, h=h: e.tensor_scalar(out=Sf[:, h, :], in0=Sf[:, h, :], scalar1=FLAG, scalar2=None, op0=ALU.mult),
                                 reads=[Bconst], writes=[BSf[h]])
                            S.op("act", lambda e, h=h: e.activation(out=Sbf[:, h, :], in_=Sf[:, h, :], func=AF.Copy), reads=[BSf[h]], writes=[BSb[h]])
                    (ql, Bql, kl, Bkl, gl, Bgl, vl, Bvl) = cur
                    pg, Bpg = PG.next()
                    S.op("pool", lambda e: e.memset(pg[:, 0:1], 0.0), writes=[Bpg])
                    S.op("dve", lambda e: e.tensor_tensor_scan(out=pg[:, 1:513], data0=ones512[:], data1=gl[:], initial=0.0,
                                                               op0=ALU.mult, op1=ALU.add), reads=[Bgl, Bg1], writes=[Bpg], acc=True)
                    aa, Baa = AA.next()
                    S.op("dve", lambda e: e.tensor_tensor(out=g3(aa[:]), in0=g3(pg[:, 1:513]),
                                                          in1=g3(pg[:, 0:512])[:, :, 0:1].to_broadcast([128, 8, 64]), op=ALU.subtract),
                         reads=[Bpg], writes=[Baa])
                    if d == 1:
                        x1, Bx1 = X1.next()
                        S.op("dve", lambda e: e.tensor_tensor(out=x1[:], in0=gl[:], in1=aa[:], op=ALU.subtract), reads=[Bgl, Baa], writes=[Bx1])
                        aa2, Baa2 = AA.next()
                        S.op("dve", lambda e: e.tensor_tensor(out=g3(aa2[:]), in0=g3(x1[:]),
                                                              in1=g3(aa[:])[:, :, 63:64].to_broadcast([128, 8, 64]), op=ALU.add),
                             reads=[Bx1, Baa], writes=[Baa2])
                        aa, Baa = aa2, Baa2
                        lastc = 0
                    else:
                        lastc = 63
                    x1, Bx1 = X1.next()
                    x4, Bx4 = X4.next()
                    S.op("dve", lambda e: e.tensor_tensor(out=g3(x1[:]), in0=g3(aa[:]), in1=g3(aa[:])[:, :, 32:33].to_broadcast([128, 8, 64]),
                                                          op=ALU.subtract), reads=[Baa], writes=[Bx1])
                    S.op("pool", lambda e: e.tensor_tensor(out=g3(x4[:]), in0=g3(aa[:]),
                                                           in1=g3(aa[:])[:, :, lastc:lastc + 1].to_broadcast([128, 8, 64]), op=ALU.subtract),
                         reads=[Baa], writes=[Bx4])
                    ee, Bee = EE.next()
                    dc, Bdc = DC.next()
                    S.op("act", lambda e: e.activation(out=ee[:, 0, :], in_=x1[:], func=AF.Exp), reads=[Bx1], writes=[Bee])
                    S.op("act", lambda e: e.activation(out=ee[:, 1, :], in_=x1[:], func=AF.Exp, scale=-1.0), reads=[Bx1], writes=[Bee], acc=True)
                    S.op("act", lambda e: e.activation(out=ee[:, 2, :], in_=aa[:], func=AF.Exp), reads=[Baa], writes=[Bee], acc=True)
                    S.op("act", lambda e: e.activation(out=ee[:, 3, :], in_=x4[:], func=AF.Exp, scale=-1.0), reads=[Bx4], writes=[Bee], acc=True)
                    S.op("act", lambda e: e.activation(out=dc[:].unsqueeze(2), in_=g3(aa[:])[:, :, lastc:lastc + 1], func=AF.Exp), reads=[Baa], writes=[Bdc])
                    qk, Bqk = QK.next()
                    S.op("dve", lambda e: e.tensor_tensor(out=qk[:, 0, :], in0=ql[:], in1=ee[:, 0, :], op=ALU.mult), reads=[Bql, Bee], writes=[Bqk])
                    S.op("pool", lambda e: e.tensor_tensor(out=qk[:, 1, :], in0=kl[:], in1=ee[:, 1, :], op=ALU.mult), reads=[Bkl, Bee], writes=[Bqk], acc=True)
                    S.op("dve", lambda e: e.tensor_tensor(out=qk[:, 2, :], in0=ql[:], in1=ee[:, 2, :], op=ALU.mult), reads=[Bql, Bee], writes=[Bqk], acc=True)
                    S.op("pool", lambda e: e.tensor_tensor(out=qk[:, 3, :], in0=kl[:], in1=ee[:, 3, :], op=ALU.mult), reads=[Bkl, Bee], writes=[Bqk], acc=True)
                    yield
                    pos = (psb[0], psb[1])
                    Bpos = (PS.bufs[0], PS.bufs[1])
                    chunks = (0, 1) if d == 0 else (1, 0)
                    for c in chunks:
                        pA, BpA = psb[3][:, 256 * c:256 * (c + 1)], PS.bufs[3]
                        BpK = PS.bufs[2]
                        pKb = psb[2][:].bitcast(BF16)[:, 512 * c:512 * (c + 1)]
                        for h in range(4):
                            cs = slice(128 * h + 64 * c, 128 * h + 64 * c + 64)
                            S.op("pe", lambda e, h=h, cs=cs: e.matmul(pA[0:64, 64 * h:64 * (h + 1)], lhsT=qk[:, 1, cs], rhs=qk[:, 0, cs],
                                                                      start=True, stop=True), reads=[Bqk], writes=[BpA], acc=True)
                        for h in range(4):
                            cs = slice(128 * h + 64 * c, 128 * h + 64 * c + 64)
                            S.op("pe", lambda e, h=h, cs=cs: e.transpose(out=pKb[0:64, 128 * h:128 * (h + 1)], in_=qk[:, 3, cs], identity=ident_b),
                                 reads=[Bqk, Bconst], writes=[BpK], acc=True)
                        ams, kts = [], []
                        for h in range(4):
                            am, Bam = AM.next()
                            kt, Bkt = KT.next()
                            S.op("dve", lambda e, h=h: e.tensor_tensor(out=am[:], in0=pA[0:64, 64 * h:64 * (h + 1)], in1=mask, op=ALU.mult),
                                 reads=[BpA, Bconst], writes=[Bam])
                            S.op("act", lambda e, h=h: e.activation(out=kt[:], in_=pKb[0:64, 128 * h:128 * (h + 1)], func=AF.Copy), reads=[BpK], writes=[Bkt])
                            ams.append((am, Bam))
                            kts.append((kt, Bkt))
                        yield
                        for h in range(4):
                            cs = slice(128 * h + 64 * c, 128 * h + 64 * c + 64)
                            am, Bam = ams[h]
                            kt, Bkt = kts[h]
                            po = pos[c][0:64, 128 * h:128 * (h + 1)]
                            S.op("pe", lambda e, h=h: e.matmul(po, lhsT=am[:], rhs=vl[:, c, 128 * h:128 * (h + 1)], start=True, stop=False),
                                 reads=[Bam, Bvl], writes=[Bpos[c]], acc=(h > 0))
                            S.op("pe", lambda e, h=h, cs=cs: e.matmul(po, lhsT=qk[:, 2, cs], rhs=Sbf[:, h, :], start=False, stop=True),
                                 reads=[Bqk, BSb[h]], writes=[Bpos[c]], acc=True)
                            pS, BpS = psb[4 + c], PS.bufs[4 + c]
                            S.op("pe", lambda e, h=h: e.matmul(pS[:, 128 * h:128 * (h + 1)], lhsT=kt[:], rhs=vl[:, c, 128 * h:128 * (h + 1)], start=True, stop=True),
                                 reads=[Bkt, Bvl], writes=[BpS], acc=(h > 0))
                            S.op("dve", lambda e, h=h: e.scalar_tensor_tensor(out=Sf[:, h, :], in0=Sf[:, h, :], scalar=dc[:, 2 * h + c:2 * h + c + 1],
                                                                              in1=pS[:, 128 * h:128 * (h + 1)], op0=ALU.mult, op1=ALU.add),
                                 reads=[BpS, Bdc], writes=[BSf[h]])
                            S.op("act", lambda e, h=h: e.activation(out=Sbf[:, h, :], in_=Sf[:, h, :], func=AF.Copy), reads=[BSf[h]], writes=[BSb[h]])
                        yield
                    oo, Boo = OO.next()
                    OFv = OF[n].rearrange("(c t) f -> t c f", c=2)
                    if d == 0:
                        for c in range(2):
                            S.op("act", lambda e, c=c: e.activation(out=oo[:, c, :], in_=pos[c][0:64, :], func=AF.Copy), reads=[Bpos[c]], writes=[Boo], acc=(c > 0))
                        S.dma("pool", OFv, oo[:], key=Boo, reads=[Boo], writes=[BOF])
                    else:
                        ofl, Bofl = OFL.next()
                        ogl, Bogl = OGL.next()
                        S.dma("sp", ofl[:], OFv, key=Bofl, reads=[BOF], writes=[Bofl], acc=False)
                        S.dma("sp", ogl[:], OGT[n].rearrange("(c t) f -> t c f", c=2), key=Bogl, writes=[Bogl], acc=False)
                        for c in range(2):
                            S.op("dve", lambda e, c=c: e.tensor_tensor(out=oo[:, c, :], in0=pos[c][0:64, :], in1=ofl[:, c, :], op=ALU.add),
                                 reads=[Bpos[c], Bofl], writes=[Boo], acc=(c > 0))
                        sq_, Bsq_ = SQ.next()
                        ms, Bms = MS.next()
                        S.op("pool", lambda e: e.tensor_tensor(out=sq_[:], in0=oo[:], in1=oo[:], op=ALU.mult), reads=[Boo], writes=[Bsq_])
                        S.op("dve", lambda e: e.tensor_reduce(out=ms[:], in_=sq_[:].rearrange("p c (h v) -> p (c h) v", v=128), axis=AX.X, op=ALU.add),
                             reads=[Bsq_], writes=[Bms])
                        S.op("dve", lambda e: e.tensor_scalar(out=ms[:], in0=ms[:], scalar1=1.0 / 128, scalar2=EPS, op0=ALU.mult, op1=ALU.add),
                             reads=[Bms], writes=[Bms])
                        S.op("act", lambda e: e.activation(out=ms[:], in_=ms[:], func=AF.Ln), reads=[Bms], writes=[Bms])
                        S.op("act", lambda e: e.activation(out=ms[:], in_=ms[:], func=AF.Exp, scale=-0.5), reads=[Bms], writes=[Bms])
                        S.op("dve", lambda e: e.tensor_tensor(out=oo[:].rearrange("p c (h v) -> p (c h) v", v=128),
                                                              in0=oo[:].rearrange("p c (h v) -> p (c h) v", v=128),
                                                              in1=ms[:].unsqueeze(2).to_broadcast([64, 8, 128]), op=ALU.mult),
                             reads=[Boo, Bms], writes=[Boo])
                        S.op("pool", lambda e: e.tensor_tensor(out=oo[:], in0=oo[:], in1=gnb[:].unsqueeze(1).to_broadcast([64, 2, 512]), op=ALU.mult),
                             reads=[Boo, Bg1], writes=[Boo])
                        yb, Byb2 = YB.next()
                        S.op("dve", lambda e: e.tensor_tensor(out=yb[:], in0=oo[:], in1=ogl[:], op=ALU.mult), reads=[Boo, Bogl], writes=[Byb2])
                        pT, BpT = psb[2], PS.bufs[2]
                        pTb = pT[:].bitcast(BF16)
                        for h in range(4):
                            for c in range(2):
                                S.op("pe", lambda e, h=h, c=c: e.transpose(out=pTb[:, 128 * h + 64 * c:128 * h + 64 * c + 64],
                                                                           in_=yb[:, c, 128 * h:128 * (h + 1)], identity=ident_b[0:64, 0:64]),
                                     reads=[Byb2, Bconst], writes=[BpT], acc=(h + c > 0))
                        yt, Byt = YT.next()
                        S.op("act", lambda e: e.activation(out=yt[:].rearrange("p a b -> p (a b)"), in_=pTb[:, 0:512], func=AF.Copy), reads=[BpT], writes=[Byt])
                        S.dma("pool", YBT[n], yt[:], key=Byt, reads=[Byt])
                    yield

            gH, gG = h_gen(), g_gen()
            aliveH, aliveG = True, True
            it, ng = 0, 0
            while aliveH or aliveG:
                if aliveH:
                    try:
                        next(gH)
                    except StopIteration:
                        aliveH = False
                k = 1 + (it % 2)
                for _ in range(k):
                    if aliveG:
                        try:
                            next(gG)
                            ng += 1
                        except StopIteration:
                            aliveG = False
                it += 1
            S.barrier()

        X1 = dscr("X1", [NT, 128, 1024], F32)
        BX1 = Buf("X1")
        By = Buf("y")
        with ExitStack() as stc:
            GTb = sb(stc, "GTb", [128, 2, 2048], F32)
            Bgt = Buf("gtb")
            with ExitStack() as st:
                dg = sb(st, "dgc", [128, 128], F32)
                Bdg = Buf("dgc")
                for s_ in range(2):
                    for gi, j0 in enumerate((16, 40)):
                        for jj in range(8):
                            j = j0 + jj
                            S.op("dve", lambda e, j=j, s_=s_: e.tensor_scalar(out=dg[:], in0=ident_f, scalar1=modT[:, j, s_:s_ + 1], scalar2=None,
                                                                               op0=ALU.mult), reads=[Bmod, Bconst], writes=[Bdg])
                            pt, Bp = PS.next()
                            S.op("pe", lambda e: e.matmul(pt[:, 0:128], lhsT=ones_f[:], rhs=dg[:], start=True, stop=True),
                                 reads=[Bdg, Bconst], writes=[Bp])
                            c0 = gi * 1024 + jj * 128
                            S.op("act", lambda e, c0=c0, s_=s_: e.activation(out=GTb[:, s_, c0:c0 + 128], in_=pt[:, 0:128], func=AF.Copy),
                                 reads=[Bp], writes=[Bgt], acc=True)
                S.barrier()

            def load_w(st2, dst, src, nk, width, Bdst, tag):
                stg = [sb(st2, f"cstg{tag}{i}", [128, 2048], F32) for i in range(2)]
                STG = Rot("cstg" + tag, stg)
                ei = 0
                for k in range(nk):
                    for c0 in range(0, width, 2048):
                        wd = min(2048, width - c0)
                        stt, Bst = STG.next()
                        S.dma("sp", stt[:, 0:wd], src[128 * k:128 * (k + 1), c0:c0 + wd], key=Bst, writes=[Bst], acc=False)
                        en = ("dve", "pool", "act")[ei % 3]
                        ei += 1
                        if en == "act":
                            S.op("act", lambda e: e.activation(out=dst[:, k, c0:c0 + wd], in_=stt[:, 0:wd], func=AF.Copy),
                                 reads=[Bst], writes=[Bdst], acc=True)
                        else:
                            S.op(en, lambda e: e.tensor_copy(out=dst[:, k, c0:c0 + wd], in_=stt[:, 0:wd]), reads=[Bst], writes=[Bdst], acc=True)

            with ExitStack() as st:
                Wb = sb(st, "Wb", [128, 8, 1024], BF16)
                Wo = sb(st, "Wo", [128, 8, 1024], BF16)
                Bwb = Buf("wb")
                with ExitStack() as st2:
                    load_w(st2, Wb, w_branch, 8, 1024, Bwb, 'a')
                    load_w(st2, Wo, w_out, 8, 1024, Bwb, 'b')
                    S.barrier()
                XL = Rot("xl", [sb(st, f"xl{i}", [128, 1024], F32) for i in range(2)])
                YA = Rot("ya", [sb(st, f"ya{i}", [128, 4, 128], BF16) for i in range(2)])
                YBL = Rot("ybl", [sb(st, f"ybl{i}", [128, 4, 128], BF16) for i in range(2)])
                GL2 = Rot("gl2", [sb(st, f"gl2{i}", [128, 2048], BF16) for i in range(2)])
                T1 = Rot("t1", [sb(st, f"t1{i}", [128, 512], F32) for i in range(2)])
                T2 = Rot("t2", [sb(st, f"t2{i}", [128, 512], F32) for i in range(2)])
                MT = Rot("mt", [sb(st, f"mt{i}", [128, 8, 128], BF16) for i in range(2)])
                TX = Rot("tx", [sb(st, f"tx{i}", [128, 1024], F32) for i in range(2)])
                XO = Rot("xo", [sb(st, f"xo{i}", [128, 1024], F32) for i in range(2)])
                def ca_load(n):
                    xl, Bxl = XL.next()
                    ya, Bya = YA.next()
                    ybl, Bybl = YBL.next()
                    gl2, Bgl2 = GL2.next()
                    S.dma("sp", xl[:], x[128 * n:128 * (n + 1), :], key=Bxl, writes=[Bxl], acc=False)
                    S.dma("sp", ya[:], YAT[n], key=Bya, writes=[Bya], acc=False)
                    S.dma("sp", ybl[:], YBT[n], key=Bybl, writes=[Bybl], acc=False)
                    S.dma("sp", gl2[:], GTT[n], key=Bgl2, writes=[Bgl2], acc=False)
                    return (xl, Bxl, ya, Bya, ybl, Bybl, gl2, Bgl2)

                ca_next = ca_load(0)
                for n in range(NT):
                    seg = n // SEG
                    (xl, Bxl, ya, Bya, ybl, Bybl, gl2, Bgl2) = ca_next
                    if n + 1 < NT:
                        ca_next = ca_load(n + 1)
                    mt, Bmt = MT.next()
                    for q in range(2):
                        pa, Bpa = PS.next()
                        pb, Bpb = PS.next()
                        for r4 in range(4):
                            nch = 4 * q + r4
                            for cc in range(4):
                                S.op("pe", lambda e, r4=r4, nch=nch, cc=cc: e.matmul(pa[:, 128 * r4:128 * (r4 + 1)],
                                                                                     lhsT=Wb[:, cc, 128 * nch:128 * (nch + 1)], rhs=ya[:, cc, :],
                                                                                     start=(cc == 0), stop=(cc == 3)),
                                     reads=[Bwb, Bya], writes=[Bpa], acc=(r4 + cc > 0))
                        for r4 in range(4):
                            nch = 4 * q + r4
                            for cc in range(4):
                                S.op("pe", lambda e, r4=r4, nch=nch, cc=cc: e.matmul(pb[:, 128 * r4:128 * (r4 + 1)],
                                                                                     lhsT=Wb[:, 4 + cc, 128 * nch:128 * (nch + 1)], rhs=ybl[:, cc, :],
                                                                                     start=(cc == 0), stop=(cc == 3)),
                                     reads=[Bwb, Bybl], writes=[Bpb], acc=(r4 + cc > 0))
                        t1, Bt1 = T1.next()
                        t2, Bt2 = T2.next()
                        S.op("dve", lambda e: e.tensor_tensor(out=t1[:], in0=pa[:, :], in1=gl2[:, 512 * q:512 * (q + 1)], op=ALU.mult),
                             reads=[Bpa, Bgl2], writes=[Bt1])
                        S.op("dve", lambda e: e.tensor_tensor(out=t2[:], in0=pb[:, :], in1=gl2[:, 1024 + 512 * q:1024 + 512 * (q + 1)], op=ALU.mult),
                             reads=[Bpb, Bgl2], writes=[Bt2])
                        S.op("dve", lambda e: e.tensor_tensor(out=mt[:, 4 * q:4 * (q + 1), :].rearrange("p a b -> p (a b)"), in0=t1[:], in1=t2[:], op=ALU.add),
                             reads=[Bt1, Bt2], writes=[Bmt], acc=(q > 0))
                    tx, Btx = TX.next()
                    xo, Bxo = XO.next()
                    for dh in range(2):
                        po, Bpo = PS.next()
                        for nch in range(8):
                            S.op("pe", lambda e, nch=nch: e.matmul(po[:, :], lhsT=mt[:, nch, :], rhs=Wo[:, nch, 512 * dh:512 * (dh + 1)],
                                                                   start=(nch == 0), stop=(nch == 7)), reads=[Bmt, Bwb], writes=[Bpo], acc=(nch > 0))
                        S.op("dve", lambda e: e.tensor_tensor(out=tx[:, 512 * dh:512 * (dh + 1)], in0=po[:, :], in1=GTb[:, seg, 512 * dh:512 * (dh + 1)],
                                                              op=ALU.mult), reads=[Bpo, Bgt], writes=[Btx], acc=(dh > 0))
                    S.op("dve", lambda e: e.tensor_tensor(out=xo[:], in0=tx[:], in1=xl[:], op=ALU.add), reads=[Btx, Bxl], writes=[Bxo])
                    S.dma("sp", X1[n], xo[:], key=Bxo, reads=[Bxo], writes=[BX1])
                S.barrier()

            NSUP = NT // 4
            ATD = dscr("ATD", [NSUP, 128, 32 * 512], BF16)
            BATD = Buf("ATD")
            _csk = False

            def rstd_of(SQ2, src, Bsrc, ss, Bss, col):
                sq2, Bsq2 = SQ2.next()
                S.op("dve", lambda e: e.tensor_tensor(out=sq2[:], in0=src, in1=src, op=ALU.mult), reads=[Bsrc], writes=[Bsq2])
                S.op("dve", lambda e: e.tensor_reduce(out=ss[:, col:col + 1], in_=sq2[:], axis=AX.X, op=ALU.add), reads=[Bsq2], writes=[Bss], acc=True)
                S.op("dve", lambda e: e.tensor_scalar(out=ss[:, col + 1:col + 2], in0=ss[:, col:col + 1], scalar1=1.0 / 1024, scalar2=EPS,
                                                      op0=ALU.mult, op1=ALU.add), reads=[Bss], writes=[Bss], acc=True)
                S.op("act", lambda e: e.activation(out=ss[:, col + 1:col + 2], in_=ss[:, col + 1:col + 2], func=AF.Ln), reads=[Bss], writes=[Bss], acc=True)
                S.op("act", lambda e: e.activation(out=ss[:, col + 1:col + 2], in_=ss[:, col + 1:col + 2], func=AF.Exp, scale=-0.5),
                     reads=[Bss], writes=[Bss], acc=True)

            with ExitStack() as st:
                W1 = sb(st, "W1", [128, 8, 4096], BF16)
                Bw1 = Buf("w1")
                with ExitStack() as st2:
                    load_w(st2, W1, w_ff1, 8, 4096, Bw1, 'c')
                    S.barrier()
                XL = Rot("xl1", [sb(st, f"xl1{i}", [128, 1024], F32) for i in range(2)])
                SQ2 = Rot("sq2", [sb(st, f"sq2{i}", [128, 1024], F32) for i in range(1)])
                SS2 = Rot("ss2", [sb(st, f"ss2{i}", [128, 4], F32) for i in range(2)])
                XN2 = Rot("xn2", [sb(st, f"xn2{i}", [128, 1024], BF16) for i in range(8)])
                H2 = Rot("h2", [sb(st, f"h2{i}", [128, 8, 512], BF16) for i in range(2)])
                RR = Rot("rr", [sb(st, f"rr{i}", [128, 512], F32) for i in range(3)])
                AT = Rot("at", [sb(st, f"at{i}", [128, 32, 512], BF16) for i in range(2)])
                xn2s = {}

                def cb_stageA(su):
                    for j in range(4):
                        n = 4 * su + j
                        xl, Bxl = XL.next()
                        S.dma("sp", xl[:], X1[n], key=Bxl, reads=[BX1], writes=[Bxl], acc=False)
                        ss, Bss = SS2.next()
                        S.op("pool", lambda e: e.memset(ss[:], 0.0), writes=[Bss])
                        rstd_of(SQ2, xl[:], Bxl, ss, Bss, 0)
                        xn, Bxn = XN2.next()
                        S.op("act", lambda e: e.activation(out=xn[:], in_=xl[:], func=AF.Copy, scale=ss[:, 1:2]), reads=[Bxl, Bss], writes=[Bxn])
                        xn2s[n] = (xn, Bxn)

                h2s = {}

                def cb_stageB(su):
                    h2, Bh2 = H2.next()
                    h2s[su] = (h2, Bh2)
                    for j in range(4):
                        n = 4 * su + j
                        seg = n // SEG
                        xn, Bxn = xn2s.pop(n)
                        pt, Bp = PS.next()
                        ptb = pt[:].bitcast(BF16)
                        for k in range(8):
                            S.op("pe", lambda e, k=k: e.transpose(out=ptb[:, 128 * k:128 * (k + 1)], in_=xn[:, 128 * k:128 * (k + 1)], identity=ident_b),
                                 reads=[Bxn, Bconst], writes=[Bp], acc=(k > 0))
                        for k in range(8):
                            if k % 2 == 0:
                                S.op("dve", lambda e, k=k: e.tensor_scalar(out=h2[:, k, 128 * j:128 * (j + 1)], in0=ptb[:, 128 * k:128 * (k + 1)],
                                                                           scalar1=a2T[:, k, seg:seg + 1], scalar2=modT[:, 24 + k, seg:seg + 1],
                                                                           op0=ALU.mult, op1=ALU.add), reads=[Bp, Bmod], writes=[Bh2], acc=(j + k > 0))
                            else:
                                S.op("act", lambda e, k=k: e.activation(out=h2[:, k, 128 * j:128 * (j + 1)], in_=ptb[:, 128 * k:128 * (k + 1)],
                                                                        func=AF.Identity, scale=a2T[:, k, seg:seg + 1], bias=modT[:, 24 + k, seg:seg + 1]),
                                     reads=[Bp, Bmod], writes=[Bh2], acc=(j + k > 0))

                if not _csk:
                    cb_stageA(0)
                    cb_stageB(0)
                for su in range(0 if _csk else NSUP):
                    h2, Bh2 = h2s.pop(su)
                    if su + 1 < NSUP:
                        cb_stageA(su + 1)
                    at, Bat = AT.next()
                    for fch in range(32):
                        pf, Bpf = PS.next()
                        for k in range(8):
                            S.op("pe", lambda e, fch=fch, k=k: e.matmul(pf[:, :], lhsT=W1[:, k, 128 * fch:128 * (fch + 1)], rhs=h2[:, k, :],
                                                                        start=(k == 0), stop=(k == 7)), reads=[Bw1, Bh2], writes=[Bpf], acc=(k > 0))
                        rr, Brr = RR.next()
                        S.op("act", lambda e: e.activation(out=rr[:], in_=pf[:, :], func=AF.Relu), reads=[Bpf], writes=[Brr])
                        en = "dve"
                        S.op(en, lambda e, fch=fch: e.tensor_tensor(out=at[:, fch, :], in0=rr[:], in1=rr[:], op=ALU.mult),
                             reads=[Brr], writes=[Bat], acc=(fch > 0))
                    S.dma("pool", ATD[su], at[:].rearrange("p a b -> p (a b)"), key=Bat, reads=[Bat], writes=[BATD])
                    if su + 1 < NSUP:
                        cb_stageB(su + 1)
                S.barrier()

            with ExitStack() as st:
                W2 = sb(st, "W2", [128, 32, 1024], BF16)
                fgb = sb(st, "fgb", [128, 1024], F32)
                Bw2, Bfg = Buf("w2"), Buf("fgb")
                S.dma("sp", fgb[:], finalg_b[:, :], key=Bfg, writes=[Bfg])
                with ExitStack() as st2:
                    load_w(st2, W2, w_ff2, 32, 1024, Bw2, 'd')
                    S.barrier()
                XL = Rot("xl2", [sb(st, f"xl2{i}", [128, 1024], F32) for i in range(2)])
                SQ2 = Rot("sq3", [sb(st, f"sq3{i}", [128, 1024], F32) for i in range(1)])
                SS2 = Rot("ss3", [sb(st, f"ss3{i}", [128, 4], F32) for i in range(2)])
                AT = Rot("at2", [sb(st, f"at2{i}", [128, 32, 512], BF16) for i in range(2)])
                X2 = Rot("x2", [sb(st, f"x2{i}", [128, 1024], F32) for i in range(2)])
                YO = Rot("yo", [sb(st, f"yo{i}", [128, 1024], F32) for i in range(2)])
                def at_load(su):
                    at, Bat = AT.next()
                    S.dma("sp", at[:].rearrange("p a b -> p (a b)"), ATD[su], key=Bat, reads=[BATD], writes=[Bat], acc=False)
                    return at, Bat

                def xl_load(n):
                    xl, Bxl = XL.next()
                    S.dma("sp", xl[:], X1[n], key=Bxl, reads=[BX1], writes=[Bxl], acc=False)
                    return xl, Bxl

                at_next = None if _csk else at_load(0)
                xl_next = None if _csk else xl_load(0)
                for su in range(0 if _csk else NSUP):
                    at, Bat = at_next
                    if su + 1 < NSUP:
                        at_next = at_load(su + 1)
                    for j in range(4):
                        n = 4 * su + j
                        seg = n // SEG
                        xl, Bxl = xl_next
                        if n + 1 < NT:
                            xl_next = xl_load(n + 1)
                        x2, Bx2 = X2.next()
                        for dh in range(2):
                            po, Bpo = PS.next()
                            for fch in range(32):
                                S.op("pe", lambda e, fch=fch: e.matmul(po[:, :], lhsT=at[:, fch, 128 * j:128 * (j + 1)], rhs=W2[:, fch, 512 * dh:512 * (dh + 1)],
                                                                       start=(fch == 0), stop=(fch == 31)), reads=[Bat, Bw2], writes=[Bpo], acc=(fch > 0))
                            S.op("dve", lambda e: e.tensor_tensor(out=x2[:, 512 * dh:512 * (dh + 1)], in0=po[:, :],
                                                                  in1=GTb[:, seg, 1024 + 512 * dh:1024 + 512 * (dh + 1)], op=ALU.mult),
                                 reads=[Bpo, Bgt], writes=[Bx2], acc=(dh > 0))
                        S.op("dve", lambda e: e.tensor_tensor(out=x2[:], in0=x2[:], in1=xl[:], op=ALU.add), reads=[Bx2, Bxl], writes=[Bx2])
                        ss, Bss = SS2.next()
                        S.op("pool", lambda e: e.memset(ss[:], 0.0), writes=[Bss])
                        rstd_of(SQ2, x2[:], Bx2, ss, Bss, 2)
                        yo, Byo = YO.next()
                        S.op("act", lambda e: e.activation(out=yo[:], in_=x2[:], func=AF.Copy, scale=ss[:, 3:4]), reads=[Bx2, Bss], writes=[Byo])
                        S.op("dve", lambda e: e.tensor_tensor(out=yo[:], in0=yo[:], in1=fgb[:], op=ALU.mult), reads=[Byo, Bfg], writes=[Byo])
                        S.dma("sp", y[128 * n:128 * (n + 1), :], yo[:], key=Byo, reads=[Byo], writes=[By])
                S.wait_bufs("sp", [By])
                S.barrier()
        print("instructions:", S.nins)
    return nc


def _consts():
    cm = np.zeros((128, 512), np.float32)
    cm[:, 0:128] = np.eye(128)
    cm[:, 128:256] = np.eye(128)[::-1]
    lo = np.triu(np.ones((64, 64), np.float32))
    up = np.tril(np.ones((64, 64), np.float32))
    cm[0:64, 256:320] = lo
    cm[64:128, 256:320] = lo
    cm[0:64, 320:384] = up
    cm[64:128, 320:384] = up
    return cm


def make_in_maps(inp):
    f = lambda a: np.ascontiguousarray(np.asarray(a, dtype=np.float32))
    xp, xs = f(inp["x_prompt"]), f(inp["x_sample"])
    cp, cs = f(inp["c_prompt"]), f(inp["c_sample"])
    fT = lambda v, n: np.ascontiguousarray(v.reshape(n, 128).T)
    rep = lambda v: np.ascontiguousarray(np.broadcast_to(v.reshape(1, -1), (128, v.size)))
    shared = {
        "ada_w": f(inp["ada_w"][0]), "ada_bT": fT(f(inp["ada_b"][0]), 48),
        "g1T": fT(f(inp["norm1_g"][0]), 8), "g2T": fT(f(inp["norm2_g"][0]), 8),
        "w_in": f(inp["w_in"][0]),
        "convw_b": rep(f(inp["conv_w"][0]).reshape(-1)), "convb_b": rep(f(inp["conv_b"][0])),
        "fw1": f(inp["filt_w1"][0]), "fw2": f(inp["filt_w2"][0]), "fw3": f(inp["filt_w3"][0]), "fwo": f(inp["filt_wo"][0]),
        "fbT": np.ascontiguousarray(np.stack([f(inp["filt_b1"][0]), f(inp["filt_b2"][0]), f(inp["filt_b3"][0])], 1)),
        "ffreqT": np.ascontiguousarray(f(inp["filt_freq"][0]).T),
        "fbiasT": fT(f(inp["filt_bias"][0]), 4),
        "lbT": np.ascontiguousarray(f(inp["hgrn_lb"]).reshape(2, 2, 4, 128).transpose(3, 0, 1, 2).reshape(128, 16)),
        "gnorm_b": rep(f(inp["gnorm_g"][0])), "finalg_b": rep(f(inp["final_g"])),
        "w_branch": f(inp["w_branch"][0]), "w_out": f(inp["w_out"][0]),
        "w_ff1": f(inp["w_ff1"][0]), "w_ff2": f(inp["w_ff2"][0]),
        "cmat": _consts(),
    }
    fb = np.linspace(1e-4, 15.0, 16, dtype=np.float32)
    fconst = np.zeros((128, 8), np.float32)
    fconst[1:17, 0] = fb
    fconst[1:17, 1] = np.pi / 2
    fconst[17:33, 0] = -fb
    deltas = np.abs(np.linspace(math.log(1e-2) / 1.5, math.log(1e-2) / 0.3, 512, dtype=np.float32))
    fconst[:, 2:6] = deltas.reshape(4, 128).T
    shared["fconst"] = fconst
    maps = []
    for r in range(8):
        m = dict(shared)
        if r < 4:
            m["x"] = np.ascontiguousarray(xp[2 * r:2 * r + 2].reshape(8192, 1024))
            cc = cp[2 * r:2 * r + 2]
            L, flag = 4096.0, 0.0
        else:
            m["x"] = np.ascontiguousarray(xs[r - 4].reshape(8192, 1024))
            cc = np.stack([cs[r - 4], cs[r - 4]])
            L, flag = 8192.0, 1.0
        m["cT"] = np.ascontiguousarray(cc.reshape(2, 8, 128).transpose(2, 1, 0).reshape(128, 16))
        meta = np.zeros((128, 8), np.float32)
        meta[:, 0] = flag
        meta[:, 1] = 1.0 - flag
        meta[:, 2] = 1.0 / (L - 1.0)
        meta[:, 3] = TWO_PI / L
        meta[:, 4] = L
        m["meta"] = meta
        maps.append(m)
    return maps


def kernel(**inputs):
    nc = build()
    maps = make_in_maps(inputs)
    res = run_bass_kernel_spmd(nc, maps, core_ids=list(range(8)))
    ys = [np.asarray(res.results[r]["y"], dtype=np.float32) for r in range(8)]
    y_prompt = np.concatenate([ys[r].reshape(2, 4096, 1024) for r in range(4)], 0)
    y_sample = np.stack([ys[r].reshape(8192, 1024) for r in range(4, 8)], 0)
    return (y_prompt, y_sample)
```

```python
import math
import numpy as np
from contextlib import ExitStack
import concourse.bass as bass
import concourse.mybir as mybir
from concourse.bass_utils import run_bass_kernel_spmd

F32 = mybir.dt.float32
BF16 = mybir.dt.bfloat16
I32 = mybir.dt.int32
ALU = mybir.AluOpType
AF = mybir.ActivationFunctionType
AX = mybir.AxisListType

NT = 64
SEG = 32
EPS = 1e-6
TWO_PI = 2.0 * math.pi


class Buf:
    def __init__(self, name):
        self.name = name
        self.w = {}
        self.r = {}
        self.dsem = None
        self.dcnt = 0


class Sched:
    def __init__(self, nc, es):
        self.nc = nc
        self.es = es
        self.eng = {}
        self.sem = {}
        self.cnt = {}
        self.seen = {}
        self.dbufs = []
        self.nins = 0

    def add_engine(self, name, eng):
        self.eng[name] = eng
        self.sem[name] = self.es.enter_context(self.nc.semaphore("s_" + name))
        self.cnt[name] = 0
        self.seen[name] = {}

    def _wait(self, e, toks):
        eng = self.eng[e]
        seen = self.seen[e]
        for sid, (sem, val) in toks.items():
            if seen.get(sid, 0) >= val:
                continue
            eng.wait_ge(sem, val)
            self.nins += 1
            seen[sid] = val

    @staticmethod
    def _merge(dst, sid, sem, val):
        if sid not in dst or dst[sid][1] < val:
            dst[sid] = (sem, val)

    def _deps(self, reads, writes):
        toks = {}
        for b in reads:
            for sid, (sem, val) in b.w.items():
                self._merge(toks, sid, sem, val)
        for b in writes:
            for sid, (sem, val) in b.w.items():
                self._merge(toks, sid, sem, val)
            for sid, (sem, val) in b.r.items():
                self._merge(toks, sid, sem, val)
        return toks

    def op(self, e, fn, reads=(), writes=(), acc=False):
        toks = self._deps(reads, writes)
        if e == "pe":
            toks.pop(id(self.sem[e]), None)
        self._wait(e, toks)
        ins = fn(self.eng[e])
        self.cnt[e] += 1
        self.nins += 1
        sem = self.sem[e]
        ins.then_inc(sem, 1)
        sid, val = id(sem), self.cnt[e]
        for b in reads:
            self._merge(b.r, sid, sem, val)
        for b in writes:
            if acc:
                self._merge(b.w, sid, sem, val)
            else:
                b.w = {sid: (sem, val)}
                b.r = {}
        return ins

    def dma(self, q, out, in_, key, reads=(), writes=(), acc=True, **kw):
        toks = self._deps(reads, writes)
        self._wait(q, toks)
        if key.dsem is None:
            key.dsem = self.es.enter_context(self.nc.semaphore("d_" + key.name))
            self.dbufs.append(key)
        ins = self.eng[q].dma_start(out=out, in_=in_, **kw)
        key.dcnt += 1
        self.nins += 1
        ins.then_inc(key.dsem, 16)
        sid, sem, val = id(key.dsem), key.dsem, 16 * key.dcnt
        for b in reads:
            self._merge(b.r, sid, sem, val)
        for b in writes:
            if acc:
                self._merge(b.w, sid, sem, val)
            else:
                b.w = {sid: (sem, val)}
                b.r = {}
        return ins

    def wait_bufs(self, e, bufs):
        toks = {}
        for b in bufs:
            for sid, (sem, val) in b.w.items():
                self._merge(toks, sid, sem, val)
            for sid, (sem, val) in b.r.items():
                self._merge(toks, sid, sem, val)
        self._wait(e, toks)

    def barrier(self):
        toks = {}
        for e, sem in self.sem.items():
            if self.cnt[e] > 0:
                toks[id(sem)] = (sem, self.cnt[e])
        for b in self.dbufs:
            toks[id(b.dsem)] = (b.dsem, 16 * b.dcnt)
        for e in self.eng:
            t = dict(toks)
            t.pop(id(self.sem[e]), None)
            self._wait(e, t)


class Rot:
    def __init__(self, name, aps):
        self.aps = aps
        self.bufs = [Buf(f"{name}{i}") for i in range(len(aps))]
        self.i = -1

    def next(self):
        self.i = (self.i + 1) % len(self.aps)
        return self.aps[self.i], self.bufs[self.i]


def build(dbg=False):
    nc = bass.Bass("TRN2", target_bir_lowering=False)

    def din(name, shape, dt=F32):
        return nc.dram_tensor(name, list(shape), dt, kind="ExternalInput").ap()

    def dscr(name, shape, dt):
        kind = "ExternalOutput" if dbg else "Internal"
        return nc.dram_tensor(name, list(shape), dt, kind=kind).ap()

    x = din("x", [NT * 128, 1024])
    cT = din("cT", [128, 16])
    meta = din("meta", [128, 8])
    ada_w = din("ada_w", [1024, 6144])
    ada_bT = din("ada_bT", [128, 48])
    g1T = din("g1T", [128, 8])
    g2T = din("g2T", [128, 8])
    w_in = din("w_in", [1024, 6144])
    convw_b = din("convw_b", [128, 3 * 1536])
    convb_b = din("convb_b", [128, 1536])
    fw1 = din("fw1", [33, 64])
    fw2 = din("fw2", [64, 64])
    fw3 = din("fw3", [64, 64])
    fwo = din("fwo", [64, 1024])
    fbT = din("fbT", [64, 3])
    ffreqT = din("ffreqT", [64, 3])
    fbiasT = din("fbiasT", [128, 4])
    fconst = din("fconst", [128, 8])
    lbT = din("lbT", [128, 16])
    gnorm_b = din("gnorm_b", [128, 512])
    finalg_b = din("finalg_b", [128, 1024])
    w_branch = din("w_branch", [1024, 1024])
    w_out = din("w_out", [1024, 1024])
    w_ff1 = din("w_ff1", [1024, 4096])
    w_ff2 = din("w_ff2", [4096, 1024])
    cmat = din("cmat", [128, 4 * 128])
    y = nc.dram_tensor("y", [NT * 128, 1024], F32, kind="ExternalOutput").ap()

    VT = dscr("VT", [4, 128, NT, 128], BF16)
    X0T = dscr("X0T", [4, 128, NT, 128], BF16)
    IT = dscr("IT", [NT, 128, 512], BF16)
    OGT = dscr("OGT", [NT, 128, 512], BF16)
    QT = dscr("QT", [NT, 128, 512], BF16)
    KFT = dscr("KFT", [NT, 128, 2, 512], BF16)
    GFT = dscr("GFT", [NT, 128, 2, 512], F32)
    GTT = dscr("GTT", [NT, 128, 2048], BF16)
    HTD = dscr("HTD", [NT // 4, 128, 8, 512], BF16)

    with ExitStack() as es:
        S = Sched(nc, es)
        S.add_engine("sp", nc.sync)
        S.add_engine("act", nc.scalar)
        S.add_engine("dve", nc.vector)
        S.add_engine("pool", nc.gpsimd)
        S.add_engine("pe", nc.tensor)

        def sb(st, name, shape, dt):
            return st.enter_context(nc.sbuf_tensor(name, list(shape), dt))

        psb = [es.enter_context(nc.psum_tensor(f"ps{i}", [128, 512], F32)) for i in range(8)]
        PS = Rot("ps", psb)

        cm_f = sb(es, "cm_f", [128, 512], F32)
        cm_b = sb(es, "cm_b", [128, 512], BF16)
        ones_f = sb(es, "ones_f", [128, 128], F32)
        meta_t = sb(es, "meta_t", [128, 8], F32)
        modT = sb(es, "modT", [128, 48, 2], F32)
        a1T = sb(es, "a1T", [128, 8, 2], F32)
        a2T = sb(es, "a2T", [128, 8, 2], F32)
        lb_t = sb(es, "lb_t", [128, 16], F32)
        lbv = sb(es, "lbv", [128, 8], F32)
        omlv = sb(es, "omlv", [128, 8], F32)
        Bconst = Buf("const")
        Bmod = Buf("mod")
        ident_f = cm_f[:, 0:128]
        ident_b = cm_b[:, 0:128]
        anti_b = cm_b[:, 128:256]
        FLAG = meta_t[:, 0:1]
        NFLAG = meta_t[:, 1:2]

        S.dma("sp", cm_f[:], cmat[:, :], key=Bconst, writes=[Bconst])
        S.dma("sp", meta_t[:], meta[:, :], key=Bconst, writes=[Bconst])
        S.dma("sp", lb_t[:], lbT[:, :], key=Bconst, writes=[Bconst])
        S.op("dve", lambda e: e.tensor_copy(out=cm_b[:], in_=cm_f[:]), reads=[Bconst], writes=[Bconst], acc=True)
        S.op("pool", lambda e: e.memset(ones_f[:], 1.0), writes=[Bconst], acc=True)
        S.op("dve", lambda e: e.tensor_tensor(out=lbv[:], in0=lb_t[:, 0:8], in1=lb_t[:, 8:16], op=ALU.subtract),
             reads=[Bconst], writes=[Bconst], acc=True)
        S.op("act", lambda e: e.activation(out=lbv[:], in_=lbv[:], func=AF.Sigmoid), reads=[Bconst], writes=[Bconst], acc=True)
        S.op("dve", lambda e: e.tensor_scalar(out=omlv[:], in0=lbv[:], scalar1=-1.0, scalar2=1.0, op0=ALU.mult, op1=ALU.add),
             reads=[Bconst], writes=[Bconst], acc=True)

        with ExitStack() as st:
            scT = sb(st, "scT", [128, 16], F32)
            adab = sb(st, "adab", [128, 48], F32)
            g12 = sb(st, "g12", [128, 16], F32)
            acc = sb(st, "acc", [128, 96], F32)
            stg = [sb(st, f"adastg{i}", [128, 3072], F32) for i in range(2)]
            STG = Rot("adastg", stg)
            Bsc, Bacc = Buf("sc"), Buf("acc")
            S.dma("sp", scT[:], cT[:, :], key=Bsc, writes=[Bsc])
            S.dma("sp", adab[:], ada_bT[:, :], key=Bsc, writes=[Bsc])
            S.dma("sp", g12[:, 0:8], g1T[:, :], key=Bsc, writes=[Bsc])
            S.dma("sp", g12[:, 8:16], g2T[:, :], key=Bsc, writes=[Bsc])
            S.op("act", lambda e: e.activation(out=scT[:], in_=scT[:], func=AF.Silu), reads=[Bsc], writes=[Bsc], acc=True)
            S.op("pool", lambda e: e.memset(acc[:], 0.0), writes=[Bacc])
            for k in range(8):
                for hf in range(2):
                    stt, Bst = STG.next()
                    S.dma("sp", stt[:], ada_w[128 * k:128 * (k + 1), 3072 * hf:3072 * (hf + 1)], key=Bst, writes=[Bst], acc=False)
                    pt, Bp = PS.next()
                    for jj in range(24):
                        S.op("pe", lambda e, jj=jj: e.matmul(pt[:, 2 * jj:2 * jj + 2], lhsT=stt[:, 128 * jj:128 * (jj + 1)],
                                                             rhs=scT[:, 2 * k:2 * k + 2], start=True, stop=True),
                             reads=[Bst, Bsc], writes=[Bp], acc=(jj > 0))
                    S.op("dve", lambda e: e.tensor_tensor(out=acc[:, 48 * hf:48 * (hf + 1)], in0=acc[:, 48 * hf:48 * (hf + 1)],
                                                          in1=pt[:, 0:48], op=ALU.add), reads=[Bp, Bacc], writes=[Bacc], acc=True)
            S.op("dve", lambda e: e.tensor_tensor(out=modT[:], in0=acc[:].rearrange("p (j s) -> p j s", s=2),
                                                  in1=adab[:].unsqueeze(2).to_broadcast([128, 48, 2]), op=ALU.add),
                 reads=[Bacc, Bsc], writes=[Bmod])
            for (aT, goff, joff) in ((a1T, 0, 8), (a2T, 8, 32)):
                S.op("dve", lambda e, aT=aT, joff=joff: e.tensor_scalar(out=aT[:], in0=modT[:, joff:joff + 8, :], scalar1=1.0,
                                                                         scalar2=None, op0=ALU.add), reads=[Bmod], writes=[Bmod], acc=True)
                S.op("dve", lambda e, aT=aT, goff=goff: e.tensor_tensor(out=aT[:], in0=aT[:],
                                                                         in1=g12[:, goff:goff + 8].unsqueeze(2).to_broadcast([128, 8, 2]),
                                                                         op=ALU.mult), reads=[Bmod, Bsc], writes=[Bmod], acc=True)
            S.barrier()

        with ExitStack() as st:
            Wc = sb(st, "Wc", [128, 3, 8, 1536], BF16)
            Wr = sb(st, "Wr", [128, 8, 4608], BF16)
            Bw = Buf("w")
            with ExitStack() as st2:
                cwb = sb(st2, "cwb", [128, 4608], F32)
                stg = [sb(st2, f"wstg{i}", [128, 3072], F32) for i in range(2)]
                STG = Rot("wstg", stg)
                Bcw = Buf("cw")
                S.dma("sp", cwb[:], convw_b[:, :], key=Bcw, writes=[Bcw])
                ei = 0
                for k in range(8):
                    for hf in range(2):
                        stt, Bst = STG.next()
                        S.dma("sp", stt[:], w_in[128 * k:128 * (k + 1), 3072 * hf:3072 * (hf + 1)], key=Bst, writes=[Bst], acc=False)
                        if hf == 0:
                            for j in range(3):
                                en = ("dve", "pool")[ei % 2]
                                ei += 1
                                S.op(en, lambda e, j=j: e.tensor_tensor(out=Wc[:, j, k, :], in0=stt[:, 0:1536],
                                                                        in1=cwb[:, 1536 * j:1536 * (j + 1)], op=ALU.mult),
                                     reads=[Bst, Bcw], writes=[Bw], acc=True)
                            S.op("act", lambda e: e.activation(out=Wr[:, k, 0:1536], in_=stt[:, 1536:3072], func=AF.Copy),
                                 reads=[Bst], writes=[Bw], acc=True)
                        else:
                            S.op("act", lambda e: e.activation(out=Wr[:, k, 1536:3072], in_=stt[:, 0:1536], func=AF.Copy),
                                 reads=[Bst], writes=[Bw], acc=True)
                            en = ("dve", "pool")[ei % 2]
                            ei += 1
                            S.op(en, lambda e: e.tensor_copy(out=Wr[:, k, 3072:4608], in_=stt[:, 1536:3072]),
                                 reads=[Bst], writes=[Bw], acc=True)
                S.barrier()

            cvb = sb(st, "cvb", [128, 1536], F32)
            Bcv = Buf("cvb")
            S.dma("sp", cvb[:], convb_b[:, :], key=Bcv, writes=[Bcv])
            XT = Rot("xt", [sb(st, f"xt{i}", [128, 1024], F32) for i in range(2)])
            SS = Rot("ss", [sb(st, f"ss{i}", [128, 2], F32) for i in range(2)])
            XN = Rot("xn", [sb(st, f"xn{i}", [128, 1024], BF16) for i in range(3)])
            HT = Rot("hT", [sb(st, f"hT{i}", [128, 8, 130], BF16) for i in range(4)])
            HY = Rot("hy", [sb(st, f"hy{i}", [128, 1536], F32) for i in range(1)])
            VP = Rot("vp", [sb(st, f"vp{i}", [128, 512], BF16) for i in range(2)])
            VR = Rot("vr", [sb(st, f"vr{i}", [128, 512], BF16) for i in range(2)])
            X0 = Rot("x0", [sb(st, f"x0{i}", [128, 512], BF16) for i in range(2)])
            IO = Rot("io", [sb(st, f"io{i}", [128, 2, 512], BF16) for i in range(2)])

            hslots = {}
            BHTD = Buf("HTD")

            xns = {}

            def prepA(n):
                xt, Bx = XT.next()
                S.dma("sp", xt[:], x[128 * n:128 * (n + 1), :], key=Bx, writes=[Bx], acc=False)
                ss, Bss = SS.next()
                sq, Bsq = HY.aps[0], HY.bufs[0]
                S.op("dve", lambda e: e.tensor_tensor(out=sq[:, 0:1024], in0=xt[:], in1=xt[:], op=ALU.mult), reads=[Bx], writes=[Bsq])
                S.op("dve", lambda e: e.tensor_reduce(out=ss[:, 0:1], in_=sq[:, 0:1024], axis=AX.X, op=ALU.add), reads=[Bsq], writes=[Bss])
                S.op("dve", lambda e: e.tensor_scalar(out=ss[:, 1:2], in0=ss[:, 0:1], scalar1=1.0 / 1024, scalar2=EPS,
                                                      op0=ALU.mult, op1=ALU.add), reads=[Bss], writes=[Bss], acc=True)
                S.op("act", lambda e: e.activation(out=ss[:, 1:2], in_=ss[:, 1:2], func=AF.Ln), reads=[Bss], writes=[Bss], acc=True)
                S.op("act", lambda e: e.activation(out=ss[:, 1:2], in_=ss[:, 1:2], func=AF.Exp, scale=-0.5), reads=[Bss], writes=[Bss], acc=True)
                xn, Bxn = XN.next()
                S.op("act", lambda e: e.activation(out=xn[:], in_=xt[:], func=AF.Copy, scale=ss[:, 1:2]), reads=[Bx, Bss], writes=[Bxn])
                xns[n] = (xn, Bxn)

            def prepB(n):
                seg = n // SEG
                xn, Bxn = xns.pop(n)
                pt, Bp = PS.next()
                ptb = pt[:].bitcast(BF16)
                for k in range(8):
                    S.op("pe", lambda e, k=k: e.transpose(out=ptb[:, 128 * k:128 * (k + 1)], in_=xn[:, 128 * k:128 * (k + 1)], identity=ident_b),
                         reads=[Bxn, Bconst], writes=[Bp], acc=(k > 0))
                hT, Bh = HT.next()
                for k in range(8):
                    en = ("dve", "act")[k % 2]
                    if en == "dve":
                        S.op("dve", lambda e, k=k: e.tensor_scalar(out=hT[:, k, 1:129], in0=ptb[:, 128 * k:128 * (k + 1)],
                                                                   scalar1=a1T[:, k, seg:seg + 1], scalar2=modT[:, k, seg:seg + 1],
                                                                   op0=ALU.mult, op1=ALU.add), reads=[Bp, Bmod], writes=[Bh], acc=(k > 0))
                    else:
                        S.op("act", lambda e, k=k: e.activation(out=hT[:, k, 1:129], in_=ptb[:, 128 * k:128 * (k + 1)], func=AF.Identity,
                                                                scale=a1T[:, k, seg:seg + 1], bias=modT[:, k, seg:seg + 1]),
                             reads=[Bp, Bmod], writes=[Bh], acc=(k > 0))
                hslots[n] = (hT, Bh)
                S.dma("sp", HTD[n // 4, :, :, 128 * (n % 4):128 * (n % 4 + 1)], hT[:, :, 1:129], key=Bh, reads=[Bh], writes=[BHTD])
                if n == 0:
                    S.op("pool", lambda e: e.memset(hT[:, :, 0:1], 0.0), writes=[Bh], acc=True)
                else:
                    pT, Bph = hslots[n - 1]
                    if n == SEG:
                        S.op("dve", lambda e: e.tensor_scalar(out=hT[:, :, 0:1], in0=pT[:, :, 128:129], scalar1=FLAG, scalar2=None, op0=ALU.mult),
                             reads=[Bph, Bconst], writes=[Bh], acc=True)
                        S.op("dve", lambda e: e.tensor_scalar(out=pT[:, :, 129:130], in0=hT[:, :, 1:2], scalar1=FLAG, scalar2=None, op0=ALU.mult),
                             reads=[Bh, Bconst], writes=[Bph], acc=True)
                    else:
                        S.op("dve", lambda e: e.tensor_copy(out=hT[:, :, 0:1], in_=pT[:, :, 128:129]), reads=[Bph], writes=[Bh], acc=True)
                        S.op("dve", lambda e: e.tensor_copy(out=pT[:, :, 129:130], in_=hT[:, :, 1:2]), reads=[Bh], writes=[Bph], acc=True)
                if n == NT - 1:
                    S.op("pool", lambda e: e.memset(hT[:, :, 129:130], 0.0), writes=[Bh], acc=True)

            def mm_tile(n):
                hT, Bh = hslots[n]
                hy, Bhy = HY.next()
                for cb in range(3):
                    pt, Bp = PS.next()
                    i = 0
                    for j in range(3):
                        for k in range(8):
                            S.op("pe", lambda e, j=j, k=k, i=i: e.matmul(pt[:, :], lhsT=hT[:, k, j:j + 128],
                                                                         rhs=Wc[:, j, k, 512 * cb:512 * (cb + 1)],
                                                                         start=(i == 0), stop=(i == 23)),
                                 reads=[Bh, Bw], writes=[Bp], acc=(i > 0))
                            i += 1
                    S.op("dve", lambda e: e.tensor_tensor(out=hy[:, 512 * cb:512 * (cb + 1)], in0=pt[:, :], in1=cvb[:, 512 * cb:512 * (cb + 1)],
                                                          op=ALU.add), reads=[Bp, Bcv], writes=[Bhy], acc=(cb > 0))
                vp, Bvp = VP.next()
                S.op("dve", lambda e: e.tensor_tensor(out=vp[:], in0=hy[:, 1024:1536], in1=hy[:, 512:1024], op=ALU.mult),
                     reads=[Bhy], writes=[Bvp])
                x0, Bx0 = X0.next()
                S.op("act", lambda e: e.activation(out=x0[:], in_=hy[:, 0:512], func=AF.Copy), reads=[Bhy], writes=[Bx0])
                S.dma("sp", X0T[:, :, n, :].rearrange("g p c -> p g c"), x0[:].rearrange("p (g c) -> p g c", g=4), key=Bx0, reads=[Bx0])
                io, Bio = IO.next()
                for ii, (c0, fn) in enumerate(((512, AF.Copy), (2048, AF.Silu))):
                    pt, Bp = PS.next()
                    for k in range(8):
                        S.op("pe", lambda e, k=k: e.matmul(pt[:, :], lhsT=hT[:, k, 1:129], rhs=Wr[:, k, c0:c0 + 512],
                                                           start=(k == 0), stop=(k == 7)), reads=[Bh, Bw], writes=[Bp], acc=(k > 0))
                    S.op("act", lambda e: e.activation(out=io[:, ii, :], in_=pt[:, :], func=fn), reads=[Bp], writes=[Bio], acc=(ii > 0))
                pt, Bp = PS.next()
                S.op("pe", lambda e: e.matmul(pt[:, :], lhsT=anti_b, rhs=vp[:], start=True, stop=True), reads=[Bvp, Bconst], writes=[Bp])
                vr, Bvr = VR.next()
                S.op("act", lambda e: e.activation(out=vr[:], in_=pt[:, :], func=AF.Copy), reads=[Bp], writes=[Bvr])
                S.dma("sp", VT[:, :, n, :].rearrange("g p c -> p g c"), vr[:].rearrange("p (g c) -> p g c", g=4), key=Bvr, reads=[Bvr])
                S.dma("sp", IT[n], io[:, 0, :], key=Bio, reads=[Bio])
                S.dma("sp", OGT[n], io[:, 1, :], key=Bio, reads=[Bio])


            prepA(0)
            prepA(1)
            prepA(2)
            prepB(0)
            prepB(1)
            for n in range(NT):
                if n + 3 < NT:
                    prepA(n + 3)
                if n + 2 < NT:
                    prepB(n + 2)
                mm_tile(n)
            S.barrier()

        with ExitStack() as st:
            WrB = sb(st, "WrB", [128, 8, 3584], BF16)
            BwB = Buf("wB")
            with ExitStack() as st2:
                stg = [sb(st2, f"wstgB{i}", [128, 2048], F32) for i in range(2)]
                STG = Rot("wstgB", stg)
                ei = 0
                for k in range(8):
                    for (src0, wd, dst0) in ((1536, 512, 0), (2560, 1024, 512), (4096, 2048, 1536)):
                        stt, Bst = STG.next()
                        S.dma("sp", stt[:, 0:wd], w_in[128 * k:128 * (k + 1), src0:src0 + wd], key=Bst, writes=[Bst], acc=False)
                        en = ("dve", "pool", "act")[ei % 3]
                        ei += 1
                        if en == "act":
                            S.op("act", lambda e: e.activation(out=WrB[:, k, dst0:dst0 + wd], in_=stt[:, 0:wd], func=AF.Copy),
                                 reads=[Bst], writes=[BwB], acc=True)
                        else:
                            S.op(en, lambda e: e.tensor_copy(out=WrB[:, k, dst0:dst0 + wd], in_=stt[:, 0:wd]), reads=[Bst], writes=[BwB], acc=True)
                S.barrier()
            HS = Rot("hs", [sb(st, f"hs{i}", [128, 8, 512], BF16) for i in range(2)])
            QS = Rot("qs", [sb(st, f"qs{i}", [128, 4, 4, 128], BF16) for i in range(2)])
            SG = Rot("sg", [sb(st, f"sg{i}", [128, 4, 8, 128], F32) for i in range(1)])
            KS = Rot("ks", [sb(st, f"ks{i}", [128, 4, 8, 128], BF16) for i in range(2)])
            GS = Rot("gs", [sb(st, f"gs{i}", [128, 4, 8, 128], F32) for i in range(1)])
            GA = Rot("ga", [sb(st, f"ga{i}", [128, 4, 16, 128], BF16) for i in range(2)])
            def hs_load(su):
                hs, Bhs = HS.next()
                S.dma("sp", hs[:], HTD[su], key=Bhs, reads=[BHTD], writes=[Bhs], acc=False)
                return hs, Bhs

            hs_next = hs_load(0)
            for su in range(NT // 4):
                hs, Bhs = hs_next
                if su + 1 < NT // 4:
                    hs_next = hs_load(su + 1)

                def fm(c0):
                    pt, Bp = PS.next()
                    for k in range(8):
                        S.op("pe", lambda e, k=k: e.matmul(pt[:, :], lhsT=WrB[:, k, c0:c0 + 128], rhs=hs[:, k, :], start=(k == 0), stop=(k == 7)),
                             reads=[BwB, Bhs], writes=[Bp], acc=(k > 0))
                    return pt[:, :].rearrange("p (j t) -> p j t", t=128), Bp

                qs, Bqs = QS.next()
                for h in range(4):
                    pv, Bp = fm(128 * h)
                    S.op("act", lambda e, h=h: e.activation(out=qs[:, :, h, :], in_=pv, func=AF.Silu), reads=[Bp], writes=[Bqs], acc=(h > 0))
                sg, Bsg = SG.next()
                for dh in range(8):
                    pv, Bp = fm(512 + 128 * dh)
                    S.op("act", lambda e, dh=dh: e.activation(out=sg[:, :, dh, :], in_=pv, func=AF.Sigmoid), reads=[Bp], writes=[Bsg], acc=(dh > 0))
                    S.op("dve", lambda e, dh=dh: e.tensor_scalar(out=sg[:, :, dh, :], in0=sg[:, :, dh, :], scalar1=omlv[:, dh:dh + 1],
                                                                 scalar2=lbv[:, dh:dh + 1], op0=ALU.mult, op1=ALU.add),
                         reads=[Bsg, Bconst], writes=[Bsg], acc=True)
                ks, Bks = KS.next()
                gs, Bgs = GS.next()
                S.op("dve", lambda e: e.tensor_scalar(out=ks[:], in0=sg[:], scalar1=-1.0, scalar2=1.0, op0=ALU.mult, op1=ALU.add),
                     reads=[Bsg], writes=[Bks])
                S.op("act", lambda e: e.activation(out=gs[:], in_=sg[:], func=AF.Ln), reads=[Bsg], writes=[Bgs])
                ga, Bga = GA.next()
                for gq in range(16):
                    pv, Bp = fm(1536 + 128 * gq)
                    S.op("act", lambda e, gq=gq: e.activation(out=ga[:, :, gq, :], in_=pv, func=AF.Sigmoid), reads=[Bp], writes=[Bga], acc=(gq > 0))
                for j in range(4):
                    n = 4 * su + j
                    S.dma("pool", QT[n], qs[:, j, :, :].rearrange("p a b -> p (a b)"), key=Bqs, reads=[Bqs])
                    S.dma("pool", KFT[n], ks[:, j, :, :].rearrange("p (d h) t -> p d (h t)", d=2), key=Bks, reads=[Bks])
                    S.dma("pool", GFT[n], gs[:, j, :, :].rearrange("p (d h) t -> p d (h t)", d=2), key=Bgs, reads=[Bgs])
                    S.dma("pool", GTT[n], ga[:, j, :, :].rearrange("p a b -> p (a b)"), key=Bga, reads=[Bga])
            S.barrier()

        G = dscr("G", [512, 16384], BF16)
        YAT = dscr("YAT", [NT, 128, 4, 128], BF16)
        BG = Buf("G")
        RNb = sb(es, "RNb", [128, 3, 512], F32)
        Brn = Buf("rn")
        with ExitStack() as st:
            w1t = sb(st, "w1t", [33, 64], F32)
            w2t = sb(st, "w2t", [64, 64], F32)
            w3t = sb(st, "w3t", [64, 64], F32)
            wot = sb(st, "wot", [64, 1024], F32)
            fbt = sb(st, "fbt", [64, 3], F32)
            frq = sb(st, "frq", [64, 3], F32)
            bfq = sb(st, "bfq", [64, 3], F32)
            fcs = sb(st, "fcs", [128, 8], F32)
            fbi = sb(st, "fbi", [128, 4], F32)
            acol = sb(st, "acol", [128, 1], F32)
            wsc = sb(st, "wsc", [128, 4], F32)
            nsum = sb(st, "nsum", [128, 2, 4, 16], F32)
            lag0 = sb(st, "lag0", [128, 4], F32)
            nrm = sb(st, "nrm", [128, 4], F32)
            rn = sb(st, "rn", [128, 4], F32)
            pch = sb(st, "pch", [128, 4], BF16)
            dg = sb(st, "dgf", [128, 128], F32)
            Bfp, Bns, Bdg = Buf("fp"), Buf("ns"), Buf("dgf")
            for (t_, d_) in ((w1t, fw1), (w2t, fw2), (w3t, fw3), (wot, fwo), (fbt, fbT), (frq, ffreqT), (fcs, fconst), (fbi, fbiasT)):
                S.dma("sp", t_[:], d_[:, :], key=Bfp, writes=[Bfp])
            S.op("dve", lambda e: e.tensor_tensor(out=bfq[:], in0=fbt[:], in1=frq[:], op=ALU.mult), reads=[Bfp], writes=[Bfp], acc=True)
            S.op("dve", lambda e: e.tensor_tensor(out=acol[:], in0=fcs[:, 0:1], in1=meta_t[:, 3:4], op=ALU.mult),
                 reads=[Bfp, Bconst], writes=[Bfp], acc=True)
            S.op("dve", lambda e: e.tensor_scalar(out=wsc[:], in0=fcs[:, 2:6], scalar1=meta_t[:, 2:3], scalar2=-1.0, op0=ALU.mult, op1=ALU.mult),
                 reads=[Bfp, Bconst], writes=[Bfp], acc=True)
            S.op("pool", lambda e: e.memset(nsum[:], 0.0), writes=[Bns])
            NI = Rot("ni", [sb(st, f"ni{i}", [128, 512], I32) for i in range(2)])
            NF = Rot("nf", [sb(st, f"nf{i}", [128, 512], F32) for i in range(4)])
            MK = Rot("mk", [sb(st, f"mk{i}", [128, 512], F32) for i in range(2)])
            AR = Rot("ar", [sb(st, f"ar{i}", [64, 512], F32) for i in range(4)])
            KI = Rot("ki", [sb(st, f"ki{i}", [64, 512], I32) for i in range(4)])
            KF = Rot("kf", [sb(st, f"kf{i}", [64, 512], F32) for i in range(4)])
            HH = Rot("hh", [sb(st, f"hh{i}", [64, 512], F32) for i in range(6)])
            WN = Rot("wn", [sb(st, f"wn{i}", [128, 512], F32) for i in range(4)])
            HW = Rot("hw", [sb(st, f"hw{i}", [128, 512], F32) for i in range(4)])
            AB = Rot("ab", [sb(st, f"ab{i}", [128, 512], F32) for i in range(4)])
            HB = Rot("hb", [sb(st, f"hb{i}", [128, 512], BF16) for i in range(4)])

            def sin_reduce(arg, Barg, rows, out, Bout):
                ki, Bki = KI.next()
                kf, Bkf = KF.next()
                S.op("dve", lambda e: e.tensor_scalar(out=ki[0:rows, :], in0=arg[0:rows, :], scalar1=1.0 / TWO_PI, scalar2=None, op0=ALU.mult),
                     reads=[Barg], writes=[Bki])
                S.op("dve", lambda e: e.tensor_copy(out=kf[0:rows, :], in_=ki[0:rows, :]), reads=[Bki], writes=[Bkf])
                S.op("dve", lambda e: e.scalar_tensor_tensor(out=arg[0:rows, :], in0=kf[0:rows, :], scalar=-TWO_PI, in1=arg[0:rows, :],
                                                             op0=ALU.mult, op1=ALU.add), reads=[Bkf, Barg], writes=[Barg])
                S.op("dve", lambda e: e.tensor_scalar(out=arg[0:rows, :], in0=arg[0:rows, :], scalar1=-3.141592, scalar2=3.141592,
                                                      op0=ALU.max, op1=ALU.min), reads=[Barg], writes=[Barg])
                S.op("act", lambda e: e.activation(out=out[0:rows, :], in_=arg[0:rows, :], func=AF.Sin), reads=[Barg], writes=[Bout])

            def f_gen(tiles):
                for ptile in tiles:
                    m0 = 512 * ptile
                    ni, Bni = NI.next()
                    nf, Bnf = NF.next()
                    S.op("pool", lambda e: e.iota(ni[:], pattern=[[1, 512]], base=m0 - 8192, channel_multiplier=0), writes=[Bni])
                    S.op("dve", lambda e: e.tensor_copy(out=nf[:], in_=ni[:]), reads=[Bni], writes=[Bnf])
                    S.op("dve", lambda e: e.scalar_tensor_tensor(out=nf[:], in0=nf[:], scalar=-1.0, in1=nf[:], op0=ALU.mult, op1=ALU.max),
                         reads=[Bnf], writes=[Bnf])
                    mk, Bmk = MK.next()
                    S.op("dve", lambda e: e.tensor_scalar(out=mk[:], in0=nf[:], scalar1=meta_t[:, 4:5], scalar2=None, op0=ALU.is_ge),
                         reads=[Bnf, Bconst], writes=[Bmk])
                    S.op("dve", lambda e: e.scalar_tensor_tensor(out=mk[:], in0=mk[:], scalar=1.0e6, in1=nf[:], op0=ALU.mult, op1=ALU.add),
                         reads=[Bmk, Bnf], writes=[Bmk])
                    ar, Bar = AR.next()
                    S.op("dve", lambda e: e.tensor_scalar(out=ar[0:33, :], in0=nf[0:33, :], scalar1=acol[0:33, 0:1], scalar2=fcs[0:33, 1:2],
                                                          op0=ALU.mult, op1=ALU.add), reads=[Bnf, Bfp], writes=[Bar])
                    hcur, Bhc = HH.next()
                    sin_reduce(ar, Bar, 33, hcur, Bhc)
                    S.op("dve", lambda e: e.tensor_scalar(out=hcur[0:1, :], in0=nf[0:1, :], scalar1=meta_t[0:1, 2:3], scalar2=None, op0=ALU.mult),
                         reads=[Bnf, Bconst], writes=[Bhc], acc=True)
                    yield
                    K_in = 33
                    for li, wt in enumerate((w1t, w2t, w3t)):
                        pt, Bp = PS.next()
                        S.op("pe", lambda e: e.matmul(pt[0:64, :], lhsT=wt[0:K_in, :], rhs=hcur[0:K_in, :], start=True, stop=True),
                             reads=[Bfp, Bhc], writes=[Bp])
                        ar, Bar = AR.next()
                        S.op("dve", lambda e: e.tensor_scalar(out=ar[:, :], in0=pt[0:64, :], scalar1=frq[:, li:li + 1], scalar2=bfq[:, li:li + 1],
                                                              op0=ALU.mult, op1=ALU.add), reads=[Bp, Bfp], writes=[Bar])
                        hcur, Bhc = HH.next()
                        sin_reduce(ar, Bar, 64, hcur, Bhc)
                        K_in = 64
                        yield
                    bwd = ptile < 16
                    for j in range(4):
                        c0 = (512 if bwd else 0) + 128 * j
                        pt, Bp = PS.next()
                        S.op("pe", lambda e: e.matmul(pt[:, :], lhsT=wot[:, c0:c0 + 128], rhs=hcur[0:64, :], start=True, stop=True),
                             reads=[Bfp, Bhc], writes=[Bp])
                        wn, Bwn = WN.next()
                        S.op("act", lambda e: e.activation(out=wn[:], in_=mk[:], func=AF.Exp, scale=wsc[:, j:j + 1]), reads=[Bmk, Bfp], writes=[Bwn])
                        hw, Bhw = HW.next()
                        S.op("dve", lambda e: e.tensor_tensor(out=hw[:], in0=pt[:, :], in1=wn[:], op=ALU.mult), reads=[Bp, Bwn], writes=[Bhw])
                        ab, Bab = AB.next()
                        S.op("act", lambda e: e.activation(out=ab[:], in_=hw[:], func=AF.Abs), reads=[Bhw], writes=[Bab])
                        S.op("dve", lambda e: e.tensor_reduce(out=nsum[:, (0 if bwd else 1), j, (ptile % 16):(ptile % 16) + 1], in_=ab[:],
                                                              axis=AX.X, op=ALU.add), reads=[Bab], writes=[Bns], acc=True)
                        if ptile == 16:
                            S.op("dve", lambda e: e.tensor_copy(out=lag0[:, j:j + 1], in_=hw[:, 0:1]), reads=[Bhw], writes=[Bns], acc=True)
                        hb, Bhb = HB.next()
                        S.op("act", lambda e: e.activation(out=hb[:], in_=hw[:], func=AF.Copy), reads=[Bhw], writes=[Bhb])
                        S.dma("sp", G[128 * j:128 * (j + 1), m0:m0 + 512], hb[:], key=Bhb, reads=[Bhb], writes=[BG])
                        yield

            gens = [f_gen(list(range(0, 16))), f_gen(list(range(16, 32)))]
            alive = [True, True]
            while any(alive):
                for gi in range(2):
                    if alive[gi]:
                        try:
                            next(gens[gi])
                        except StopIteration:
                            alive[gi] = False
            S.op("dve", lambda e: e.tensor_reduce(out=nrm[:], in_=nsum[:].rearrange("p d j t -> p j d t"), axis=AX.XY, op=ALU.add),
                 reads=[Bns], writes=[Bns], acc=True)
            S.op("dve", lambda e: e.tensor_scalar(out=nrm[:], in0=nrm[:], scalar1=EPS, scalar2=None, op0=ALU.add), reads=[Bns], writes=[Bns], acc=True)
            S.op("dve", lambda e: e.reciprocal(out=rn[:], in_=nrm[:]), reads=[Bns], writes=[Bns], acc=True)
            S.op("dve", lambda e: e.tensor_tensor(out=nrm[:], in0=nrm[:], in1=fbi[:], op=ALU.mult), reads=[Bns, Bfp], writes=[Bns], acc=True)
            S.op("dve", lambda e: e.tensor_tensor(out=pch[:], in0=nrm[:], in1=lag0[:], op=ALU.add), reads=[Bns], writes=[Bns], acc=True)
            S.barrier()
            for j in range(4):
                S.dma("sp", G[128 * j:128 * (j + 1), 8192:8193], pch[:, j:j + 1], key=Bns, reads=[Bns], writes=[BG], allow_slow_non_contiguous=True)
                S.op("dve", lambda e: e.tensor_scalar(out=dg[:], in0=ident_f, scalar1=rn[:, j:j + 1], scalar2=None, op0=ALU.mult),
                     reads=[Bns, Bconst], writes=[Bdg])
                pt, Bp = PS.next()
                S.op("pe", lambda e: e.matmul(pt[:, 0:128], lhsT=ones_f[:], rhs=dg[:], start=True, stop=True), reads=[Bdg, Bconst], writes=[Bp])
                S.op("act", lambda e: e.activation(out=RNb[:, 0, 128 * j:128 * (j + 1)], in_=pt[:, 0:128], func=AF.Copy),
                     reads=[Bp], writes=[Brn], acc=True)
            S.op("dve", lambda e: e.tensor_scalar(out=RNb[:, 1, :], in0=RNb[:, 0, :], scalar1=FLAG, scalar2=None, op0=ALU.mult),
                 reads=[Brn, Bconst], writes=[Brn], acc=True)
            S.op("dve", lambda e: e.tensor_scalar(out=RNb[:, 2, :], in0=RNb[:, 0, :], scalar1=NFLAG, scalar2=None, op0=ALU.mult),
                 reads=[Brn, Bconst], writes=[Brn], acc=True)
            S.barrier()

        OF = dscr("OF", [NT, 128, 512], F32)
        YBT = dscr("YBT", [NT, 128, 4, 128], BF16)
        BOF = Buf("OF")
        with ExitStack() as st:
            V2 = sb(st, "V2", [128, 128, 96], BF16)
            X0g = sb(st, "X0g", [128, 64, 128], BF16)
            Yb = sb(st, "Yb", [128, 64, 128], BF16)
            HK = Rot("hk", [sb(st, f"hk{i}", [128, 16264], BF16) for i in range(2)])
            YS = Rot("ys", [sb(st, f"ys{i}", [128, 8, 128], BF16) for i in range(2)])
            TH = Rot("th", [sb(st, f"th{i}", [128, 2, 32], F32) for i in range(2)])
            TH0 = Rot("th0", [sb(st, f"th0{i}", [128, 32], F32) for i in range(2)])
            Bve, Bxg, Byb = Buf("vext"), Buf("x0g"), Buf("yb")
            dlist = [0] + [dd for a in range(1, 64) for dd in (a, -a)]
            HBK = Rot("hbk", [psb[6], psb[7]])
            HBK.bufs = [PS.bufs[6], PS.bufs[7]]

            def h_gen():
                for cg in range(4):
                    S.dma("sp", Yb[:], VT[cg], key=Byb, writes=[Byb], acc=False)
                    S.dma("sp", X0g[:], X0T[cg], key=Bxg, writes=[Bxg], acc=False)
                    S.op("dve", lambda e: e.tensor_copy(out=V2[:, :, 0:32].rearrange("p c s -> p s c"), in_=Yb[:, 0:32, :]),
                         reads=[Byb], writes=[Bve])
                    S.op("dve", lambda e: e.tensor_scalar(out=V2[:, :, 32:64].rearrange("p c s -> p s c"), in0=Yb[:, 32:64, :], scalar1=FLAG,
                                                          scalar2=None, op0=ALU.mult), reads=[Byb, Bconst], writes=[Bve], acc=True)
                    S.op("act", lambda e: e.activation(out=V2[:, :, 64:96].rearrange("p c s -> p s c"), in_=Yb[:, 32:64, :], func=AF.Copy,
                                                       scale=NFLAG), reads=[Byb, Bconst], writes=[Bve], acc=True)
                    for c in range(128):
                        cc = 128 * cg + c
                        hk, Bhk = HK.next()
                        src = bass.AP(G.tensor, G.offset + cc * 16384, [[1, 128], [1, 16257]])
                        S.dma("sp", hk[:, 0:16257], src, key=Bhk, reads=[BG], writes=[Bhk], acc=False)
                        pt, Bp = HBK.next()
                        for di, d in enumerate(dlist):
                            yd = 8065 + 128 * d
                            if d >= 0:
                                o0, o1, i0 = d, 96, 0
                            else:
                                o0, o1, i0 = 0, 96 + d, -d
                            nn = o1 - o0
                            S.op("pe", lambda e, yd=yd, o0=o0, o1=o1, i0=i0, nn=nn, di=di: e.matmul(
                                pt[:, o0:o1], lhsT=hk[:, yd:yd + 128], rhs=V2[:, c, i0:i0 + nn],
                                start=(di == 0), stop=(di == len(dlist) - 1), skip_group_check=True),
                                reads=[Bhk, Bve], writes=[Bp], acc=(di > 0))
                        th, Bth = TH.next()
                        th0, Bth0 = TH0.next()
                        S.op("act", lambda e: e.activation(out=th0[:], in_=pt[:, 0:32], func=AF.Copy, scale=RNb[:, 0, cc:cc + 1]),
                             reads=[Bp, Brn], writes=[Bth0])
                        S.op("pool", lambda e: e.tensor_tensor(out=Yb[:, 0:32, c], in0=th0[:], in1=X0g[:, 0:32, c], op=ALU.mult),
                             reads=[Bth0, Bxg], writes=[Byb], acc=True)
                        S.op("act", lambda e: e.activation(out=th[:, 0, :], in_=pt[:, 32:64], func=AF.Copy, scale=RNb[:, 1, cc:cc + 1]),
                             reads=[Bp, Brn], writes=[Bth])
                        S.op("dve", lambda e: e.scalar_tensor_tensor(out=th[:, 1, :], in0=pt[:, 64:96], scalar=RNb[:, 2, cc:cc + 1],
                                                                     in1=th[:, 0, :], op0=ALU.mult, op1=ALU.add),
                             reads=[Bp, Brn, Bth], writes=[Bth], acc=True)
                        S.op("dve", lambda e: e.tensor_tensor(out=Yb[:, 32:64, c], in0=th[:, 1, :], in1=X0g[:, 32:64, c], op=ALU.mult),
                             reads=[Bth, Bxg], writes=[Byb], acc=True)
                        yield
                    for n0 in range(0, NT, 8):
                        pt, Bp = HBK.next()
                        ptb = pt[:].bitcast(BF16)
                        for q in range(8):
                            S.op("pe", lambda e, q=q: e.transpose(out=ptb[:, 128 * q:128 * (q + 1)], in_=Yb[:, n0 + q, :], identity=ident_b),
                                 reads=[Byb, Bconst], writes=[Bp], acc=(q > 0))
                        ys, Bys = YS.next()
                        S.op("act", lambda e: e.activation(out=ys[:].rearrange("p a b -> p (a b)"), in_=ptb[:, :], func=AF.Copy), reads=[Bp], writes=[Bys])
                        S.dma("pool", YAT[n0:n0 + 8, :, cg, :].rearrange("n p t -> p n t"), ys[:], key=Bys, reads=[Bys])
                    yield

            ones512 = sb(st, "ones512", [128, 512], F32)
            gnb = sb(st, "gnb", [64, 512], F32)
            Sf = sb(st, "Sf", [128, 4, 128], F32)
            Sbf = sb(st, "Sbf", [128, 4, 128], BF16)
            BSf = [Buf(f"Sf{h}") for h in range(4)]
            BSb = [Buf(f"Sb{h}") for h in range(4)]
            Bg1 = Buf("g1c")
            S.op("pool", lambda e: e.memset(ones512[:], 1.0), writes=[Bg1])
            S.dma("sp", gnb[:], gnorm_b[0:64, :], key=Bg1, writes=[Bg1])
            QL = Rot("ql", [sb(st, f"ql{i}", [128, 512], BF16) for i in range(2)])
            KL = Rot("kl", [sb(st, f"kl{i}", [128, 512], BF16) for i in range(2)])
            GL = Rot("gl", [sb(st, f"gl{i}", [128, 512], F32) for i in range(2)])
            VL = Rot("vl", [sb(st, f"vl{i}", [64, 2, 512], BF16) for i in range(2)])
            PG = Rot("pg", [sb(st, f"pg{i}", [128, 520], F32) for i in range(2)])
            AA = Rot("aa", [sb(st, f"aa{i}", [128, 512], F32) for i in range(2)])
            X1 = Rot("x1", [sb(st, f"x1{i}", [128, 512], F32) for i in range(2)])
            X4 = Rot("x4", [sb(st, f"x4{i}", [128, 512], F32) for i in range(1)])
            EE = Rot("ee", [sb(st, f"ee{i}", [128, 4, 512], F32) for i in range(1)])
            DC = Rot("dc", [sb(st, f"dc{i}", [128, 8], F32) for i in range(2)])
            QK = Rot("qk", [sb(st, f"qk{i}", [128, 4, 512], BF16) for i in range(1)])
            AM = Rot("am", [sb(st, f"am{i}", [64, 64], BF16) for i in range(8)])
            KT = Rot("kt", [sb(st, f"kt{i}", [64, 128], BF16) for i in range(8)])
            OO = Rot("oo", [sb(st, f"oo{i}", [64, 2, 512], F32) for i in range(1)])
            OFL = Rot("ofl", [sb(st, f"ofl{i}", [64, 2, 512], F32) for i in range(1)])
            OGL = Rot("ogl", [sb(st, f"ogl{i}", [64, 2, 512], BF16) for i in range(1)])
            SQ = Rot("sq", [sb(st, f"sq{i}", [64, 2, 512], F32) for i in range(1)])
            MS = Rot("ms", [sb(st, f"ms{i}", [64, 8], F32) for i in range(2)])
            YB = Rot("yb", [sb(st, f"yb{i}", [64, 2, 512], BF16) for i in range(1)])
            YT = Rot("yt", [sb(st, f"yt{i}", [128, 4, 128], BF16) for i in range(2)])

            def g3(ap):
                return ap.rearrange("p (g t) -> p g t", t=64)

            def g_loads(d, n):
                ql, Bql = QL.next()
                kl, Bkl = KL.next()
                gl, Bgl = GL.next()
                vl, Bvl = VL.next()
                S.dma("sp", ql[:], QT[n], key=Bql, writes=[Bql], acc=False)
                S.dma("sp", kl[:], KFT[n, :, d, :], key=Bkl, writes=[Bkl], acc=False)
                S.dma("sp", gl[:], GFT[n, :, d, :], key=Bgl, writes=[Bgl], acc=False)
                S.dma("sp", vl[:], IT[n].rearrange("(c t) f -> t c f", c=2), key=Bvl, writes=[Bvl], acc=False)
                return (ql, Bql, kl, Bkl, gl, Bgl, vl, Bvl)

            def g_gen():
                steps = [(0, n) for n in range(NT)] + [(1, n) for n in range(NT - 1, -1, -1)]
                pend = g_loads(*steps[0]) if steps else None
                for si, (d, n) in enumerate(steps):
                    cur = pend
                    pend = g_loads(*steps[si + 1]) if si + 1 < len(steps) else None
                    if n == (0 if d == 0 else NT - 1):
                        for h in range(4):
                            S.op("pool", lambda e, h=h: e.memset(Sf[:, h, :], 0.0), writes=[BSf[h]])
                            S.op("pool", lambda e, h=h: e.memset(Sbf[:, h, :], 0.0), writes=[BSb[h]])
                    mask = cm_b[0:64, 256:320] if d == 0 else cm_b[0:64, 320:384]
                    if (d == 0 and n == SEG) or (d == 1 and n == SEG - 1):
                        for h in range(4):
                            S.op("dve", lambda e, h=h: e.tensor_scalar(out=Sf[:, h, :], in0=Sf[:, h, :], scalar1=FLAG, scalar2=None, op0=ALU.mult),
                                 reads=[Bconst], writes=[BSf[h]])
                            S.op("act", lambda e, h=h: e.activation(out=Sbf[:, h, :], in_=Sf[:, h, :], func=AF.Copy), reads=[BSf[h]], writes=[BSb[h]])
                    (ql, Bql, kl, Bkl, gl, Bgl, vl, Bvl) = cur
                    pg, Bpg = PG.next()
                    S.op("pool", lambda e: e.memset(pg[:, 0:1], 0.0), writes=[Bpg])
                    S.op("dve", lambda e: e.tensor_tensor_scan(out=pg[:, 1:513], data0=ones512[:], data1=gl[:], initial=0.0,
                                                               op0=ALU.mult, op1=ALU.add), reads=[Bgl, Bg1], writes=[Bpg], acc=True)
                    aa, Baa = AA.next()
                    S.op("dve", lambda e: e.tensor_tensor(out=g3(aa[:]), in0=g3(pg[:, 1:513]),
                                                          in1=g3(pg[:, 0:512])[:, :, 0:1].to_broadcast([128, 8, 64]), op=ALU.subtract),
                         reads=[Bpg], writes=[Baa])
                    if d == 1:
                        x1, Bx1 = X1.next()
                        S.op("dve", lambda e: e.tensor_tensor(out=x1[:], in0=gl[:], in1=aa[:], op=ALU.subtract), reads=[Bgl, Baa], writes=[Bx1])
                        aa2, Baa2 = AA.next()
                        S.op("dve", lambda e: e.tensor_tensor(out=g3(aa2[:]), in0=g3(x1[:]),
                                                              in1=g3(aa[:])[:, :, 63:64].to_broadcast([128, 8, 64]), op=ALU.add),
                             reads=[Bx1, Baa], writes=[Baa2])
                        aa, Baa = aa2, Baa2
                        lastc = 0
                    else:
                        lastc = 63
                    x1, Bx1 = X1.next()
                    x4, Bx4 = X4.next()
                    S.op("dve", lambda e: e.tensor_tensor(out=g3(x1[:]), in0=g3(aa[:]), in1=g3(aa[:])[:, :, 32:33].to_broadcast([128, 8, 64]),
                                                          op=ALU.subtract), reads=[Baa], writes=[Bx1])
                    S.op("pool", lambda e: e.tensor_tensor(out=g3(x4[:]), in0=g3(aa[:]),
                                                           in1=g3(aa[:])[:, :, lastc:lastc + 1].to_broadcast([128, 8, 64]), op=ALU.subtract),
                         reads=[Baa], writes=[Bx4])
                    ee, Bee = EE.next()
                    dc, Bdc = DC.next()
                    S.op("act", lambda e: e.activation(out=ee[:, 0, :], in_=x1[:], func=AF.Exp), reads=[Bx1], writes=[Bee])
                    S.op("act", lambda e: e.activation(out=ee[:, 1, :], in_=x1[:], func=AF.Exp, scale=-1.0), reads=[Bx1], writes=[Bee], acc=True)
                    S.op("act", lambda e: e.activation(out=ee[:, 2, :], in_=aa[:], func=AF.Exp), reads=[Baa], writes=[Bee], acc=True)
                    S.op("act", lambda e: e.activation(out=ee[:, 3, :], in_=x4[:], func=AF.Exp, scale=-1.0), reads=[Bx4], writes=[Bee], acc=True)
                    S.op("act", lambda e: e.activation(out=dc[:].unsqueeze(2), in_=g3(aa[:])[:, :, lastc:lastc + 1], func=AF.Exp), reads=[Baa], writes=[Bdc])
                    qk, Bqk = QK.next()
                    S.op("dve", lambda e: e.tensor_tensor(out=qk[:, 0, :], in0=ql[:], in1=ee[:, 0, :], op=ALU.mult), reads=[Bql, Bee], writes=[Bqk])
                    S.op("pool", lambda e: e.tensor_tensor(out=qk[:, 1, :], in0=kl[:], in1=ee[:, 1, :], op=ALU.mult), reads=[Bkl, Bee], writes=[Bqk], acc=True)
                    S.op("dve", lambda e: e.tensor_tensor(out=qk[:, 2, :], in0=ql[:], in1=ee[:, 2, :], op=ALU.mult), reads=[Bql, Bee], writes=[Bqk], acc=True)
                    S.op("pool", lambda e: e.tensor_tensor(out=qk[:, 3, :], in0=kl[:], in1=ee[:, 3, :], op=ALU.mult), reads=[Bkl, Bee], writes=[Bqk], acc=True)
                    yield
                    pos = (psb[0], psb[1])
                    Bpos = (PS.bufs[0], PS.bufs[1])
                    chunks = (0, 1) if d == 0 else (1, 0)
                    for c in chunks:
                        pA, BpA = psb[3][:, 256 * c:256 * (c + 1)], PS.bufs[3]
                        BpK = PS.bufs[2]
                        pKb = psb[2][:].bitcast(BF16)[:, 512 * c:512 * (c + 1)]
                        for h in range(4):
                            cs = slice(128 * h + 64 * c, 128 * h + 64 * c + 64)
                            S.op("pe", lambda e, h=h, cs=cs: e.matmul(pA[0:64, 64 * h:64 * (h + 1)], lhsT=qk[:, 1, cs], rhs=qk[:, 0, cs],
                                                                      start=True, stop=True), reads=[Bqk], writes=[BpA], acc=True)
                        for h in range(4):
                            cs = slice(128 * h + 64 * c, 128 * h + 64 * c + 64)
                            S.op("pe", lambda e, h=h, cs=cs: e.transpose(out=pKb[0:64, 128 * h:128 * (h + 1)], in_=qk[:, 3, cs], identity=ident_b),
                                 reads=[Bqk, Bconst], writes=[BpK], acc=True)
                        ams, kts = [], []
                        for h in range(4):
                            am, Bam = AM.next()
                            kt, Bkt = KT.next()
                            S.op("dve", lambda e, h=h: e.tensor_tensor(out=am[:], in0=pA[0:64, 64 * h:64 * (h + 1)], in1=mask, op=ALU.mult),
                                 reads=[BpA, Bconst], writes=[Bam])
                            S.op("act", lambda e, h=h: e.activation(out=kt[:], in_=pKb[0:64, 128 * h:128 * (h + 1)], func=AF.Copy), reads=[BpK], writes=[Bkt])
                            ams.append((am, Bam))
                            kts.append((kt, Bkt))
                        yield
                        for h in range(4):
                            cs = slice(128 * h + 64 * c, 128 * h + 64 * c + 64)
                            am, Bam = ams[h]
                            kt, Bkt = kts[h]
                            po = pos[c][0:64, 128 * h:128 * (h + 1)]
                            S.op("pe", lambda e, h=h: e.matmul(po, lhsT=am[:], rhs=vl[:, c, 128 * h:128 * (h + 1)], start=True, stop=False),
                                 reads=[Bam, Bvl], writes=[Bpos[c]], acc=(h > 0))
                            S.op("pe", lambda e, h=h, cs=cs: e.matmul(po, lhsT=qk[:, 2, cs], rhs=Sbf[:, h, :], start=False, stop=True),
                                 reads=[Bqk, BSb[h]], writes=[Bpos[c]], acc=True)
                            pS, BpS = psb[4 + c], PS.bufs[4 + c]
                            S.op("pe", lambda e, h=h: e.matmul(pS[:, 128 * h:128 * (h + 1)], lhsT=kt[:], rhs=vl[:, c, 128 * h:128 * (h + 1)], start=True, stop=True),
                                 reads=[Bkt, Bvl], writes=[BpS], acc=(h > 0))
                            S.op("dve", lambda e, h=h: e.scalar_tensor_tensor(out=Sf[:, h, :], in0=Sf[:, h, :], scalar=dc[:, 2 * h + c:2 * h + c + 1],
                                                                              in1=pS[:, 128 * h:128 * (h + 1)], op0=ALU.mult, op1=ALU.add),
                                 reads=[BpS, Bdc], writes=[BSf[h]])
                            S.op("act", lambda e, h=h: e.activation(out=Sbf[:, h, :], in_=Sf[:, h, :], func=AF.Copy), reads=[BSf[h]], writes=[BSb[h]])
                        yield
                    oo, Boo = OO.next()
                    OFv = OF[n].rearrange("(c t) f -> t c f", c=2)
                    if d == 0:
                        for c in range(2):
                            S.op("act", lambda e, c=c: e.activation(out=oo[:, c, :], in_=pos[c][0:64, :], func=AF.Copy), reads=[Bpos[c]], writes=[Boo], acc=(c > 0))
                        S.dma("pool", OFv, oo[:], key=Boo, reads=[Boo], writes=[BOF])
                    else:
                        ofl, Bofl = OFL.next()
                        ogl, Bogl = OGL.next()
                        S.dma("sp", ofl[:], OFv, key=Bofl, reads=[BOF], writes=[Bofl], acc=False)
                        S.dma("sp", ogl[:], OGT[n].rearrange("(c t) f -> t c f", c=2), key=Bogl, writes=[Bogl], acc=False)
                        for c in range(2):
                            S.op("dve", lambda e, c=c: e.tensor_tensor(out=oo[:, c, :], in0=pos[c][0:64, :], in1=ofl[:, c, :], op=ALU.add),
                                 reads=[Bpos[c], Bofl], writes=[Boo], acc=(c > 0))
                        sq_, Bsq_ = SQ.next()
                        ms, Bms = MS.next()
                        S.op("pool", lambda e: e.tensor_tensor(out=sq_[:], in0=oo[:], in1=oo[:], op=ALU.mult), reads=[Boo], writes=[Bsq_])
                        S.op("dve", lambda e: e.tensor_reduce(out=ms[:], in_=sq_[:].rearrange("p c (h v) -> p (c h) v", v=128), axis=AX.X, op=ALU.add),
                             reads=[Bsq_], writes=[Bms])
                        S.op("dve", lambda e: e.tensor_scalar(out=ms[:], in0=ms[:], scalar1=1.0 / 128, scalar2=EPS, op0=ALU.mult, op1=ALU.add),
                             reads=[Bms], writes=[Bms])
                        S.op("act", lambda e: e.activation(out=ms[:], in_=ms[:], func=AF.Ln), reads=[Bms], writes=[Bms])
                        S.op("act", lambda e: e.activation(out=ms[:], in_=ms[:], func=AF.Exp, scale=-0.5), reads=[Bms], writes=[Bms])
                        S.op("dve", lambda e: e.tensor_tensor(out=oo[:].rearrange("p c (h v) -> p (c h) v", v=128),
                                                              in0=oo[:].rearrange("p c (h v) -> p (c h) v", v=128),
                                                              in1=ms[:].unsqueeze(2).to_broadcast([64, 8, 128]), op=ALU.mult),
                             reads=[Boo, Bms], writes=[Boo])
                        S.op("pool", lambda e: e.tensor_tensor(out=oo[:], in0=oo[:], in1=gnb[:].unsqueeze(1).to_broadcast([64, 2, 512]), op=ALU.mult),
                             reads=[Boo, Bg1], writes=[Boo])
                        yb, Byb2 = YB.next()
                        S.op("dve", lambda e: e.tensor_tensor(out=yb[:], in0=oo[:], in1=ogl[:], op=ALU.mult), reads=[Boo, Bogl], writes=[Byb2])
                        pT, BpT = psb[2], PS.bufs[2]
                        pTb = pT[:].bitcast(BF16)
                        for h in range(4):
                            for c in range(2):
                                S.op("pe", lambda e, h=h, c=c: e.transpose(out=pTb[:, 128 * h + 64 * c:128 * h + 64 * c + 64],
                                                                           in_=yb[:, c, 128 * h:128 * (h + 1)], identity=ident_b[0:64, 0:64]),
                                     reads=[Byb2, Bconst], writes=[BpT], acc=(h + c > 0))
                        yt, Byt = YT.next()
                        S.op("act", lambda e: e.activation(out=yt[:].rearrange("p a b -> p (a b)"), in_=pTb[:, 0:512], func=AF.Copy), reads=[BpT], writes=[Byt])
                        S.dma("pool", YBT[n], yt[:], key=Byt, reads=[Byt])
                    yield

            gH, gG = h_gen(), g_gen()
            aliveH, aliveG = True, True
            it, ng = 0, 0
            while aliveH or aliveG:
                if aliveH:
                    try:
                        next(gH)
                    except StopIteration:
                        aliveH = False
                k = 1 + (it % 2)
                for _ in range(k):
                    if aliveG:
                        try:
                            next(gG)
                            ng += 1
                        except StopIteration:
                            aliveG = False
                it += 1
            S.barrier()

        X1 = dscr("X1", [NT, 128, 1024], F32)
        BX1 = Buf("X1")
        By = Buf("y")
        with ExitStack() as stc:
            GTb = sb(stc, "GTb", [128, 2, 2048], F32)
            Bgt = Buf("gtb")
            with ExitStack() as st:
                dg = sb(st, "dgc", [128, 128], F32)
                Bdg = Buf("dgc")
                for s_ in range(2):
                    for gi, j0 in enumerate((16, 40)):
                        for jj in range(8):
                            j = j0 + jj
                            S.op("dve", lambda e, j=j, s_=s_: e.tensor_scalar(out=dg[:], in0=ident_f, scalar1=modT[:, j, s_:s_ + 1], scalar2=None,
                                                                               op0=ALU.mult), reads=[Bmod, Bconst], writes=[Bdg])
                            pt, Bp = PS.next()
                            S.op("pe", lambda e: e.matmul(pt[:, 0:128], lhsT=ones_f[:], rhs=dg[:], start=True, stop=True),
                                 reads=[Bdg, Bconst], writes=[Bp])
                            c0 = gi * 1024 + jj * 128
                            S.op("act", lambda e, c0=c0, s_=s_: e.activation(out=GTb[:, s_, c0:c0 + 128], in_=pt[:, 0:128], func=AF.Copy),
                                 reads=[Bp], writes=[Bgt], acc=True)
                S.barrier()

            def load_w(st2, dst, src, nk, width, Bdst, tag):
                stg = [sb(st2, f"cstg{tag}{i}", [128, 2048], F32) for i in range(2)]
                STG = Rot("cstg" + tag, stg)
                ei = 0
                for k in range(nk):
                    for c0 in range(0, width, 2048):
                        wd = min(2048, width - c0)
                        stt, Bst = STG.next()
                        S.dma("sp", stt[:, 0:wd], src[128 * k:128 * (k + 1), c0:c0 + wd], key=Bst, writes=[Bst], acc=False)
                        en = ("dve", "pool", "act")[ei % 3]
                        ei += 1
                        if en == "act":
                            S.op("act", lambda e: e.activation(out=dst[:, k, c0:c0 + wd], in_=stt[:, 0:wd], func=AF.Copy),
                                 reads=[Bst], writes=[Bdst], acc=True)
                        else:
                            S.op(en, lambda e: e.tensor_copy(out=dst[:, k, c0:c0 + wd], in_=stt[:, 0:wd]), reads=[Bst], writes=[Bdst], acc=True)

            with ExitStack() as st:
                Wb = sb(st, "Wb", [128, 8, 1024], BF16)
                Wo = sb(st, "Wo", [128, 8, 1024], BF16)
                Bwb = Buf("wb")
                with ExitStack() as st2:
                    load_w(st2, Wb, w_branch, 8, 1024, Bwb, 'a')
                    load_w(st2, Wo, w_out, 8, 1024, Bwb, 'b')
                    S.barrier()
                XL = Rot("xl", [sb(st, f"xl{i}", [128, 1024], F32) for i in range(2)])
                YA = Rot("ya", [sb(st, f"ya{i}", [128, 4, 128], BF16) for i in range(2)])
                YBL = Rot("ybl", [sb(st, f"ybl{i}", [128, 4, 128], BF16) for i in range(2)])
                GL2 = Rot("gl2", [sb(st, f"gl2{i}", [128, 2048], BF16) for i in range(2)])
                T1 = Rot("t1", [sb(st, f"t1{i}", [128, 512], F32) for i in range(2)])
                T2 = Rot("t2", [sb(st, f"t2{i}", [128, 512], F32) for i in range(2)])
                MT = Rot("mt", [sb(st, f"mt{i}", [128, 8, 128], BF16) for i in range(2)])
                TX = Rot("tx", [sb(st, f"tx{i}", [128, 1024], F32) for i in range(2)])
                XO = Rot("xo", [sb(st, f"xo{i}", [128, 1024], F32) for i in range(2)])
                def ca_load(n):
                    xl, Bxl = XL.next()
                    ya, Bya = YA.next()
                    ybl, Bybl = YBL.next()
                    gl2, Bgl2 = GL2.next()
                    S.dma("sp", xl[:], x[128 * n:128 * (n + 1), :], key=Bxl, writes=[Bxl], acc=False)
                    S.dma("sp", ya[:], YAT[n], key=Bya, writes=[Bya], acc=False)
                    S.dma("sp", ybl[:], YBT[n], key=Bybl, writes=[Bybl], acc=False)
                    S.dma("sp", gl2[:], GTT[n], key=Bgl2, writes=[Bgl2], acc=False)
                    return (xl, Bxl, ya, Bya, ybl, Bybl, gl2, Bgl2)

                ca_next = ca_load(0)
                for n in range(NT):
                    seg = n // SEG
                    (xl, Bxl, ya, Bya, ybl, Bybl, gl2, Bgl2) = ca_next
                    if n + 1 < NT:
                        ca_next = ca_load(n + 1)
                    mt, Bmt = MT.next()
                    for q in range(2):
                        pa, Bpa = PS.next()
                        pb, Bpb = PS.next()
                        for r4 in range(4):
                            nch = 4 * q + r4
                            for cc in range(4):
                                S.op("pe", lambda e, r4=r4, nch=nch, cc=cc: e.matmul(pa[:, 128 * r4:128 * (r4 + 1)],
                                                                                     lhsT=Wb[:, cc, 128 * nch:128 * (nch + 1)], rhs=ya[:, cc, :],
                                                                                     start=(cc == 0), stop=(cc == 3)),
                                     reads=[Bwb, Bya], writes=[Bpa], acc=(r4 + cc > 0))
                        for r4 in range(4):
                            nch = 4 * q + r4
                            for cc in range(4):
                                S.op("pe", lambda e, r4=r4, nch=nch, cc=cc: e.matmul(pb[:, 128 * r4:128 * (r4 + 1)],
                                                                                     lhsT=Wb[:, 4 + cc, 128 * nch:128 * (nch + 1)], rhs=ybl[:, cc, :],
                                                                                     start=(cc == 0), stop=(cc == 3)),
                                     reads=[Bwb, Bybl], writes=[Bpb], acc=(r4 + cc > 0))
                        t1, Bt1 = T1.next()
                        t2, Bt2 = T2.next()
                        S.op("dve", lambda e: e.tensor_tensor(out=t1[:], in0=pa[:, :], in1=gl2[:, 512 * q:512 * (q + 1)], op=ALU.mult),
                             reads=[Bpa, Bgl2], writes=[Bt1])
                        S.op("dve", lambda e: e.tensor_tensor(out=t2[:], in0=pb[:, :], in1=gl2[:, 1024 + 512 * q:1024 + 512 * (q + 1)], op=ALU.mult),
                             reads=[Bpb, Bgl2], writes=[Bt2])
                        S.op("dve", lambda e: e.tensor_tensor(out=mt[:, 4 * q:4 * (q + 1), :].rearrange("p a b -> p (a b)"), in0=t1[:], in1=t2[:], op=ALU.add),
                             reads=[Bt1, Bt2], writes=[Bmt], acc=(q > 0))
                    tx, Btx = TX.next()
                    xo, Bxo = XO.next()
                    for dh in range(2):
                        po, Bpo = PS.next()
                        for nch in range(8):
                            S.op("pe", lambda e, nch=nch: e.matmul(po[:, :], lhsT=mt[:, nch, :], rhs=Wo[:, nch, 512 * dh:512 * (dh + 1)],
                                                                   start=(nch == 0), stop=(nch == 7)), reads=[Bmt, Bwb], writes=[Bpo], acc=(nch > 0))
                        S.op("dve", lambda e: e.tensor_tensor(out=tx[:, 512 * dh:512 * (dh + 1)], in0=po[:, :], in1=GTb[:, seg, 512 * dh:512 * (dh + 1)],
                                                              op=ALU.mult), reads=[Bpo, Bgt], writes=[Btx], acc=(dh > 0))
                    S.op("dve", lambda e: e.tensor_tensor(out=xo[:], in0=tx[:], in1=xl[:], op=ALU.add), reads=[Btx, Bxl], writes=[Bxo])
                    S.dma("sp", X1[n], xo[:], key=Bxo, reads=[Bxo], writes=[BX1])
                S.barrier()

            NSUP = NT // 4
            ATD = dscr("ATD", [NSUP, 128, 32 * 512], BF16)
            BATD = Buf("ATD")
            _csk = False

            def rstd_of(SQ2, src, Bsrc, ss, Bss, col):
                sq2, Bsq2 = SQ2.next()
                S.op("dve", lambda e: e.tensor_tensor(out=sq2[:], in0=src, in1=src, op=ALU.mult), reads=[Bsrc], writes=[Bsq2])
                S.op("dve", lambda e: e.tensor_reduce(out=ss[:, col:col + 1], in_=sq2[:], axis=AX.X, op=ALU.add), reads=[Bsq2], writes=[Bss], acc=True)
                S.op("dve", lambda e: e.tensor_scalar(out=ss[:, col + 1:col + 2], in0=ss[:, col:col + 1], scalar1=1.0 / 1024, scalar2=EPS,
                                                      op0=ALU.mult, op1=ALU.add), reads=[Bss], writes=[Bss], acc=True)
                S.op("act", lambda e: e.activation(out=ss[:, col + 1:col + 2], in_=ss[:, col + 1:col + 2], func=AF.Ln), reads=[Bss], writes=[Bss], acc=True)
                S.op("act", lambda e: e.activation(out=ss[:, col + 1:col + 2], in_=ss[:, col + 1:col + 2], func=AF.Exp, scale=-0.5),
                     reads=[Bss], writes=[Bss], acc=True)

            with ExitStack() as st:
                W1 = sb(st, "W1", [128, 8, 4096], BF16)
                Bw1 = Buf("w1")
                with ExitStack() as st2:
                    load_w(st2, W1, w_ff1, 8, 4096, Bw1, 'c')
                    S.barrier()
                XL = Rot("xl1", [sb(st, f"xl1{i}", [128, 1024], F32) for i in range(2)])
                SQ2 = Rot("sq2", [sb(st, f"sq2{i}", [128, 1024], F32) for i in range(1)])
                SS2 = Rot("ss2", [sb(st, f"ss2{i}", [128, 4], F32) for i in range(2)])
                XN2 = Rot("xn2", [sb(st, f"xn2{i}", [128, 1024], BF16) for i in range(8)])
                H2 = Rot("h2", [sb(st, f"h2{i}", [128, 8, 512], BF16) for i in range(2)])
                RR = Rot("rr", [sb(st, f"rr{i}", [128, 512], F32) for i in range(3)])
                AT = Rot("at", [sb(st, f"at{i}", [128, 32, 512], BF16) for i in range(2)])
                xn2s = {}

                def cb_stageA(su):
                    for j in range(4):
                        n = 4 * su + j
                        xl, Bxl = XL.next()
                        S.dma("sp", xl[:], X1[n], key=Bxl, reads=[BX1], writes=[Bxl], acc=False)
                        ss, Bss = SS2.next()
                        S.op("pool", lambda e: e.memset(ss[:], 0.0), writes=[Bss])
                        rstd_of(SQ2, xl[:], Bxl, ss, Bss, 0)
                        xn, Bxn = XN2.next()
                        S.op("act", lambda e: e.activation(out=xn[:], in_=xl[:], func=AF.Copy, scale=ss[:, 1:2]), reads=[Bxl, Bss], writes=[Bxn])
                        xn2s[n] = (xn, Bxn)

                h2s = {}

                def cb_stageB(su):
                    h2, Bh2 = H2.next()
                    h2s[su] = (h2, Bh2)
                    for j in range(4):
                        n = 4 * su + j
                        seg = n // SEG
                        xn, Bxn = xn2s.pop(n)
                        pt, Bp = PS.next()
                        ptb = pt[:].bitcast(BF16)
                        for k in range(8):
                            S.op("pe", lambda e, k=k: e.transpose(out=ptb[:, 128 * k:128 * (k + 1)], in_=xn[:, 128 * k:128 * (k + 1)], identity=ident_b),
                                 reads=[Bxn, Bconst], writes=[Bp], acc=(k > 0))
                        for k in range(8):
                            if k % 2 == 0:
                                S.op("dve", lambda e, k=k: e.tensor_scalar(out=h2[:, k, 128 * j:128 * (j + 1)], in0=ptb[:, 128 * k:128 * (k + 1)],
                                                                           scalar1=a2T[:, k, seg:seg + 1], scalar2=modT[:, 24 + k, seg:seg + 1],
                                                                           op0=ALU.mult, op1=ALU.add), reads=[Bp, Bmod], writes=[Bh2], acc=(j + k > 0))
                            else:
                                S.op("act", lambda e, k=k: e.activation(out=h2[:, k, 128 * j:128 * (j + 1)], in_=ptb[:, 128 * k:128 * (k + 1)],
                                                                        func=AF.Identity, scale=a2T[:, k, seg:seg + 1], bias=modT[:, 24 + k, seg:seg + 1]),
                                     reads=[Bp, Bmod], writes=[Bh2], acc=(j + k > 0))

                if not _csk:
                    cb_stageA(0)
                    cb_stageB(0)
                for su in range(0 if _csk else NSUP):
                    h2, Bh2 = h2s.pop(su)
                    if su + 1 < NSUP:
                        cb_stageA(su + 1)
                    at, Bat = AT.next()
                    for fch in range(32):
                        pf, Bpf = PS.next()
                        for k in range(8):
                            S.op("pe", lambda e, fch=fch, k=k: e.matmul(pf[:, :], lhsT=W1[:, k, 128 * fch:128 * (fch + 1)], rhs=h2[:, k, :],
                                                                        start=(k == 0), stop=(k == 7)), reads=[Bw1, Bh2], writes=[Bpf], acc=(k > 0))
                        rr, Brr = RR.next()
                        S.op("act", lambda e: e.activation(out=rr[:], in_=pf[:, :], func=AF.Relu), reads=[Bpf], writes=[Brr])
                        en = "dve"
                        S.op(en, lambda e, fch=fch: e.tensor_tensor(out=at[:, fch, :], in0=rr[:], in1=rr[:], op=ALU.mult),
                             reads=[Brr], writes=[Bat], acc=(fch > 0))
                    S.dma("pool", ATD[su], at[:].rearrange("p a b -> p (a b)"), key=Bat, reads=[Bat], writes=[BATD])
                    if su + 1 < NSUP:
                        cb_stageB(su + 1)
                S.barrier()

            with ExitStack() as st:
                W2 = sb(st, "W2", [128, 32, 1024], BF16)
                fgb = sb(st, "fgb", [128, 1024], F32)
                Bw2, Bfg = Buf("w2"), Buf("fgb")
                S.dma("sp", fgb[:], finalg_b[:, :], key=Bfg, writes=[Bfg])
                with ExitStack() as st2:
                    load_w(st2, W2, w_ff2, 32, 1024, Bw2, 'd')
                    S.barrier()
                XL = Rot("xl2", [sb(st, f"xl2{i}", [128, 1024], F32) for i in range(2)])
                SQ2 = Rot("sq3", [sb(st, f"sq3{i}", [128, 1024], F32) for i in range(1)])
                SS2 = Rot("ss3", [sb(st, f"ss3{i}", [128, 4], F32) for i in range(2)])
                AT = Rot("at2", [sb(st, f"at2{i}", [128, 32, 512], BF16) for i in range(2)])
                X2 = Rot("x2", [sb(st, f"x2{i}", [128, 1024], F32) for i in range(2)])
                YO = Rot("yo", [sb(st, f"yo{i}", [128, 1024], F32) for i in range(2)])
                def at_load(su):
                    at, Bat = AT.next()
                    S.dma("sp", at[:].rearrange("p a b -> p (a b)"), ATD[su], key=Bat, reads=[BATD], writes=[Bat], acc=False)
                    return at, Bat

                def xl_load(n):
                    xl, Bxl = XL.next()
                    S.dma("sp", xl[:], X1[n], key=Bxl, reads=[BX1], writes=[Bxl], acc=False)
                    return xl, Bxl

                at_next = None if _csk else at_load(0)
                xl_next = None if _csk else xl_load(0)
                for su in range(0 if _csk else NSUP):
                    at, Bat = at_next
                    if su + 1 < NSUP:
                        at_next = at_load(su + 1)
                    for j in range(4):
                        n = 4 * su + j
                        seg = n // SEG
                        xl, Bxl = xl_next
                        if n + 1 < NT:
                            xl_next = xl_load(n + 1)
                        x2, Bx2 = X2.next()
                        for dh in range(2):
                            po, Bpo = PS.next()
                            for fch in range(32):
                                S.op("pe", lambda e, fch=fch: e.matmul(po[:, :], lhsT=at[:, fch, 128 * j:128 * (j + 1)], rhs=W2[:, fch, 512 * dh:512 * (dh + 1)],
                                                                       start=(fch == 0), stop=(fch == 31)), reads=[Bat, Bw2], writes=[Bpo], acc=(fch > 0))
                            S.op("dve", lambda e: e.tensor_tensor(out=x2[:, 512 * dh:512 * (dh + 1)], in0=po[:, :],
                                                                  in1=GTb[:, seg, 1024 + 512 * dh:1024 + 512 * (dh + 1)], op=ALU.mult),
                                 reads=[Bpo, Bgt], writes=[Bx2], acc=(dh > 0))
                        S.op("dve", lambda e: e.tensor_tensor(out=x2[:], in0=x2[:], in1=xl[:], op=ALU.add), reads=[Bx2, Bxl], writes=[Bx2])
                        ss, Bss = SS2.next()
                        S.op("pool", lambda e: e.memset(ss[:], 0.0), writes=[Bss])
                        rstd_of(SQ2, x2[:], Bx2, ss, Bss, 2)
                        yo, Byo = YO.next()
                        S.op("act", lambda e: e.activation(out=yo[:], in_=x2[:], func=AF.Copy, scale=ss[:, 3:4]), reads=[Bx2, Bss], writes=[Byo])
                        S.op("dve", lambda e: e.tensor_tensor(out=yo[:], in0=yo[:], in1=fgb[:], op=ALU.mult), reads=[Byo, Bfg], writes=[Byo])
                        S.dma("sp", y[128 * n:128 * (n + 1), :], yo[:], key=Byo, reads=[Byo], writes=[By])
                S.wait_bufs("sp", [By])
                S.barrier()
        print("instructions:", S.nins)
    return nc


def _consts():
    cm = np.zeros((128, 512), np.float32)
    cm[:, 0:128] = np.eye(128)
    cm[:, 128:256] = np.eye(128)[::-1]
    lo = np.triu(np.ones((64, 64), np.float32))
    up = np.tril(np.ones((64, 64), np.float32))
    cm[0:64, 256:320] = lo
    cm[64:128, 256:320] = lo
    cm[0:64, 320:384] = up
    cm[64:128, 320:384] = up
    return cm


def make_in_maps(inp):
    f = lambda a: np.ascontiguousarray(np.asarray(a, dtype=np.float32))
    xp, xs = f(inp["x_prompt"]), f(inp["x_sample"])
    cp, cs = f(inp["c_prompt"]), f(inp["c_sample"])
    fT = lambda v, n: np.ascontiguousarray(v.reshape(n, 128).T)
    rep = lambda v: np.ascontiguousarray(np.broadcast_to(v.reshape(1, -1), (128, v.size)))
    shared = {
        "ada_w": f(inp["ada_w"][0]), "ada_bT": fT(f(inp["ada_b"][0]), 48),
        "g1T": fT(f(inp["norm1_g"][0]), 8), "g2T": fT(f(inp["norm2_g"][0]), 8),
        "w_in": f(inp["w_in"][0]),
        "convw_b": rep(f(inp["conv_w"][0]).reshape(-1)), "convb_b": rep(f(inp["conv_b"][0])),
        "fw1": f(inp["filt_w1"][0]), "fw2": f(inp["filt_w2"][0]), "fw3": f(inp["filt_w3"][0]), "fwo": f(inp["filt_wo"][0]),
        "fbT": np.ascontiguousarray(np.stack([f(inp["filt_b1"][0]), f(inp["filt_b2"][0]), f(inp["filt_b3"][0])], 1)),
        "ffreqT": np.ascontiguousarray(f(inp["filt_freq"][0]).T),
        "fbiasT": fT(f(inp["filt_bias"][0]), 4),
        "lbT": np.ascontiguousarray(f(inp["hgrn_lb"]).reshape(2, 2, 4, 128).transpose(3, 0, 1, 2).reshape(128, 16)),
        "gnorm_b": rep(f(inp["gnorm_g"][0])), "finalg_b": rep(f(inp["final_g"])),
        "w_branch": f(inp["w_branch"][0]), "w_out": f(inp["w_out"][0]),
        "w_ff1": f(inp["w_ff1"][0]), "w_ff2": f(inp["w_ff2"][0]),
        "cmat": _consts(),
    }
    fb = np.linspace(1e-4, 15.0, 16, dtype=np.float32)
    fconst = np.zeros((128, 8), np.float32)
    fconst[1:17, 0] = fb
    fconst[1:17, 1] = np.pi / 2
    fconst[17:33, 0] = -fb
    deltas = np.abs(np.linspace(math.log(1e-2) / 1.5, math.log(1e-2) / 0.3, 512, dtype=np.float32))
    fconst[:, 2:6] = deltas.reshape(4, 128).T
    shared["fconst"] = fconst
    maps = []
    for r in range(8):
        m = dict(shared)
        if r < 4:
            m["x"] = np.ascontiguousarray(xp[2 * r:2 * r + 2].reshape(8192, 1024))
            cc = cp[2 * r:2 * r + 2]
            L, flag = 4096.0, 0.0
        else:
            m["x"] = np.ascontiguousarray(xs[r - 4].reshape(8192, 1024))
            cc = np.stack([cs[r - 4], cs[r - 4]])
            L, flag = 8192.0, 1.0
        m["cT"] = np.ascontiguousarray(cc.reshape(2, 8, 128).transpose(2, 1, 0).reshape(128, 16))
        meta = np.zeros((128, 8), np.float32)
        meta[:, 0] = flag
        meta[:, 1] = 1.0 - flag
        meta[:, 2] = 1.0 / (L - 1.0)
        meta[:, 3] = TWO_PI / L
        meta[:, 4] = L
        m["meta"] = meta
        maps.append(m)
    return maps


def kernel(**inputs):
    nc = build()
    maps = make_in_maps(inputs)
    res = run_bass_kernel_spmd(nc, maps, core_ids=list(range(8)))
    ys = [np.asarray(res.results[r]["y"], dtype=np.float32) for r in range(8)]
    y_prompt = np.concatenate([ys[r].reshape(2, 4096, 1024) for r in range(4)], 0)
    y_sample = np.stack([ys[r].reshape(8192, 1024) for r in range(4, 8)], 0)
    return (y_prompt, y_sample)
```

```python
import math
import numpy as np
from contextlib import ExitStack
import concourse.bass as bass
import concourse.mybir as mybir
from concourse.bass_utils import run_bass_kernel_spmd

F32 = mybir.dt.float32
BF16 = mybir.dt.bfloat16
I32 = mybir.dt.int32
ALU = mybir.AluOpType
AF = mybir.ActivationFunctionType
AX = mybir.AxisListType

NT = 64
SEG = 32
EPS = 1e-6
TWO_PI = 2.0 * math.pi


class Buf:
    def __init__(self, name):
        self.name = name
        self.w = {}
        self.r = {}
        self.dsem = None
        self.dcnt = 0


class Sched:
    def __init__(self, nc, es):
        self.nc = nc
        self.es = es
        self.eng = {}
        self.sem = {}
        self.cnt = {}
        self.seen = {}
        self.dbufs = []
        self.nins = 0

    def add_engine(self, name, eng):
        self.eng[name] = eng
        self.sem[name] = self.es.enter_context(self.nc.semaphore("s_" + name))
        self.cnt[name] = 0
        self.seen[name] = {}

    def _wait(self, e, toks):
        eng = self.eng[e]
        seen = self.seen[e]
        for sid, (sem, val) in toks.items():
            if seen.get(sid, 0) >= val:
                continue
            eng.wait_ge(sem, val)
            self.nins += 1
            seen[sid] = val

    @staticmethod
    def _merge(dst, sid, sem, val):
        if sid not in dst or dst[sid][1] < val:
            dst[sid] = (sem, val)

    def _deps(self, reads, writes):
        toks = {}
        for b in reads:
            for sid, (sem, val) in b.w.items():
                self._merge(toks, sid, sem, val)
        for b in writes:
            for sid, (sem, val) in b.w.items():
                self._merge(toks, sid, sem, val)
            for sid, (sem, val) in b.r.items():
                self._merge(toks, sid, sem, val)
        return toks

    def op(self, e, fn, reads=(), writes=(), acc=False):
        toks = self._deps(reads, writes)
        if e == "pe":
            toks.pop(id(self.sem[e]), None)
        self._wait(e, toks)
        ins = fn(self.eng[e])
        self.cnt[e] += 1
        self.nins += 1
        sem = self.sem[e]
        ins.then_inc(sem, 1)
        sid, val = id(sem), self.cnt[e]
        for b in reads:
            self._merge(b.r, sid, sem, val)
        for b in writes:
            if acc:
                self._merge(b.w, sid, sem, val)
            else:
                b.w = {sid: (sem, val)}
                b.r = {}
        return ins

    def dma(self, q, out, in_, key, reads=(), writes=(), acc=True, **kw):
        toks = self._deps(reads, writes)
        self._wait(q, toks)
        if key.dsem is None:
            key.dsem = self.es.enter_context(self.nc.semaphore("d_" + key.name))
            self.dbufs.append(key)
        ins = self.eng[q].dma_start(out=out, in_=in_, **kw)
        key.dcnt += 1
        self.nins += 1
        ins.then_inc(key.dsem, 16)
        sid, sem, val = id(key.dsem), key.dsem, 16 * key.dcnt
        for b in reads:
            self._merge(b.r, sid, sem, val)
        for b in writes:
            if acc:
                self._merge(b.w, sid, sem, val)
            else:
                b.w = {sid: (sem, val)}
                b.r = {}
        return ins

    def wait_bufs(self, e, bufs):
        toks = {}
        for b in bufs:
            for sid, (sem, val) in b.w.items():
                self._merge(toks, sid, sem, val)
            for sid, (sem, val) in b.r.items():
                self._merge(toks, sid, sem, val)
        self._wait(e, toks)

    def barrier(self):
        toks = {}
        for e, sem in self.sem.items():
            if self.cnt[e] > 0:
                toks[id(sem)] = (sem, self.cnt[e])
        for b in self.dbufs:
            toks[id(b.dsem)] = (b.dsem, 16 * b.dcnt)
        for e in self.eng:
            t = dict(toks)
            t.pop(id(self.sem[e]), None)
            self._wait(e, t)


class Rot:
    def __init__(self, name, aps):
        self.aps = aps
        self.bufs = [Buf(f"{name}{i}") for i in range(len(aps))]
        self.i = -1

    def next(self):
        self.i = (self.i + 1) % len(self.aps)
        return self.aps[self.i], self.bufs[self.i]


def build(dbg=False):
    nc = bass.Bass("TRN2", target_bir_lowering=False)

    def din(name, shape, dt=F32):
        return nc.dram_tensor(name, list(shape), dt, kind="ExternalInput").ap()

    def dscr(name, shape, dt):
        kind = "ExternalOutput" if dbg else "Internal"
        return nc.dram_tensor(name, list(shape), dt, kind=kind).ap()

    x = din("x", [NT * 128, 1024])
    cT = din("cT", [128, 16])
    meta = din("meta", [128, 8])
    ada_w = din("ada_w", [1024, 6144])
    ada_bT = din("ada_bT", [128, 48])
    g1T = din("g1T", [128, 8])
    g2T = din("g2T", [128, 8])
    w_in = din("w_in", [1024, 6144])
    convw_b = din("convw_b", [128, 3 * 1536])
    convb_b = din("convb_b", [128, 1536])
    fw1 = din("fw1", [33, 64])
    fw2 = din("fw2", [64, 64])
    fw3 = din("fw3", [64, 64])
    fwo = din("fwo", [64, 1024])
    fbT = din("fbT", [64, 3])
    ffreqT = din("ffreqT", [64, 3])
    fbiasT = din("fbiasT", [128, 4])
    fconst = din("fconst", [128, 8])
    lbT = din("lbT", [128, 16])
    gnorm_b = din("gnorm_b", [128, 512])
    finalg_b = din("finalg_b", [128, 1024])
    w_branch = din("w_branch", [1024, 1024])
    w_out = din("w_out", [1024, 1024])
    w_ff1 = din("w_ff1", [1024, 4096])
    w_ff2 = din("w_ff2", [4096, 1024])
    cmat = din("cmat", [128, 4 * 128])
    y = nc.dram_tensor("y", [NT * 128, 1024], F32, kind="ExternalOutput").ap()

    VT = dscr("VT", [4, 128, NT, 128], BF16)
    X0T = dscr("X0T", [4, 128, NT, 128], BF16)
    IT = dscr("IT", [NT, 128, 512], BF16)
    OGT = dscr("OGT", [NT, 128, 512], BF16)
    QT = dscr("QT", [NT, 128, 512], BF16)
    KFT = dscr("KFT", [NT, 128, 2, 512], BF16)
    GFT = dscr("GFT", [NT, 128, 2, 512], F32)
    GTT = dscr("GTT", [NT, 128, 2048], BF16)
    HTD = dscr("HTD", [NT // 4, 128, 8, 512], BF16)

    with ExitStack() as es:
        S = Sched(nc, es)
        S.add_engine("sp", nc.sync)
        S.add_engine("act", nc.scalar)
        S.add_engine("dve", nc.vector)
        S.add_engine("pool", nc.gpsimd)
        S.add_engine("pe", nc.tensor)

        def sb(st, name, shape, dt):
            return st.enter_context(nc.sbuf_tensor(name, list(shape), dt))

        psb = [es.enter_context(nc.psum_tensor(f"ps{i}", [128, 512], F32)) for i in range(8)]
        PS = Rot("ps", psb)

        cm_f = sb(es, "cm_f", [128, 512], F32)
        cm_b = sb(es, "cm_b", [128, 512], BF16)
        ones_f = sb(es, "ones_f", [128, 128], F32)
        meta_t = sb(es, "meta_t", [128, 8], F32)
        modT = sb(es, "modT", [128, 48, 2], F32)
        a1T = sb(es, "a1T", [128, 8, 2], F32)
        a2T = sb(es, "a2T", [128, 8, 2], F32)
        lb_t = sb(es, "lb_t", [128, 16], F32)
        lbv = sb(es, "lbv", [128, 8], F32)
        omlv = sb(es, "omlv", [128, 8], F32)
        Bconst = Buf("const")
        Bmod = Buf("mod")
        ident_f = cm_f[:, 0:128]
        ident_b = cm_b[:, 0:128]
        anti_b = cm_b[:, 128:256]
        FLAG = meta_t[:, 0:1]
        NFLAG = meta_t[:, 1:2]

        S.dma("sp", cm_f[:], cmat[:, :], key=Bconst, writes=[Bconst])
        S.dma("sp", meta_t[:], meta[:, :], key=Bconst, writes=[Bconst])
        S.dma("sp", lb_t[:], lbT[:, :], key=Bconst, writes=[Bconst])
        S.op("dve", lambda e: e.tensor_copy(out=cm_b[:], in_=cm_f[:]), reads=[Bconst], writes=[Bconst], acc=True)
        S.op("pool", lambda e: e.memset(ones_f[:], 1.0), writes=[Bconst], acc=True)
        S.op("dve", lambda e: e.tensor_tensor(out=lbv[:], in0=lb_t[:, 0:8], in1=lb_t[:, 8:16], op=ALU.subtract),
             reads=[Bconst], writes=[Bconst], acc=True)
        S.op("act", lambda e: e.activation(out=lbv[:], in_=lbv[:], func=AF.Sigmoid), reads=[Bconst], writes=[Bconst], acc=True)
        S.op("dve", lambda e: e.tensor_scalar(out=omlv[:], in0=lbv[:], scalar1=-1.0, scalar2=1.0, op0=ALU.mult, op1=ALU.add),
             reads=[Bconst], writes=[Bconst], acc=True)

        with ExitStack() as st:
            scT = sb(st, "scT", [128, 16], F32)
            adab = sb(st, "adab", [128, 48], F32)
            g12 = sb(st, "g12", [128, 16], F32)
            acc = sb(st, "acc", [128, 96], F32)
            stg = [sb(st, f"adastg{i}", [128, 3072], F32) for i in range(2)]
            STG = Rot("adastg", stg)
            Bsc, Bacc = Buf("sc"), Buf("acc")
            S.dma("sp", scT[:], cT[:, :], key=Bsc, writes=[Bsc])
            S.dma("sp", adab[:], ada_bT[:, :], key=Bsc, writes=[Bsc])
            S.dma("sp", g12[:, 0:8], g1T[:, :], key=Bsc, writes=[Bsc])
            S.dma("sp", g12[:, 8:16], g2T[:, :], key=Bsc, writes=[Bsc])
            S.op("act", lambda e: e.activation(out=scT[:], in_=scT[:], func=AF.Silu), reads=[Bsc], writes=[Bsc], acc=True)
            S.op("pool", lambda e: e.memset(acc[:], 0.0), writes=[Bacc])
            for k in range(8):
                for hf in range(2):
                    stt, Bst = STG.next()
                    S.dma("sp", stt[:], ada_w[128 * k:128 * (k + 1), 3072 * hf:3072 * (hf + 1)], key=Bst, writes=[Bst], acc=False)
                    pt, Bp = PS.next()
                    for jj in range(24):
                        S.op("pe", lambda e, jj=jj: e.matmul(pt[:, 2 * jj:2 * jj + 2], lhsT=stt[:, 128 * jj:128 * (jj + 1)],
                                                             rhs=scT[:, 2 * k:2 * k + 2], start=True, stop=True),
                             reads=[Bst, Bsc], writes=[Bp], acc=(jj > 0))
                    S.op("dve", lambda e: e.tensor_tensor(out=acc[:, 48 * hf:48 * (hf + 1)], in0=acc[:, 48 * hf:48 * (hf + 1)],
                                                          in1=pt[:, 0:48], op=ALU.add), reads=[Bp, Bacc], writes=[Bacc], acc=True)
            S.op("dve", lambda e: e.tensor_tensor(out=modT[:], in0=acc[:].rearrange("p (j s) -> p j s", s=2),
                                                  in1=adab[:].unsqueeze(2).to_broadcast([128, 48, 2]), op=ALU.add),
                 reads=[Bacc, Bsc], writes=[Bmod])
            for (aT, goff, joff) in ((a1T, 0, 8), (a2T, 8, 32)):
                S.op("dve", lambda e, aT=aT, joff=joff: e.tensor_scalar(out=aT[:], in0=modT[:, joff:joff + 8, :], scalar1=1.0,
                                                                         scalar2=None, op0=ALU.add), reads=[Bmod], writes=[Bmod], acc=True)
                S.op("dve", lambda e, aT=aT, goff=goff: e.tensor_tensor(out=aT[:], in0=aT[:],
                                                                         in1=g12[:, goff:goff + 8].unsqueeze(2).to_broadcast([128, 8, 2]),
                                                                         op=ALU.mult), reads=[Bmod, Bsc], writes=[Bmod], acc=True)
            S.barrier()

        with ExitStack() as st:
            Wc = sb(st, "Wc", [128, 3, 8, 1536], BF16)
            Wr = sb(st, "Wr", [128, 8, 4608], BF16)
            Bw = Buf("w")
            with ExitStack() as st2:
                cwb = sb(st2, "cwb", [128, 4608], F32)
                stg = [sb(st2, f"wstg{i}", [128, 3072], F32) for i in range(2)]
                STG = Rot("wstg", stg)
                Bcw = Buf("cw")
                S.dma("sp", cwb[:], convw_b[:, :], key=Bcw, writes=[Bcw])
                ei = 0
                for k in range(8):
                    for hf in range(2):
                        stt, Bst = STG.next()
                        S.dma("sp", stt[:], w_in[128 * k:128 * (k + 1), 3072 * hf:3072 * (hf + 1)], key=Bst, writes=[Bst], acc=False)
                        if hf == 0:
                            for j in range(3):
                                en = "dve"
                                ei += 1
                                S.op(en, lambda e, j=j: e.tensor_tensor(out=Wc[:, j, k, :], in0=stt[:, 0:1536],
                                                                        in1=cwb[:, 1536 * j:1536 * (j + 1)], op=ALU.mult),
                                     reads=[Bst, Bcw], writes=[Bw], acc=True)
                            S.op("act", lambda e: e.activation(out=Wr[:, k, 0:1536], in_=stt[:, 1536:3072], func=AF.Copy),
                                 reads=[Bst], writes=[Bw], acc=True)
                        else:
                            S.op("act", lambda e: e.activation(out=Wr[:, k, 1536:3072], in_=stt[:, 0:1536], func=AF.Copy),
                                 reads=[Bst], writes=[Bw], acc=True)
                            en = "dve"
                            ei += 1
                            S.op(en, lambda e: e.tensor_copy(out=Wr[:, k, 3072:4608], in_=stt[:, 1536:3072]),
                                 reads=[Bst], writes=[Bw], acc=True)
                S.barrier()

            cvb = sb(st, "cvb", [128, 1536], F32)
            Bcv = Buf("cvb")
            S.dma("sp", cvb[:], convb_b[:, :], key=Bcv, writes=[Bcv])
            XT = Rot("xt", [sb(st, f"xt{i}", [128, 1024], F32) for i in range(2)])
            SS = Rot("ss", [sb(st, f"ss{i}", [128, 2], F32) for i in range(2)])
            XN = Rot("xn", [sb(st, f"xn{i}", [128, 1024], BF16) for i in range(3)])
            HT = Rot("hT", [sb(st, f"hT{i}", [128, 8, 130], BF16) for i in range(4)])
            HY = Rot("hy", [sb(st, f"hy{i}", [128, 1536], F32) for i in range(1)])
            VP = Rot("vp", [sb(st, f"vp{i}", [128, 512], BF16) for i in range(2)])
            VR = Rot("vr", [sb(st, f"vr{i}", [128, 512], BF16) for i in range(2)])
            X0 = Rot("x0", [sb(st, f"x0{i}", [128, 512], BF16) for i in range(2)])
            IO = Rot("io", [sb(st, f"io{i}", [128, 2, 512], BF16) for i in range(2)])

            hslots = {}
            BHTD = Buf("HTD")

            xns = {}

            def prepA(n):
                xt, Bx = XT.next()
                S.dma("sp", xt[:], x[128 * n:128 * (n + 1), :], key=Bx, writes=[Bx], acc=False)
                ss, Bss = SS.next()
                sq, Bsq = HY.aps[0], HY.bufs[0]
                S.op("dve", lambda e: e.tensor_tensor(out=sq[:, 0:1024], in0=xt[:], in1=xt[:], op=ALU.mult), reads=[Bx], writes=[Bsq])
                S.op("dve", lambda e: e.tensor_reduce(out=ss[:, 0:1], in_=sq[:, 0:1024], axis=AX.X, op=ALU.add), reads=[Bsq], writes=[Bss])
                S.op("dve", lambda e: e.tensor_scalar(out=ss[:, 1:2], in0=ss[:, 0:1], scalar1=1.0 / 1024, scalar2=EPS,
                                                      op0=ALU.mult, op1=ALU.add), reads=[Bss], writes=[Bss], acc=True)
                S.op("act", lambda e: e.activation(out=ss[:, 1:2], in_=ss[:, 1:2], func=AF.Ln), reads=[Bss], writes=[Bss], acc=True)
                S.op("act", lambda e: e.activation(out=ss[:, 1:2], in_=ss[:, 1:2], func=AF.Exp, scale=-0.5), reads=[Bss], writes=[Bss], acc=True)
                xn, Bxn = XN.next()
                S.op("act", lambda e: e.activation(out=xn[:], in_=xt[:], func=AF.Copy, scale=ss[:, 1:2]), reads=[Bx, Bss], writes=[Bxn])
                xns[n] = (xn, Bxn)

            def prepB(n):
                seg = n // SEG
                xn, Bxn = xns.pop(n)
                pt, Bp = PS.next()
                ptb = pt[:].bitcast(BF16)
                for k in range(8):
                    S.op("pe", lambda e, k=k: e.transpose(out=ptb[:, 128 * k:128 * (k + 1)], in_=xn[:, 128 * k:128 * (k + 1)], identity=ident_b),
                         reads=[Bxn, Bconst], writes=[Bp], acc=(k > 0))
                hT, Bh = HT.next()
                for k in range(8):
                    en = ("dve", "act")[k % 2]
                    if en == "dve":
                        S.op("dve", lambda e, k=k: e.tensor_scalar(out=hT[:, k, 1:129], in0=ptb[:, 128 * k:128 * (k + 1)],
                                                                   scalar1=a1T[:, k, seg:seg + 1], scalar2=modT[:, k, seg:seg + 1],
                                                                   op0=ALU.mult, op1=ALU.add), reads=[Bp, Bmod], writes=[Bh], acc=(k > 0))
                    else:
                        S.op("act", lambda e, k=k: e.activation(out=hT[:, k, 1:129], in_=ptb[:, 128 * k:128 * (k + 1)], func=AF.Identity,
                                                                scale=a1T[:, k, seg:seg + 1], bias=modT[:, k, seg:seg + 1]),
                             reads=[Bp, Bmod], writes=[Bh], acc=(k > 0))
                hslots[n] = (hT, Bh)
                S.dma("sp", HTD[n // 4, :, :, 128 * (n % 4):128 * (n % 4 + 1)], hT[:, :, 1:129], key=Bh, reads=[Bh], writes=[BHTD])
                if n == 0:
                    S.op("pool", lambda e: e.memset(hT[:, :, 0:1], 0.0), writes=[Bh], acc=True)
                else:
                    pT, Bph = hslots[n - 1]
                    if n == SEG:
                        S.op("dve", lambda e: e.tensor_scalar(out=hT[:, :, 0:1], in0=pT[:, :, 128:129], scalar1=FLAG, scalar2=None, op0=ALU.mult),
                             reads=[Bph, Bconst], writes=[Bh], acc=True)
                        S.op("dve", lambda e: e.tensor_scalar(out=pT[:, :, 129:130], in0=hT[:, :, 1:2], scalar1=FLAG, scalar2=None, op0=ALU.mult),
                             reads=[Bh, Bconst], writes=[Bph], acc=True)
                    else:
                        S.op("dve", lambda e: e.tensor_copy(out=hT[:, :, 0:1], in_=pT[:, :, 128:129]), reads=[Bph], writes=[Bh], acc=True)
                        S.op("dve", lambda e: e.tensor_copy(out=pT[:, :, 129:130], in_=hT[:, :, 1:2]), reads=[Bh], writes=[Bph], acc=True)
                if n == NT - 1:
                    S.op("pool", lambda e: e.memset(hT[:, :, 129:130], 0.0), writes=[Bh], acc=True)

            def mm_tile(n):
                hT, Bh = hslots[n]
                hy, Bhy = HY.next()
                for cb in range(3):
                    pt, Bp = PS.next()
                    i = 0
                    for j in range(3):
                        for k in range(8):
                            S.op("pe", lambda e, j=j, k=k, i=i: e.matmul(pt[:, :], lhsT=hT[:, k, j:j + 128],
                                                                         rhs=Wc[:, j, k, 512 * cb:512 * (cb + 1)],
                                                                         start=(i == 0), stop=(i == 23)),
                                 reads=[Bh, Bw], writes=[Bp], acc=(i > 0))
                            i += 1
                    S.op("dve", lambda e: e.tensor_tensor(out=hy[:, 512 * cb:512 * (cb + 1)], in0=pt[:, :], in1=cvb[:, 512 * cb:512 * (cb + 1)],
                                                          op=ALU.add), reads=[Bp, Bcv], writes=[Bhy], acc=(cb > 0))
                vp, Bvp = VP.next()
                S.op("dve", lambda e: e.tensor_tensor(out=vp[:], in0=hy[:, 1024:1536], in1=hy[:, 512:1024], op=ALU.mult),
                     reads=[Bhy], writes=[Bvp])
                x0, Bx0 = X0.next()
                S.op("act", lambda e: e.activation(out=x0[:], in_=hy[:, 0:512], func=AF.Copy), reads=[Bhy], writes=[Bx0])
                S.dma("sp", X0T[:, :, n, :].rearrange("g p c -> p g c"), x0[:].rearrange("p (g c) -> p g c", g=4), key=Bx0, reads=[Bx0])
                io, Bio = IO.next()
                for ii, (c0, fn) in enumerate(((512, AF.Copy), (2048, AF.Silu))):
                    pt, Bp = PS.next()
                    for k in range(8):
                        S.op("pe", lambda e, k=k: e.matmul(pt[:, :], lhsT=hT[:, k, 1:129], rhs=Wr[:, k, c0:c0 + 512],
                                                           start=(k == 0), stop=(k == 7)), reads=[Bh, Bw], writes=[Bp], acc=(k > 0))
                    S.op("act", lambda e: e.activation(out=io[:, ii, :], in_=pt[:, :], func=fn), reads=[Bp], writes=[Bio], acc=(ii > 0))
                pt, Bp = PS.next()
                S.op("pe", lambda e: e.matmul(pt[:, :], lhsT=anti_b, rhs=vp[:], start=True, stop=True), reads=[Bvp, Bconst], writes=[Bp])
                vr, Bvr = VR.next()
                S.op("act", lambda e: e.activation(out=vr[:], in_=pt[:, :], func=AF.Copy), reads=[Bp], writes=[Bvr])
                S.dma("sp", VT[:, :, n, :].rearrange("g p c -> p g c"), vr[:].rearrange("p (g c) -> p g c", g=4), key=Bvr, reads=[Bvr])
                S.dma("sp", IT[n], io[:, 0, :], key=Bio, reads=[Bio])
                S.dma("sp", OGT[n], io[:, 1, :], key=Bio, reads=[Bio])


            prepA(0)
            prepA(1)
            prepA(2)
            prepB(0)
            prepB(1)
            for n in range(NT):
                if n + 3 < NT:
                    prepA(n + 3)
                if n + 2 < NT:
                    prepB(n + 2)
                mm_tile(n)
            S.barrier()

        with ExitStack() as st:
            WrB = sb(st, "WrB", [128, 8, 3584], BF16)
            BwB = Buf("wB")
            with ExitStack() as st2:
                stg = [sb(st2, f"wstgB{i}", [128, 2048], F32) for i in range(2)]
                STG = Rot("wstgB", stg)
                ei = 0
                for k in range(8):
                    for (src0, wd, dst0) in ((1536, 512, 0), (2560, 1024, 512), (4096, 2048, 1536)):
                        stt, Bst = STG.next()
                        S.dma("sp", stt[:, 0:wd], w_in[128 * k:128 * (k + 1), src0:src0 + wd], key=Bst, writes=[Bst], acc=False)
                        en = ("dve", "act")[ei % 2]
                        ei += 1
                        if en == "act":
                            S.op("act", lambda e: e.activation(out=WrB[:, k, dst0:dst0 + wd], in_=stt[:, 0:wd], func=AF.Copy),
                                 reads=[Bst], writes=[BwB], acc=True)
                        else:
                            S.op(en, lambda e: e.tensor_copy(out=WrB[:, k, dst0:dst0 + wd], in_=stt[:, 0:wd]), reads=[Bst], writes=[BwB], acc=True)
                S.barrier()
            HS = Rot("hs", [sb(st, f"hs{i}", [128, 8, 512], BF16) for i in range(2)])
            QS = Rot("qs", [sb(st, f"qs{i}", [128, 4, 4, 128], BF16) for i in range(2)])
            SG = Rot("sg", [sb(st, f"sg{i}", [128, 4, 8, 128], F32) for i in range(1)])
            KS = Rot("ks", [sb(st, f"ks{i}", [128, 4, 8, 128], BF16) for i in range(2)])
            GS = Rot("gs", [sb(st, f"gs{i}", [128, 4, 8, 128], F32) for i in range(1)])
            GA = Rot("ga", [sb(st, f"ga{i}", [128, 4, 16, 128], BF16) for i in range(2)])
            def hs_load(su):
                hs, Bhs = HS.next()
                S.dma("sp", hs[:], HTD[su], key=Bhs, reads=[BHTD], writes=[Bhs], acc=False)
                return hs, Bhs

            hs_next = hs_load(0)
            for su in range(NT // 4):
                hs, Bhs = hs_next
                if su + 1 < NT // 4:
                    hs_next = hs_load(su + 1)

                def fm(c0):
                    pt, Bp = PS.next()
                    for k in range(8):
                        S.op("pe", lambda e, k=k: e.matmul(pt[:, :], lhsT=WrB[:, k, c0:c0 + 128], rhs=hs[:, k, :], start=(k == 0), stop=(k == 7)),
                             reads=[BwB, Bhs], writes=[Bp], acc=(k > 0))
                    return pt[:, :].rearrange("p (j t) -> p j t", t=128), Bp

                qs, Bqs = QS.next()
                for h in range(4):
                    pv, Bp = fm(128 * h)
                    S.op("act", lambda e, h=h: e.activation(out=qs[:, :, h, :], in_=pv, func=AF.Silu), reads=[Bp], writes=[Bqs], acc=(h > 0))
                sg, Bsg = SG.next()
                for dh in range(8):
                    pv, Bp = fm(512 + 128 * dh)
                    S.op("act", lambda e, dh=dh: e.activation(out=sg[:, :, dh, :], in_=pv, func=AF.Sigmoid), reads=[Bp], writes=[Bsg], acc=(dh > 0))
                    S.op("dve", lambda e, dh=dh: e.tensor_scalar(out=sg[:, :, dh, :], in0=sg[:, :, dh, :], scalar1=omlv[:, dh:dh + 1],
                                                                 scalar2=lbv[:, dh:dh + 1], op0=ALU.mult, op1=ALU.add),
                         reads=[Bsg, Bconst], writes=[Bsg], acc=True)
                ks, Bks = KS.next()
                gs, Bgs = GS.next()
                S.op("dve", lambda e: e.tensor_scalar(out=ks[:], in0=sg[:], scalar1=-1.0, scalar2=1.0, op0=ALU.mult, op1=ALU.add),
                     reads=[Bsg], writes=[Bks])
                S.op("act", lambda e: e.activation(out=gs[:], in_=sg[:], func=AF.Ln), reads=[Bsg], writes=[Bgs])
                ga, Bga = GA.next()
                for gq in range(16):
                    pv, Bp = fm(1536 + 128 * gq)
                    S.op("act", lambda e, gq=gq: e.activation(out=ga[:, :, gq, :], in_=pv, func=AF.Sigmoid), reads=[Bp], writes=[Bga], acc=(gq > 0))
                for j in range(4):
                    n = 4 * su + j
                    S.dma("pool", QT[n], qs[:, j, :, :].rearrange("p a b -> p (a b)"), key=Bqs, reads=[Bqs])
                    S.dma("pool", KFT[n], ks[:, j, :, :].rearrange("p (d h) t -> p d (h t)", d=2), key=Bks, reads=[Bks])
                    S.dma("pool", GFT[n], gs[:, j, :, :].rearrange("p (d h) t -> p d (h t)", d=2), key=Bgs, reads=[Bgs])
                    S.dma("pool", GTT[n], ga[:, j, :, :].rearrange("p a b -> p (a b)"), key=Bga, reads=[Bga])
            S.barrier()

        G = dscr("G", [512, 16384], BF16)
        YAT = dscr("YAT", [NT, 128, 4, 128], BF16)
        BG = Buf("G")
        RNb = sb(es, "RNb", [128, 3, 512], F32)
        Brn = Buf("rn")
        with ExitStack() as st:
            w1t = sb(st, "w1t", [33, 64], F32)
            w2t = sb(st, "w2t", [64, 64], F32)
            w3t = sb(st, "w3t", [64, 64], F32)
            wot = sb(st, "wot", [64, 1024], F32)
            fbt = sb(st, "fbt", [64, 3], F32)
            frq = sb(st, "frq", [64, 3], F32)
            bfq = sb(st, "bfq", [64, 3], F32)
            fcs = sb(st, "fcs", [128, 8], F32)
            fbi = sb(st, "fbi", [128, 4], F32)
            acol = sb(st, "acol", [128, 1], F32)
            wsc = sb(st, "wsc", [128, 4], F32)
            nsum = sb(st, "nsum", [128, 2, 4, 16], F32)
            lag0 = sb(st, "lag0", [128, 4], F32)
            nrm = sb(st, "nrm", [128, 4], F32)
            rn = sb(st, "rn", [128, 4], F32)
            pch = sb(st, "pch", [128, 4], BF16)
            dg = sb(st, "dgf", [128, 128], F32)
            Bfp, Bns, Bdg = Buf("fp"), Buf("ns"), Buf("dgf")
            for (t_, d_) in ((w1t, fw1), (w2t, fw2), (w3t, fw3), (wot, fwo), (fbt, fbT), (frq, ffreqT), (fcs, fconst), (fbi, fbiasT)):
                S.dma("sp", t_[:], d_[:, :], key=Bfp, writes=[Bfp])
            S.op("dve", lambda e: e.tensor_tensor(out=bfq[:], in0=fbt[:], in1=frq[:], op=ALU.mult), reads=[Bfp], writes=[Bfp], acc=True)
            S.op("dve", lambda e: e.tensor_tensor(out=acol[:], in0=fcs[:, 0:1], in1=meta_t[:, 3:4], op=ALU.mult),
                 reads=[Bfp, Bconst], writes=[Bfp], acc=True)
            S.op("dve", lambda e: e.tensor_scalar(out=wsc[:], in0=fcs[:, 2:6], scalar1=meta_t[:, 2:3], scalar2=-1.0, op0=ALU.mult, op1=ALU.mult),
                 reads=[Bfp, Bconst], writes=[Bfp], acc=True)
            S.op("pool", lambda e: e.memset(nsum[:], 0.0), writes=[Bns])
            NI = Rot("ni", [sb(st, f"ni{i}", [128, 512], I32) for i in range(2)])
            NF = Rot("nf", [sb(st, f"nf{i}", [128, 512], F32) for i in range(4)])
            MK = Rot("mk", [sb(st, f"mk{i}", [128, 512], F32) for i in range(2)])
            AR = Rot("ar", [sb(st, f"ar{i}", [64, 512], F32) for i in range(4)])
            KI = Rot("ki", [sb(st, f"ki{i}", [64, 512], I32) for i in range(4)])
            KF = Rot("kf", [sb(st, f"kf{i}", [64, 512], F32) for i in range(4)])
            HH = Rot("hh", [sb(st, f"hh{i}", [64, 512], F32) for i in range(6)])
            WN = Rot("wn", [sb(st, f"wn{i}", [128, 512], F32) for i in range(4)])
            HW = Rot("hw", [sb(st, f"hw{i}", [128, 512], F32) for i in range(4)])
            AB = Rot("ab", [sb(st, f"ab{i}", [128, 512], F32) for i in range(4)])
            HB = Rot("hb", [sb(st, f"hb{i}", [128, 512], BF16) for i in range(4)])

            def sin_reduce(arg, Barg, rows, out, Bout):
                ki, Bki = KI.next()
                kf, Bkf = KF.next()
                S.op("dve", lambda e: e.tensor_scalar(out=ki[0:rows, :], in0=arg[0:rows, :], scalar1=1.0 / TWO_PI, scalar2=None, op0=ALU.mult),
                     reads=[Barg], writes=[Bki])
                S.op("dve", lambda e: e.tensor_copy(out=kf[0:rows, :], in_=ki[0:rows, :]), reads=[Bki], writes=[Bkf])
                S.op("dve", lambda e: e.scalar_tensor_tensor(out=arg[0:rows, :], in0=kf[0:rows, :], scalar=-TWO_PI, in1=arg[0:rows, :],
                                                             op0=ALU.mult, op1=ALU.add), reads=[Bkf, Barg], writes=[Barg])
                S.op("dve", lambda e: e.tensor_scalar(out=arg[0:rows, :], in0=arg[0:rows, :], scalar1=-3.141592, scalar2=3.141592,
                                                      op0=ALU.max, op1=ALU.min), reads=[Barg], writes=[Barg])
                S.op("act", lambda e: e.activation(out=out[0:rows, :], in_=arg[0:rows, :], func=AF.Sin), reads=[Barg], writes=[Bout])

            def f_gen(tiles):
                for ptile in tiles:
                    m0 = 512 * ptile
                    ni, Bni = NI.next()
                    nf, Bnf = NF.next()
                    S.op("pool", lambda e: e.iota(ni[:], pattern=[[1, 512]], base=m0 - 8192, channel_multiplier=0), writes=[Bni])
                    S.op("dve", lambda e: e.tensor_copy(out=nf[:], in_=ni[:]), reads=[Bni], writes=[Bnf])
                    S.op("dve", lambda e: e.scalar_tensor_tensor(out=nf[:], in0=nf[:], scalar=-1.0, in1=nf[:], op0=ALU.mult, op1=ALU.max),
                         reads=[Bnf], writes=[Bnf])
                    mk, Bmk = MK.next()
                    S.op("dve", lambda e: e.tensor_scalar(out=mk[:], in0=nf[:], scalar1=meta_t[:, 4:5], scalar2=None, op0=ALU.is_ge),
                         reads=[Bnf, Bconst], writes=[Bmk])
                    S.op("dve", lambda e: e.scalar_tensor_tensor(out=mk[:], in0=mk[:], scalar=1.0e6, in1=nf[:], op0=ALU.mult, op1=ALU.add),
                         reads=[Bmk, Bnf], writes=[Bmk])
                    ar, Bar = AR.next()
                    S.op("dve", lambda e: e.tensor_scalar(out=ar[0:33, :], in0=nf[0:33, :], scalar1=acol[0:33, 0:1], scalar2=fcs[0:33, 1:2],
                                                          op0=ALU.mult, op1=ALU.add), reads=[Bnf, Bfp], writes=[Bar])
                    hcur, Bhc = HH.next()
                    sin_reduce(ar, Bar, 33, hcur, Bhc)
                    S.op("dve", lambda e: e.tensor_scalar(out=hcur[0:1, :], in0=nf[0:1, :], scalar1=meta_t[0:1, 2:3], scalar2=None, op0=ALU.mult),
                         reads=[Bnf, Bconst], writes=[Bhc], acc=True)
                    yield
                    K_in = 33
                    for li, wt in enumerate((w1t, w2t, w3t)):
                        pt, Bp = PS.next()
                        S.op("pe", lambda e: e.matmul(pt[0:64, :], lhsT=wt[0:K_in, :], rhs=hcur[0:K_in, :], start=True, stop=True),
                             reads=[Bfp, Bhc], writes=[Bp])
                        ar, Bar = AR.next()
                        S.op("dve", lambda e: e.tensor_scalar(out=ar[:, :], in0=pt[0:64, :], scalar1=frq[:, li:li + 1], scalar2=bfq[:, li:li + 1],
                                                              op0=ALU.mult, op1=ALU.add), reads=[Bp, Bfp], writes=[Bar])
                        hcur, Bhc = HH.next()
                        sin_reduce(ar, Bar, 64, hcur, Bhc)
                        K_in = 64
                        yield
                    bwd = ptile < 16
                    for j in range(4):
                        c0 = (512 if bwd else 0) + 128 * j
                        pt, Bp = PS.next()
                        S.op("pe", lambda e: e.matmul(pt[:, :], lhsT=wot[:, c0:c0 + 128], rhs=hcur[0:64, :], start=True, stop=True),
                             reads=[Bfp, Bhc], writes=[Bp])
                        wn, Bwn = WN.next()
                        S.op("act", lambda e: e.activation(out=wn[:], in_=mk[:], func=AF.Exp, scale=wsc[:, j:j + 1]), reads=[Bmk, Bfp], writes=[Bwn])
                        hw, Bhw = HW.next()
                        S.op("dve", lambda e: e.tensor_tensor(out=hw[:], in0=pt[:, :], in1=wn[:], op=ALU.mult), reads=[Bp, Bwn], writes=[Bhw])
                        ab, Bab = AB.next()
                        S.op("act", lambda e: e.activation(out=ab[:], in_=hw[:], func=AF.Abs), reads=[Bhw], writes=[Bab])
                        S.op("dve", lambda e: e.tensor_reduce(out=nsum[:, (0 if bwd else 1), j, (ptile % 16):(ptile % 16) + 1], in_=ab[:],
                                                              axis=AX.X, op=ALU.add), reads=[Bab], writes=[Bns], acc=True)
                        if ptile == 16:
                            S.op("dve", lambda e: e.tensor_copy(out=lag0[:, j:j + 1], in_=hw[:, 0:1]), reads=[Bhw], writes=[Bns], acc=True)
                        hb, Bhb = HB.next()
                        S.op("act", lambda e: e.activation(out=hb[:], in_=hw[:], func=AF.Copy), reads=[Bhw], writes=[Bhb])
                        S.dma("sp", G[128 * j:128 * (j + 1), m0:m0 + 512], hb[:], key=Bhb, reads=[Bhb], writes=[BG])
                        yield

            gens = [f_gen(list(range(0, 16))), f_gen(list(range(16, 32)))]
            alive = [True, True]
            while any(alive):
                for gi in range(2):
                    if alive[gi]:
                        try:
                            next(gens[gi])
                        except StopIteration:
                            alive[gi] = False
            S.op("dve", lambda e: e.tensor_reduce(out=nrm[:], in_=nsum[:].rearrange("p d j t -> p j d t"), axis=AX.XY, op=ALU.add),
                 reads=[Bns], writes=[Bns], acc=True)
            S.op("dve", lambda e: e.tensor_scalar(out=nrm[:], in0=nrm[:], scalar1=EPS, scalar2=None, op0=ALU.add), reads=[Bns], writes=[Bns], acc=True)
            S.op("dve", lambda e: e.reciprocal(out=rn[:], in_=nrm[:]), reads=[Bns], writes=[Bns], acc=True)
            S.op("dve", lambda e: e.tensor_tensor(out=nrm[:], in0=nrm[:], in1=fbi[:], op=ALU.mult), reads=[Bns, Bfp], writes=[Bns], acc=True)
            S.op("dve", lambda e: e.tensor_tensor(out=pch[:], in0=nrm[:], in1=lag0[:], op=ALU.add), reads=[Bns], writes=[Bns], acc=True)
            S.barrier()
            for j in range(4):
                S.dma("sp", G[128 * j:128 * (j + 1), 8192:8193], pch[:, j:j + 1], key=Bns, reads=[Bns], writes=[BG], allow_slow_non_contiguous=True)
                S.op("dve", lambda e: e.tensor_scalar(out=dg[:], in0=ident_f, scalar1=rn[:, j:j + 1], scalar2=None, op0=ALU.mult),
                     reads=[Bns, Bconst], writes=[Bdg])
                pt, Bp = PS.next()
                S.op("pe", lambda e: e.matmul(pt[:, 0:128], lhsT=ones_f[:], rhs=dg[:], start=True, stop=True), reads=[Bdg, Bconst], writes=[Bp])
                S.op("act", lambda e: e.activation(out=RNb[:, 0, 128 * j:128 * (j + 1)], in_=pt[:, 0:128], func=AF.Copy),
                     reads=[Bp], writes=[Brn], acc=True)
            S.op("dve", lambda e: e.tensor_scalar(out=RNb[:, 1, :], in0=RNb[:, 0, :], scalar1=FLAG, scalar2=None, op0=ALU.mult),
                 reads=[Brn, Bconst], writes=[Brn], acc=True)
            S.op("dve", lambda e: e.tensor_scalar(out=RNb[:, 2, :], in0=RNb[:, 0, :], scalar1=NFLAG, scalar2=None, op0=ALU.mult),
                 reads=[Brn, Bconst], writes=[Brn], acc=True)
            S.barrier()

        OF = dscr("OF", [NT, 128, 512], F32)
        YBT = dscr("YBT", [NT, 128, 4, 128], BF16)
        BOF = Buf("OF")
        with ExitStack() as st:
            V2 = sb(st, "V2", [128, 128, 96], BF16)
            X0g = sb(st, "X0g", [128, 64, 128], BF16)
            Yb = sb(st, "Yb", [128, 64, 128], BF16)
            HK = Rot("hk", [sb(st, f"hk{i}", [128, 16264], BF16) for i in range(2)])
            YS = Rot("ys", [sb(st, f"ys{i}", [128, 8, 128], BF16) for i in range(2)])
            TH = Rot("th", [sb(st, f"th{i}", [128, 2, 32], F32) for i in range(2)])
            TH0 = Rot("th0", [sb(st, f"th0{i}", [128, 32], F32) for i in range(2)])
            Bve, Bxg, Byb = Buf("vext"), Buf("x0g"), Buf("yb")
            dlist = [0] + [dd for a in range(1, 64) for dd in (a, -a)]
            HBK = Rot("hbk", [psb[6], psb[7]])
            HBK.bufs = [PS.bufs[6], PS.bufs[7]]

            def h_gen():
                for cg in range(4):
                    S.dma("sp", Yb[:], VT[cg], key=Byb, writes=[Byb], acc=False)
                    S.dma("sp", X0g[:], X0T[cg], key=Bxg, writes=[Bxg], acc=False)
                    S.op("dve", lambda e: e.tensor_copy(out=V2[:, :, 0:32].rearrange("p c s -> p s c"), in_=Yb[:, 0:32, :]),
                         reads=[Byb], writes=[Bve])
                    S.op("dve", lambda e: e.tensor_scalar(out=V2[:, :, 32:64].rearrange("p c s -> p s c"), in0=Yb[:, 32:64, :], scalar1=FLAG,
                                                          scalar2=None, op0=ALU.mult), reads=[Byb, Bconst], writes=[Bve], acc=True)
                    S.op("act", lambda e: e.activation(out=V2[:, :, 64:96].rearrange("p c s -> p s c"), in_=Yb[:, 32:64, :], func=AF.Copy,
                                                       scale=NFLAG), reads=[Byb, Bconst], writes=[Bve], acc=True)
                    for c in range(128):
                        cc = 128 * cg + c
                        hk, Bhk = HK.next()
                        src = bass.AP(G.tensor, G.offset + cc * 16384, [[1, 128], [1, 16257]])
                        S.dma("sp", hk[:, 0:16257], src, key=Bhk, reads=[BG], writes=[Bhk], acc=False)
                        pt, Bp = HBK.next()
                        for di, d in enumerate(dlist):
                            yd = 8065 + 128 * d
                            if d >= 0:
                                o0, o1, i0 = d, 96, 0
                            else:
                                o0, o1, i0 = 0, 96 + d, -d
                            nn = o1 - o0
                            S.op("pe", lambda e, yd=yd, o0=o0, o1=o1, i0=i0, nn=nn, di=di: e.matmul(
                                pt[:, o0:o1], lhsT=hk[:, yd:yd + 128], rhs=V2[:, c, i0:i0 + nn],
                                start=(di == 0), stop=(di == len(dlist) - 1), skip_group_check=True),
                                reads=[Bhk, Bve], writes=[Bp], acc=(di > 0))
                        th, Bth = TH.next()
                        th0, Bth0 = TH0.next()
                        S.op("act", lambda e: e.activation(out=th0[:], in_=pt[:, 0:32], func=AF.Copy, scale=RNb[:, 0, cc:cc + 1]),
                             reads=[Bp, Brn], writes=[Bth0])
                        S.op("pool", lambda e: e.tensor_tensor(out=Yb[:, 0:32, c], in0=th0[:], in1=X0g[:, 0:32, c], op=ALU.mult),
                             reads=[Bth0, Bxg], writes=[Byb], acc=True)
                        S.op("act", lambda e: e.activation(out=th[:, 0, :], in_=pt[:, 32:64], func=AF.Copy, scale=RNb[:, 1, cc:cc + 1]),
                             reads=[Bp, Brn], writes=[Bth])
                        S.op("dve", lambda e: e.scalar_tensor_tensor(out=th[:, 1, :], in0=pt[:, 64:96], scalar=RNb[:, 2, cc:cc + 1],
                                                                     in1=th[:, 0, :], op0=ALU.mult, op1=ALU.add),
                             reads=[Bp, Brn, Bth], writes=[Bth], acc=True)
                        S.op("dve", lambda e: e.tensor_tensor(out=Yb[:, 32:64, c], in0=th[:, 1, :], in1=X0g[:, 32:64, c], op=ALU.mult),
                             reads=[Bth, Bxg], writes=[Byb], acc=True)
                        yield
                    for n0 in range(0, NT, 8):
                        pt, Bp = HBK.next()
                        ptb = pt[:].bitcast(BF16)
                        for q in range(8):
                            S.op("pe", lambda e, q=q: e.transpose(out=ptb[:, 128 * q:128 * (q + 1)], in_=Yb[:, n0 + q, :], identity=ident_b),
                                 reads=[Byb, Bconst], writes=[Bp], acc=(q > 0))
                        ys, Bys = YS.next()
                        S.op("act", lambda e: e.activation(out=ys[:].rearrange("p a b -> p (a b)"), in_=ptb[:, :], func=AF.Copy), reads=[Bp], writes=[Bys])
                        S.dma("pool", YAT[n0:n0 + 8, :, cg, :].rearrange("n p t -> p n t"), ys[:], key=Bys, reads=[Bys])
                    yield

            ones512 = sb(st, "ones512", [128, 512], F32)
            gnb = sb(st, "gnb", [64, 512], F32)
            Sf = sb(st, "Sf", [128, 4, 128], F32)
            Sbf = sb(st, "Sbf", [128, 4, 128], BF16)
            BSf = [Buf(f"Sf{h}") for h in range(4)]
            BSb = [Buf(f"Sb{h}") for h in range(4)]
            Bg1 = Buf("g1c")
            S.op("pool", lambda e: e.memset(ones512[:], 1.0), writes=[Bg1])
            S.dma("sp", gnb[:], gnorm_b[0:64, :], key=Bg1, writes=[Bg1])
            QL = Rot("ql", [sb(st, f"ql{i}", [128, 512], BF16) for i in range(2)])
            KL = Rot("kl", [sb(st, f"kl{i}", [128, 512], BF16) for i in range(2)])
            GL = Rot("gl", [sb(st, f"gl{i}", [128, 512], F32) for i in range(2)])
            VL = Rot("vl", [sb(st, f"vl{i}", [64, 2, 512], BF16) for i in range(2)])
            PG = Rot("pg", [sb(st, f"pg{i}", [128, 520], F32) for i in range(2)])
            AA = Rot("aa", [sb(st, f"aa{i}", [128, 512], F32) for i in range(2)])
            X1 = Rot("x1", [sb(st, f"x1{i}", [128, 512], F32) for i in range(2)])
            X4 = Rot("x4", [sb(st, f"x4{i}", [128, 512], F32) for i in range(1)])
            EE = Rot("ee", [sb(st, f"ee{i}", [128, 4, 512], F32) for i in range(1)])
            DC = Rot("dc", [sb(st, f"dc{i}", [128, 8], F32) for i in range(2)])
            QK = Rot("qk", [sb(st, f"qk{i}", [128, 4, 512], BF16) for i in range(1)])
            AM = Rot("am", [sb(st, f"am{i}", [64, 64], BF16) for i in range(8)])
            KT = Rot("kt", [sb(st, f"kt{i}", [64, 128], BF16) for i in range(8)])
            OO = Rot("oo", [sb(st, f"oo{i}", [64, 2, 512], F32) for i in range(1)])
            OFL = Rot("ofl", [sb(st, f"ofl{i}", [64, 2, 512], F32) for i in range(1)])
            OGL = Rot("ogl", [sb(st, f"ogl{i}", [64, 2, 512], BF16) for i in range(1)])
            SQ = Rot("sq", [sb(st, f"sq{i}", [64, 2, 512], F32) for i in range(1)])
            MS = Rot("ms", [sb(st, f"ms{i}", [64, 8], F32) for i in range(2)])
            YB = Rot("yb", [sb(st, f"yb{i}", [64, 2, 512], BF16) for i in range(1)])
            YT = Rot("yt", [sb(st, f"yt{i}", [128, 4, 128], BF16) for i in range(2)])

            def g3(ap):
                return ap.rearrange("p (g t) -> p g t", t=64)

            def g_loads(d, n):
                ql, Bql = QL.next()
                kl, Bkl = KL.next()
                gl, Bgl = GL.next()
                vl, Bvl = VL.next()
                S.dma("sp", ql[:], QT[n], key=Bql, writes=[Bql], acc=False)
                S.dma("sp", kl[:], KFT[n, :, d, :], key=Bkl, writes=[Bkl], acc=False)
                S.dma("sp", gl[:], GFT[n, :, d, :], key=Bgl, writes=[Bgl], acc=False)
                S.dma("sp", vl[:], IT[n].rearrange("(c t) f -> t c f", c=2), key=Bvl, writes=[Bvl], acc=False)
                return (ql, Bql, kl, Bkl, gl, Bgl, vl, Bvl)

            def g_gen():
                steps = [(0, n) for n in range(NT)] + [(1, n) for n in range(NT - 1, -1, -1)]
                pend = g_loads(*steps[0]) if steps else None
                for si, (d, n) in enumerate(steps):
                    cur = pend
                    pend = g_loads(*steps[si + 1]) if si + 1 < len(steps) else None
                    if n == (0 if d == 0 else NT - 1):
                        for h in range(4):
                            S.op("pool", lambda e, h=h: e.memset(Sf[:, h, :], 0.0), writes=[BSf[h]])
                            S.op("pool", lambda e, h=h: e.memset(Sbf[:, h, :], 0.0), writes=[BSb[h]])
                    mask = cm_b[0:64, 256:320] if d == 0 else cm_b[0:64, 320:384]
                    if (d == 0 and n == SEG) or (d == 1 and n == SEG - 1):
                        for h in range(4):
                            S.op("dve", lambda e, h=h: e.tensor_scalar(out=Sf[:, h, :], in0=Sf[:, h, :], scalar1=FLAG, scalar2=None, op0=ALU.mult),
                                 reads=[Bconst], writes=[BSf[h]])
                            S.op("act", lambda e, h=h: e.activation(out=Sbf[:, h, :], in_=Sf[:, h, :], func=AF.Copy), reads=[BSf[h]], writes=[BSb[h]])
                    (ql, Bql, kl, Bkl, gl, Bgl, vl, Bvl) = cur
                    pg, Bpg = PG.next()
                    S.op("pool", lambda e: e.memset(pg[:, 0:1], 0.0), writes=[Bpg])
                    S.op("dve", lambda e: e.tensor_tensor_scan(out=pg[:, 1:513], data0=ones512[:], data1=gl[:], initial=0.0,
                                                               op0=ALU.mult, op1=ALU.add), reads=[Bgl, Bg1], writes=[Bpg], acc=True)
                    aa, Baa = AA.next()
                    S.op("dve", lambda e: e.tensor_tensor(out=g3(aa[:]), in0=g3(pg[:, 1:513]),
                                                          in1=g3(pg[:, 0:512])[:, :, 0:1].to_broadcast([128, 8, 64]), op=ALU.subtract),
                         reads=[Bpg], writes=[Baa])
                    if d == 1:
                        x1, Bx1 = X1.next()
                        S.op("dve", lambda e: e.tensor_tensor(out=x1[:], in0=gl[:], in1=aa[:], op=ALU.subtract), reads=[Bgl, Baa], writes=[Bx1])
                        aa2, Baa2 = AA.next()
                        S.op("dve", lambda e: e.tensor_tensor(out=g3(aa2[:]), in0=g3(x1[:]),
                                                              in1=g3(aa[:])[:, :, 63:64].to_broadcast([128, 8, 64]), op=ALU.add),
                             reads=[Bx1, Baa], writes=[Baa2])
                        aa, Baa = aa2, Baa2
                        lastc = 0
                    else:
                        lastc = 63
                    x1, Bx1 = X1.next()
                    x4, Bx4 = X4.next()
                    S.op("dve", lambda e: e.tensor_tensor(out=g3(x1[:]), in0=g3(aa[:]), in1=g3(aa[:])[:, :, 32:33].to_broadcast([128, 8, 64]),
                                                          op=ALU.subtract), reads=[Baa], writes=[Bx1])
                    S.op("pool", lambda e: e.tensor_tensor(out=g3(x4[:]), in0=g3(aa[:]),
                                                           in1=g3(aa[:])[:, :, lastc:lastc + 1].to_broadcast([128, 8, 64]), op=ALU.subtract),
                         reads=[Baa], writes=[Bx4])
                    ee, Bee = EE.next()
                    dc, Bdc = DC.next()
                    S.op("act", lambda e: e.activation(out=ee[:, 0, :], in_=x1[:], func=AF.Exp), reads=[Bx1], writes=[Bee])
                    S.op("act", lambda e: e.activation(out=ee[:, 1, :], in_=x1[:], func=AF.Exp, scale=-1.0), reads=[Bx1], writes=[Bee], acc=True)
                    S.op("act", lambda e: e.activation(out=ee[:, 2, :], in_=aa[:], func=AF.Exp), reads=[Baa], writes=[Bee], acc=True)
                    S.op("act", lambda e: e.activation(out=ee[:, 3, :], in_=x4[:], func=AF.Exp, scale=-1.0), reads=[Bx4], writes=[Bee], acc=True)
                    S.op("act", lambda e: e.activation(out=dc[:].unsqueeze(2), in_=g3(aa[:])[:, :, lastc:lastc + 1], func=AF.Exp), reads=[Baa], writes=[Bdc])
                    qk, Bqk = QK.next()
                    S.op("dve", lambda e: e.tensor_tensor(out=qk[:, 0, :], in0=ql[:], in1=ee[:, 0, :], op=ALU.mult), reads=[Bql, Bee], writes=[Bqk])
                    S.op("pool", lambda e: e.tensor_tensor(out=qk[:, 1, :], in0=kl[:], in1=ee[:, 1, :], op=ALU.mult), reads=[Bkl, Bee], writes=[Bqk], acc=True)
                    S.op("dve", lambda e: e.tensor_tensor(out=qk[:, 2, :], in0=ql[:], in1=ee[:, 2, :], op=ALU.mult), reads=[Bql, Bee], writes=[Bqk], acc=True)
                    S.op("pool", lambda e: e.tensor_tensor(out=qk[:, 3, :], in0=kl[:], in1=ee[:, 3, :], op=ALU.mult), reads=[Bkl, Bee], writes=[Bqk], acc=True)
                    yield
                    pos = (psb[0], psb[1])
                    Bpos = (PS.bufs[0], PS.bufs[1])
                    chunks = (0, 1) if d == 0 else (1, 0)
                    for c in chunks:
                        pA, BpA = psb[3][:, 256 * c:256 * (c + 1)], PS.bufs[3]
                        BpK = PS.bufs[2]
                        pKb = psb[2][:].bitcast(BF16)[:, 512 * c:512 * (c + 1)]
                        for h in range(4):
                            cs = slice(128 * h + 64 * c, 128 * h + 64 * c + 64)
                            S.op("pe", lambda e, h=h, cs=cs: e.matmul(pA[0:64, 64 * h:64 * (h + 1)], lhsT=qk[:, 1, cs], rhs=qk[:, 0, cs],
                                                                      start=True, stop=True), reads=[Bqk], writes=[BpA], acc=True)
                        for h in range(4):
                            cs = slice(128 * h + 64 * c, 128 * h + 64 * c + 64)
                            S.op("pe", lambda e, h=h, cs=cs: e.transpose(out=pKb[0:64, 128 * h:128 * (h + 1)], in_=qk[:, 3, cs], identity=ident_b),
                                 reads=[Bqk, Bconst], writes=[BpK], acc=True)
                        ams, kts = [], []
                        for h in range(4):
                            am, Bam = AM.next()
                            kt, Bkt = KT.next()
                            S.op("dve", lambda e, h=h: e.tensor_tensor(out=am[:], in0=pA[0:64, 64 * h:64 * (h + 1)], in1=mask, op=ALU.mult),
                                 reads=[BpA, Bconst], writes=[Bam])
                            S.op("act", lambda e, h=h: e.activation(out=kt[:], in_=pKb[0:64, 128 * h:128 * (h + 1)], func=AF.Copy), reads=[BpK], writes=[Bkt])
                            ams.append((am, Bam))
                            kts.append((kt, Bkt))
                        yield
                        for h in range(4):
                            cs = slice(128 * h + 64 * c, 128 * h + 64 * c + 64)
                            am, Bam = ams[h]
                            kt, Bkt = kts[h]
                            po = pos[c][0:64, 128 * h:128 * (h + 1)]
                            S.op("pe", lambda e, h=h: e.matmul(po, lhsT=am[:], rhs=vl[:, c, 128 * h:128 * (h + 1)], start=True, stop=False),
                                 reads=[Bam, Bvl], writes=[Bpos[c]], acc=(h > 0))
                            S.op("pe", lambda e, h=h, cs=cs: e.matmul(po, lhsT=qk[:, 2, cs], rhs=Sbf[:, h, :], start=False, stop=True),
                                 reads=[Bqk, BSb[h]], writes=[Bpos[c]], acc=True)
                            pS, BpS = psb[4 + c], PS.bufs[4 + c]
                            S.op("pe", lambda e, h=h: e.matmul(pS[:, 128 * h:128 * (h + 1)], lhsT=kt[:], rhs=vl[:, c, 128 * h:128 * (h + 1)], start=True, stop=True),
                                 reads=[Bkt, Bvl], writes=[BpS], acc=(h > 0))
                            S.op("dve", lambda e, h=h: e.scalar_tensor_tensor(out=Sf[:, h, :], in0=Sf[:, h, :], scalar=dc[:, 2 * h + c:2 * h + c + 1],
                                                                              in1=pS[:, 128 * h:128 * (h + 1)], op0=ALU.mult, op1=ALU.add),
                                 reads=[BpS, Bdc], writes=[BSf[h]])
                            S.op("act", lambda e, h=h: e.activation(out=Sbf[:, h, :], in_=Sf[:, h, :], func=AF.Copy), reads=[BSf[h]], writes=[BSb[h]])
                        yield
                    oo, Boo = OO.next()
                    OFv = OF[n].rearrange("(c t) f -> t c f", c=2)
                    if d == 0:
                        for c in range(2):
                            S.op("act", lambda e, c=c: e.activation(out=oo[:, c, :], in_=pos[c][0:64, :], func=AF.Copy), reads=[Bpos[c]], writes=[Boo], acc=(c > 0))
                        S.dma("pool", OFv, oo[:], key=Boo, reads=[Boo], writes=[BOF])
                    else:
                        ofl, Bofl = OFL.next()
                        ogl, Bogl = OGL.next()
                        S.dma("sp", ofl[:], OFv, key=Bofl, reads=[BOF], writes=[Bofl], acc=False)
                        S.dma("sp", ogl[:], OGT[n].rearrange("(c t) f -> t c f", c=2), key=Bogl, writes=[Bogl], acc=False)
                        for c in range(2):
                            S.op("dve", lambda e, c=c: e.tensor_tensor(out=oo[:, c, :], in0=pos[c][0:64, :], in1=ofl[:, c, :], op=ALU.add),
                                 reads=[Bpos[c], Bofl], writes=[Boo], acc=(c > 0))
                        sq_, Bsq_ = SQ.next()
                        ms, Bms = MS.next()
                        S.op("pool", lambda e: e.tensor_tensor(out=sq_[:], in0=oo[:], in1=oo[:], op=ALU.mult), reads=[Boo], writes=[Bsq_])
                        S.op("dve", lambda e: e.tensor_reduce(out=ms[:], in_=sq_[:].rearrange("p c (h v) -> p (c h) v", v=128), axis=AX.X, op=ALU.add),
                             reads=[Bsq_], writes=[Bms])
                        S.op("dve", lambda e: e.tensor_scalar(out=ms[:], in0=ms[:], scalar1=1.0 / 128, scalar2=EPS, op0=ALU.mult, op1=ALU.add),
                             reads=[Bms], writes=[Bms])
                        S.op("act", lambda e: e.activation(out=ms[:], in_=ms[:], func=AF.Ln), reads=[Bms], writes=[Bms])
                        S.op("act", lambda e: e.activation(out=ms[:], in_=ms[:], func=AF.Exp, scale=-0.5), reads=[Bms], writes=[Bms])
                        S.op("dve", lambda e: e.tensor_tensor(out=oo[:].rearrange("p c (h v) -> p (c h) v", v=128),
                                                              in0=oo[:].rearrange("p c (h v) -> p (c h) v", v=128),
                                                              in1=ms[:].unsqueeze(2).to_broadcast([64, 8, 128]), op=ALU.mult),
                             reads=[Boo, Bms], writes=[Boo])
                        S.op("pool", lambda e: e.tensor_tensor(out=oo[:], in0=oo[:], in1=gnb[:].unsqueeze(1).to_broadcast([64, 2, 512]), op=ALU.mult),
                             reads=[Boo, Bg1], writes=[Boo])
                        yb, Byb2 = YB.next()
                        S.op("dve", lambda e: e.tensor_tensor(out=yb[:], in0=oo[:], in1=ogl[:], op=ALU.mult), reads=[Boo, Bogl], writes=[Byb2])
                        pT, BpT = psb[2], PS.bufs[2]
                        pTb = pT[:].bitcast(BF16)
                        for h in range(4):
                            for c in range(2):
                                S.op("pe", lambda e, h=h, c=c: e.transpose(out=pTb[:, 128 * h + 64 * c:128 * h + 64 * c + 64],
                                                                           in_=yb[:, c, 128 * h:128 * (h + 1)], identity=ident_b[0:64, 0:64]),
                                     reads=[Byb2, Bconst], writes=[BpT], acc=(h + c > 0))
                        yt, Byt = YT.next()
                        S.op("act", lambda e: e.activation(out=yt[:].rearrange("p a b -> p (a b)"), in_=pTb[:, 0:512], func=AF.Copy), reads=[BpT], writes=[Byt])
                        S.dma("pool", YBT[n], yt[:], key=Byt, reads=[Byt])
                    yield

            gH, gG = h_gen(), g_gen()
            aliveH, aliveG = True, True
            it, ng = 0, 0
            while aliveH or aliveG:
                if aliveH:
                    try:
                        next(gH)
                    except StopIteration:
                        aliveH = False
                k = 1 + (it % 2)
                for _ in range(k):
                    if aliveG:
                        try:
                            next(gG)
                            ng += 1
                        except StopIteration:
                            aliveG = False
                it += 1
            S.barrier()

        X1 = dscr("X1", [NT, 128, 1024], F32)
        BX1 = Buf("X1")
        By = Buf("y")
        with ExitStack() as stc:
            GTb = sb(stc, "GTb", [128, 2, 2048], F32)
            Bgt = Buf("gtb")
            with ExitStack() as st:
                dg = sb(st, "dgc", [128, 128], F32)
                Bdg = Buf("dgc")
                for s_ in range(2):
                    for gi, j0 in enumerate((16, 40)):
                        for jj in range(8):
                            j = j0 + jj
                            S.op("dve", lambda e, j=j, s_=s_: e.tensor_scalar(out=dg[:], in0=ident_f, scalar1=modT[:, j, s_:s_ + 1], scalar2=None,
                                                                               op0=ALU.mult), reads=[Bmod, Bconst], writes=[Bdg])
                            pt, Bp = PS.next()
                            S.op("pe", lambda e: e.matmul(pt[:, 0:128], lhsT=ones_f[:], rhs=dg[:], start=True, stop=True),
                                 reads=[Bdg, Bconst], writes=[Bp])
                            c0 = gi * 1024 + jj * 128
                            S.op("act", lambda e, c0=c0, s_=s_: e.activation(out=GTb[:, s_, c0:c0 + 128], in_=pt[:, 0:128], func=AF.Copy),
                                 reads=[Bp], writes=[Bgt], acc=True)
                S.barrier()

            def load_w(st2, dst, src, nk, width, Bdst, tag):
                stg = [sb(st2, f"cstg{tag}{i}", [128, 2048], F32) for i in range(2)]
                STG = Rot("cstg" + tag, stg)
                ei = 0
                for k in range(nk):
                    for c0 in range(0, width, 2048):
                        wd = min(2048, width - c0)
                        stt, Bst = STG.next()
                        S.dma("sp", stt[:, 0:wd], src[128 * k:128 * (k + 1), c0:c0 + wd], key=Bst, writes=[Bst], acc=False)
                        en = ("dve", "act")[ei % 2]
                        ei += 1
                        if en == "act":
                            S.op("act", lambda e: e.activation(out=dst[:, k, c0:c0 + wd], in_=stt[:, 0:wd], func=AF.Copy),
                                 reads=[Bst], writes=[Bdst], acc=True)
                        else:
                            S.op(en, lambda e: e.tensor_copy(out=dst[:, k, c0:c0 + wd], in_=stt[:, 0:wd]), reads=[Bst], writes=[Bdst], acc=True)

            with ExitStack() as st:
                Wb = sb(st, "Wb", [128, 8, 1024], BF16)
                Wo = sb(st, "Wo", [128, 8, 1024], BF16)
                Bwb = Buf("wb")
                with ExitStack() as st2:
                    load_w(st2, Wb, w_branch, 8, 1024, Bwb, 'a')
                    load_w(st2, Wo, w_out, 8, 1024, Bwb, 'b')
                    S.barrier()
                XL = Rot("xl", [sb(st, f"xl{i}", [128, 1024], F32) for i in range(2)])
                YA = Rot("ya", [sb(st, f"ya{i}", [128, 4, 128], BF16) for i in range(2)])
                YBL = Rot("ybl", [sb(st, f"ybl{i}", [128, 4, 128], BF16) for i in range(2)])
                GL2 = Rot("gl2", [sb(st, f"gl2{i}", [128, 2048], BF16) for i in range(2)])
                T1 = Rot("t1", [sb(st, f"t1{i}", [128, 512], F32) for i in range(2)])
                T2 = Rot("t2", [sb(st, f"t2{i}", [128, 512], F32) for i in range(2)])
                MT = Rot("mt", [sb(st, f"mt{i}", [128, 8, 128], BF16) for i in range(2)])
                TX = Rot("tx", [sb(st, f"tx{i}", [128, 1024], F32) for i in range(2)])
                XO = Rot("xo", [sb(st, f"xo{i}", [128, 1024], F32) for i in range(2)])
                def ca_load(n):
                    xl, Bxl = XL.next()
                    ya, Bya = YA.next()
                    ybl, Bybl = YBL.next()
                    gl2, Bgl2 = GL2.next()
                    S.dma("sp", xl[:], x[128 * n:128 * (n + 1), :], key=Bxl, writes=[Bxl], acc=False)
                    S.dma("sp", ya[:], YAT[n], key=Bya, writes=[Bya], acc=False)
                    S.dma("sp", ybl[:], YBT[n], key=Bybl, writes=[Bybl], acc=False)
                    S.dma("sp", gl2[:], GTT[n], key=Bgl2, writes=[Bgl2], acc=False)
                    return (xl, Bxl, ya, Bya, ybl, Bybl, gl2, Bgl2)

                ca_next = ca_load(0)
                for n in range(NT):
                    seg = n // SEG
                    (xl, Bxl, ya, Bya, ybl, Bybl, gl2, Bgl2) = ca_next
                    if n + 1 < NT:
                        ca_next = ca_load(n + 1)
                    mt, Bmt = MT.next()
                    for q in range(2):
                        pa, Bpa = PS.next()
                        pb, Bpb = PS.next()
                        for r4 in range(4):
                            nch = 4 * q + r4
                            for cc in range(4):
                                S.op("pe", lambda e, r4=r4, nch=nch, cc=cc: e.matmul(pa[:, 128 * r4:128 * (r4 + 1)],
                                                                                     lhsT=Wb[:, cc, 128 * nch:128 * (nch + 1)], rhs=ya[:, cc, :],
                                                                                     start=(cc == 0), stop=(cc == 3)),
                                     reads=[Bwb, Bya], writes=[Bpa], acc=(r4 + cc > 0))
                        for r4 in range(4):
                            nch = 4 * q + r4
                            for cc in range(4):
                                S.op("pe", lambda e, r4=r4, nch=nch, cc=cc: e.matmul(pb[:, 128 * r4:128 * (r4 + 1)],
                                                                                     lhsT=Wb[:, 4 + cc, 128 * nch:128 * (nch + 1)], rhs=ybl[:, cc, :],
                                                                                     start=(cc == 0), stop=(cc == 3)),
                                     reads=[Bwb, Bybl], writes=[Bpb], acc=(r4 + cc > 0))
                        t1, Bt1 = T1.next()
                        t2, Bt2 = T2.next()
                        S.op("dve", lambda e: e.tensor_tensor(out=t1[:], in0=pa[:, :], in1=gl2[:, 512 * q:512 * (q + 1)], op=ALU.mult),
                             reads=[Bpa, Bgl2], writes=[Bt1])
                        S.op("dve", lambda e: e.tensor_tensor(out=t2[:], in0=pb[:, :], in1=gl2[:, 1024 + 512 * q:1024 + 512 * (q + 1)], op=ALU.mult),
                             reads=[Bpb, Bgl2], writes=[Bt2])
                        S.op("dve", lambda e: e.tensor_tensor(out=mt[:, 4 * q:4 * (q + 1), :].rearrange("p a b -> p (a b)"), in0=t1[:], in1=t2[:], op=ALU.add),
                             reads=[Bt1, Bt2], writes=[Bmt], acc=(q > 0))
                    tx, Btx = TX.next()
                    xo, Bxo = XO.next()
                    for dh in range(2):
                        po, Bpo = PS.next()
                        for nch in range(8):
                            S.op("pe", lambda e, nch=nch: e.matmul(po[:, :], lhsT=mt[:, nch, :], rhs=Wo[:, nch, 512 * dh:512 * (dh + 1)],
                                                                   start=(nch == 0), stop=(nch == 7)), reads=[Bmt, Bwb], writes=[Bpo], acc=(nch > 0))
                        S.op("dve", lambda e: e.tensor_tensor(out=tx[:, 512 * dh:512 * (dh + 1)], in0=po[:, :], in1=GTb[:, seg, 512 * dh:512 * (dh + 1)],
                                                              op=ALU.mult), reads=[Bpo, Bgt], writes=[Btx], acc=(dh > 0))
                    S.op("dve", lambda e: e.tensor_tensor(out=xo[:], in0=tx[:], in1=xl[:], op=ALU.add), reads=[Btx, Bxl], writes=[Bxo])
                    S.dma("sp", X1[n], xo[:], key=Bxo, reads=[Bxo], writes=[BX1])
                S.barrier()

            NSUP = NT // 4
            ATD = dscr("ATD", [NSUP, 128, 32 * 512], BF16)
            BATD = Buf("ATD")
            _csk = False

            def rstd_of(SQ2, src, Bsrc, ss, Bss, col):
                sq2, Bsq2 = SQ2.next()
                S.op("dve", lambda e: e.tensor_tensor(out=sq2[:], in0=src, in1=src, op=ALU.mult), reads=[Bsrc], writes=[Bsq2])
                S.op("dve", lambda e: e.tensor_reduce(out=ss[:, col:col + 1], in_=sq2[:], axis=AX.X, op=ALU.add), reads=[Bsq2], writes=[Bss], acc=True)
                S.op("dve", lambda e: e.tensor_scalar(out=ss[:, col + 1:col + 2], in0=ss[:, col:col + 1], scalar1=1.0 / 1024, scalar2=EPS,
                                                      op0=ALU.mult, op1=ALU.add), reads=[Bss], writes=[Bss], acc=True)
                S.op("act", lambda e: e.activation(out=ss[:, col + 1:col + 2], in_=ss[:, col + 1:col + 2], func=AF.Ln), reads=[Bss], writes=[Bss], acc=True)
                S.op("act", lambda e: e.activation(out=ss[:, col + 1:col + 2], in_=ss[:, col + 1:col + 2], func=AF.Exp, scale=-0.5),
                     reads=[Bss], writes=[Bss], acc=True)

            with ExitStack() as st:
                W1 = sb(st, "W1", [128, 8, 4096], BF16)
                Bw1 = Buf("w1")
                with ExitStack() as st2:
                    load_w(st2, W1, w_ff1, 8, 4096, Bw1, 'c')
                    S.barrier()
                XL = Rot("xl1", [sb(st, f"xl1{i}", [128, 1024], F32) for i in range(2)])
                SQ2 = Rot("sq2", [sb(st, f"sq2{i}", [128, 1024], F32) for i in range(1)])
                SS2 = Rot("ss2", [sb(st, f"ss2{i}", [128, 4], F32) for i in range(2)])
                XN2 = Rot("xn2", [sb(st, f"xn2{i}", [128, 1024], BF16) for i in range(8)])
                H2 = Rot("h2", [sb(st, f"h2{i}", [128, 8, 512], BF16) for i in range(2)])
                RR = Rot("rr", [sb(st, f"rr{i}", [128, 512], F32) for i in range(3)])
                AT = Rot("at", [sb(st, f"at{i}", [128, 32, 512], BF16) for i in range(2)])
                xn2s = {}

                def cb_stageA(su):
                    for j in range(4):
                        n = 4 * su + j
                        xl, Bxl = XL.next()
                        S.dma("sp", xl[:], X1[n], key=Bxl, reads=[BX1], writes=[Bxl], acc=False)
                        ss, Bss = SS2.next()
                        S.op("pool", lambda e: e.memset(ss[:], 0.0), writes=[Bss])
                        rstd_of(SQ2, xl[:], Bxl, ss, Bss, 0)
                        xn, Bxn = XN2.next()
                        S.op("act", lambda e: e.activation(out=xn[:], in_=xl[:], func=AF.Copy, scale=ss[:, 1:2]), reads=[Bxl, Bss], writes=[Bxn])
                        xn2s[n] = (xn, Bxn)

                h2s = {}

                def cb_stageB(su):
                    h2, Bh2 = H2.next()
                    h2s[su] = (h2, Bh2)
                    for j in range(4):
                        n = 4 * su + j
                        seg = n // SEG
                        xn, Bxn = xn2s.pop(n)
                        pt, Bp = PS.next()
                        ptb = pt[:].bitcast(BF16)
                        for k in range(8):
                            S.op("pe", lambda e, k=k: e.transpose(out=ptb[:, 128 * k:128 * (k + 1)], in_=xn[:, 128 * k:128 * (k + 1)], identity=ident_b),
                                 reads=[Bxn, Bconst], writes=[Bp], acc=(k > 0))
                        for k in range(8):
                            if k % 2 == 0:
                                S.op("dve", lambda e, k=k: e.tensor_scalar(out=h2[:, k, 128 * j:128 * (j + 1)], in0=ptb[:, 128 * k:128 * (k + 1)],
                                                                           scalar1=a2T[:, k, seg:seg + 1], scalar2=modT[:, 24 + k, seg:seg + 1],
                                                                           op0=ALU.mult, op1=ALU.add), reads=[Bp, Bmod], writes=[Bh2], acc=(j + k > 0))
                            else:
                                S.op("act", lambda e, k=k: e.activation(out=h2[:, k, 128 * j:128 * (j + 1)], in_=ptb[:, 128 * k:128 * (k + 1)],
                                                                        func=AF.Identity, scale=a2T[:, k, seg:seg + 1], bias=modT[:, 24 + k, seg:seg + 1]),
                                     reads=[Bp, Bmod], writes=[Bh2], acc=(j + k > 0))

                if not _csk:
                    cb_stageA(0)
                    cb_stageB(0)
                for su in range(0 if _csk else NSUP):
                    h2, Bh2 = h2s.pop(su)
                    if su + 1 < NSUP:
                        cb_stageA(su + 1)
                    at, Bat = AT.next()
                    for fch in range(32):
                        pf, Bpf = PS.next()
                        for k in range(8):
                            S.op("pe", lambda e, fch=fch, k=k: e.matmul(pf[:, :], lhsT=W1[:, k, 128 * fch:128 * (fch + 1)], rhs=h2[:, k, :],
                                                                        start=(k == 0), stop=(k == 7)), reads=[Bw1, Bh2], writes=[Bpf], acc=(k > 0))
                        rr, Brr = RR.next()
                        S.op("act", lambda e: e.activation(out=rr[:], in_=pf[:, :], func=AF.Relu), reads=[Bpf], writes=[Brr])
                        en = "dve"
                        S.op(en, lambda e, fch=fch: e.tensor_tensor(out=at[:, fch, :], in0=rr[:], in1=rr[:], op=ALU.mult),
                             reads=[Brr], writes=[Bat], acc=(fch > 0))
                    S.dma("pool", ATD[su], at[:].rearrange("p a b -> p (a b)"), key=Bat, reads=[Bat], writes=[BATD])
                    if su + 1 < NSUP:
                        cb_stageB(su + 1)
                S.barrier()

            with ExitStack() as st:
                W2 = sb(st, "W2", [128, 32, 1024], BF16)
                fgb = sb(st, "fgb", [128, 1024], F32)
                Bw2, Bfg = Buf("w2"), Buf("fgb")
                S.dma("sp", fgb[:], finalg_b[:, :], key=Bfg, writes=[Bfg])
                with ExitStack() as st2:
                    load_w(st2, W2, w_ff2, 32, 1024, Bw2, 'd')
                    S.barrier()
                XL = Rot("xl2", [sb(st, f"xl2{i}", [128, 1024], F32) for i in range(2)])
                SQ2 = Rot("sq3", [sb(st, f"sq3{i}", [128, 1024], F32) for i in range(1)])
                SS2 = Rot("ss3", [sb(st, f"ss3{i}", [128, 4], F32) for i in range(2)])
                AT = Rot("at2", [sb(st, f"at2{i}", [128, 32, 512], BF16) for i in range(2)])
                X2 = Rot("x2", [sb(st, f"x2{i}", [128, 1024], F32) for i in range(2)])
                YO = Rot("yo", [sb(st, f"yo{i}", [128, 1024], F32) for i in range(2)])
                def at_load(su):
                    at, Bat = AT.next()
                    S.dma("sp", at[:].rearrange("p a b -> p (a b)"), ATD[su], key=Bat, reads=[BATD], writes=[Bat], acc=False)
                    return at, Bat

                def xl_load(n):
                    xl, Bxl = XL.next()
                    S.dma("sp", xl[:], X1[n], key=Bxl, reads=[BX1], writes=[Bxl], acc=False)
                    return xl, Bxl

                at_next = None if _csk else at_load(0)
                xl_next = None if _csk else xl_load(0)
                for su in range(0 if _csk else NSUP):
                    at, Bat = at_next
                    if su + 1 < NSUP:
                        at_next = at_load(su + 1)
                    for j in range(4):
                        n = 4 * su + j
                        seg = n // SEG
                        xl, Bxl = xl_next
                        if n + 1 < NT:
                            xl_next = xl_load(n + 1)
                        x2, Bx2 = X2.next()
                        for dh in range(2):
                            po, Bpo = PS.next()
                            for fch in range(32):
                                S.op("pe", lambda e, fch=fch: e.matmul(po[:, :], lhsT=at[:, fch, 128 * j:128 * (j + 1)], rhs=W2[:, fch, 512 * dh:512 * (dh + 1)],
                                                                       start=(fch == 0), stop=(fch == 31)), reads=[Bat, Bw2], writes=[Bpo], acc=(fch > 0))
                            S.op("dve", lambda e: e.tensor_tensor(out=x2[:, 512 * dh:512 * (dh + 1)], in0=po[:, :],
                                                                  in1=GTb[:, seg, 1024 + 512 * dh:1024 + 512 * (dh + 1)], op=ALU.mult),
                                 reads=[Bpo, Bgt], writes=[Bx2], acc=(dh > 0))
                        S.op("dve", lambda e: e.tensor_tensor(out=x2[:], in0=x2[:], in1=xl[:], op=ALU.add), reads=[Bx2, Bxl], writes=[Bx2])
                        ss, Bss = SS2.next()
                        S.op("pool", lambda e: e.memset(ss[:], 0.0), writes=[Bss])
                        rstd_of(SQ2, x2[:], Bx2, ss, Bss, 2)
                        yo, Byo = YO.next()
                        S.op("act", lambda e: e.activation(out=yo[:], in_=x2[:], func=AF.Copy, scale=ss[:, 3:4]), reads=[Bx2, Bss], writes=[Byo])
                        S.op("dve", lambda e: e.tensor_tensor(out=yo[:], in0=yo[:], in1=fgb[:], op=ALU.mult), reads=[Byo, Bfg], writes=[Byo])
                        S.dma("sp", y[128 * n:128 * (n + 1), :], yo[:], key=Byo, reads=[Byo], writes=[By])
                S.wait_bufs("sp", [By])
                S.barrier()
        print("instructions:", S.nins)
    return nc


def _consts():
    cm = np.zeros((128, 512), np.float32)
    cm[:, 0:128] = np.eye(128)
    cm[:, 128:256] = np.eye(128)[::-1]
    lo = np.triu(np.ones((64, 64), np.float32))
    up = np.tril(np.ones((64, 64), np.float32))
    cm[0:64, 256:320] = lo
    cm[64:128, 256:320] = lo
    cm[0:64, 320:384] = up
    cm[64:128, 320:384] = up
    return cm


def make_in_maps(inp):
    f = lambda a: np.ascontiguousarray(np.asarray(a, dtype=np.float32))
    xp, xs = f(inp["x_prompt"]), f(inp["x_sample"])
    cp, cs = f(inp["c_prompt"]), f(inp["c_sample"])
    fT = lambda v, n: np.ascontiguousarray(v.reshape(n, 128).T)
    rep = lambda v: np.ascontiguousarray(np.broadcast_to(v.reshape(1, -1), (128, v.size)))
    shared = {
        "ada_w": f(inp["ada_w"][0]), "ada_bT": fT(f(inp["ada_b"][0]), 48),
        "g1T": fT(f(inp["norm1_g"][0]), 8), "g2T": fT(f(inp["norm2_g"][0]), 8),
        "w_in": f(inp["w_in"][0]),
        "convw_b": rep(f(inp["conv_w"][0]).reshape(-1)), "convb_b": rep(f(inp["conv_b"][0])),
        "fw1": f(inp["filt_w1"][0]), "fw2": f(inp["filt_w2"][0]), "fw3": f(inp["filt_w3"][0]), "fwo": f(inp["filt_wo"][0]),
        "fbT": np.ascontiguousarray(np.stack([f(inp["filt_b1"][0]), f(inp["filt_b2"][0]), f(inp["filt_b3"][0])], 1)),
        "ffreqT": np.ascontiguousarray(f(inp["filt_freq"][0]).T),
        "fbiasT": fT(f(inp["filt_bias"][0]), 4),
        "lbT": np.ascontiguousarray(f(inp["hgrn_lb"]).reshape(2, 2, 4, 128).transpose(3, 0, 1, 2).reshape(128, 16)),
        "gnorm_b": rep(f(inp["gnorm_g"][0])), "finalg_b": rep(f(inp["final_g"])),
        "w_branch": f(inp["w_branch"][0]), "w_out": f(inp["w_out"][0]),
        "w_ff1": f(inp["w_ff1"][0]), "w_ff2": f(inp["w_ff2"][0]),
        "cmat": _consts(),
    }
    fb = np.linspace(1e-4, 15.0, 16, dtype=np.float32)
    fconst = np.zeros((128, 8), np.float32)
    fconst[1:17, 0] = fb
    fconst[1:17, 1] = np.pi / 2
    fconst[17:33, 0] = -fb
    deltas = np.abs(np.linspace(math.log(1e-2) / 1.5, math.log(1e-2) / 0.3, 512, dtype=np.float32))
    fconst[:, 2:6] = deltas.reshape(4, 128).T
    shared["fconst"] = fconst
    maps = []
    for r in range(8):
        m = dict(shared)
        if r < 4:
            m["x"] = np.ascontiguousarray(xp[2 * r:2 * r + 2].reshape(8192, 1024))
            cc = cp[2 * r:2 * r + 2]
            L, flag = 4096.0, 0.0
        else:
            m["x"] = np.ascontiguousarray(xs[r - 4].reshape(8192, 1024))
            cc = np.stack([cs[r - 4], cs[r - 4]])
            L, flag = 8192.0, 1.0
        m["cT"] = np.ascontiguousarray(cc.reshape(2, 8, 128).transpose(2, 1, 0).reshape(128, 16))
        meta = np.zeros((128, 8), np.float32)
        meta[:, 0] = flag
        meta[:, 1] = 1.0 - flag
        meta[:, 2] = 1.0 / (L - 1.0)
        meta[:, 3] = TWO_PI / L
        meta[:, 4] = L
        m["meta"] = meta
        maps.append(m)
    return maps


def kernel(**inputs):
    nc = build()
    maps = make_in_maps(inputs)
    res = run_bass_kernel_spmd(nc, maps, core_ids=list(range(8)))
    ys = [np.asarray(res.results[r]["y"], dtype=np.float32) for r in range(8)]
    y_prompt = np.concatenate([ys[r].reshape(2, 4096, 1024) for r in range(4)], 0)
    y_sample = np.stack([ys[r].reshape(8192, 1024) for r in range(4, 8)], 0)
    return (y_prompt, y_sample)
```
